# Optimizing a Trainium2 kernel written in Bass

```python
import math
import jax, jax.numpy as jnp
from jax import lax
import numpy as np

D_MODEL = 1024
BATCH = 2
SEQ = 8192
DEPTH = 2

GRID_W = 64
CTX_LEN = 256
FOURIER_W = D_MODEL // 4
FOURIER_GC = 64
FOURIER_GROUPS = FOURIER_W // FOURIER_GC
DIFF_HD = 64
DIFF_VD = 2 * DIFF_HD
DIFF_HEADS = (D_MODEL - FOURIER_W) // DIFF_VD
QK_W = DIFF_HEADS * 2 * DIFF_HD
V_W = DIFF_HEADS * DIFF_VD
EVEN_IN_W = FOURIER_W + 2 * QK_W + V_W
DIFF_SCALE = DIFF_HD ** -0.5
Q_BLOCK = 128
ROPE_BASE = 10000.0
ROPE_FREQS = DIFF_HD // 4
S5_W = D_MODEL // 2
S5_GC = 16
S5_GROUPS = S5_W // S5_GC
S5_STATE = 64
GMLP_W = D_MODEL // 2
GMLP_GC = 128
GMLP_GROUPS = GMLP_W // GMLP_GC
CHUNK = 128
ODD_IN_W = S5_W + 2 * GMLP_W
MIX_W = D_MODEL
FFN_W = 4 * D_MODEL
LN_EPS = 1e-5
ALPHA = (2 * DEPTH) ** 0.25
BETA = (8 * DEPTH) ** -0.25
N_EVEN = (DEPTH + 1) // 2
N_ODD = DEPTH // 2

kernel_name = 'hybrid_fourier_diffattn_s5_gmlp_diffusion_trunk'


def _layernorm(x):
    xf = x.astype(jnp.float32)
    xc = xf - jnp.mean(xf, -1, keepdims=True)
    var = jnp.mean(xc * xc, -1, keepdims=True)
    return (xc * lax.rsqrt(var + LN_EPS)).astype(x.dtype)


def _modulate(x, shift, scale):
    return _layernorm(x) * (1 + scale) + shift


def _post_norm(x, y, gate, g, b):
    return _layernorm(ALPHA * x + gate * y) * g + b


def _axial_rope_tables(n):
    rows = n // GRID_W
    row = jnp.repeat(jnp.arange(rows, dtype=jnp.float32), GRID_W)
    col = jnp.tile(jnp.arange(GRID_W, dtype=jnp.float32), rows)
    inv = jnp.power(ROPE_BASE, -jnp.arange(ROPE_FREQS, dtype=jnp.float32) / ROPE_FREQS)
    ang = jnp.stack([row[:, None] * inv, col[:, None] * inv], axis=1)
    return jnp.cos(ang), jnp.sin(ang)


def _apply_rope(t, cos, sin):
    sh = t.shape
    tr = t.reshape(sh[:-1] + (2, 2, ROPE_FREQS))
    c = cos[None, :, None, None].astype(t.dtype)
    s = sin[None, :, None, None].astype(t.dtype)
    t1 = tr[..., 0, :]
    t2 = tr[..., 1, :]
    out = jnp.stack([t1 * c - t2 * s, t2 * c + t1 * s], axis=-2)
    return out.reshape(sh)


def _fourier_mix(f):
    B, L, _ = f.shape
    fg = f.astype(jnp.float32).reshape(B, L, FOURIER_GROUPS, FOURIER_GC)
    out = jnp.fft.fft2(fg, axes=(1, 3), norm='ortho').real
    return out.reshape(B, L, FOURIER_W).astype(f.dtype)


def _diff_attend(q, k_all, v_all, lam):
    s = jnp.einsum('bqhcd,bkhcd->bhcqk', q, k_all).astype(jnp.float32) * DIFF_SCALE
    p = jax.nn.softmax(s, axis=-1)
    a = p[:, :, 0] - lam * p[:, :, 1]
    return jnp.einsum('bhqk,bkhe->bqhe', a.astype(v_all.dtype), v_all)


def _diff_post(o, subln_g, lam_init):
    of = o.astype(jnp.float32)
    of = of * lax.rsqrt(jnp.mean(of * of, -1, keepdims=True) + LN_EPS)
    out = of.astype(o.dtype) * subln_g * (1 - lam_init)
    return out.reshape(o.shape[0], o.shape[1], V_W)


def _even_mixer(h, hc, w_in, lam_q1, lam_k1, lam_q2, lam_k2, subln_g, cos, sin, layer_idx, need_ctx):
    B, L, _ = h.shape
    Lc = hc.shape[1]
    kv0 = FOURIER_W + QK_W
    z = h @ w_in
    f = z[..., :FOURIER_W]
    q = z[..., FOURIER_W:kv0].reshape(B, L, DIFF_HEADS, 2, DIFF_HD)
    k = z[..., kv0:kv0 + QK_W].reshape(B, L, DIFF_HEADS, 2, DIFF_HD)
    v = z[..., kv0 + QK_W:].reshape(B, L, DIFF_HEADS, DIFF_VD)
    if need_ctx:
        zc = hc @ w_in
        fc = zc[..., :FOURIER_W]
        qc = zc[..., FOURIER_W:kv0].reshape(B, Lc, DIFF_HEADS, 2, DIFF_HD)
        zc_kv = zc[..., kv0:]
    else:
        zc_kv = hc @ w_in[:, kv0:]
    kc = zc_kv[..., :QK_W].reshape(B, Lc, DIFF_HEADS, 2, DIFF_HD)
    vc = zc_kv[..., QK_W:].reshape(B, Lc, DIFF_HEADS, DIFF_VD)
    lam_init = 0.8 - 0.6 * math.exp(-0.3 * layer_idx)
    f32 = jnp.float32
    lam = (jnp.exp(jnp.sum(lam_q1.astype(f32) * lam_k1.astype(f32)))
           - jnp.exp(jnp.sum(lam_q2.astype(f32) * lam_k2.astype(f32))) + lam_init)
    q = _apply_rope(q, cos, sin)
    k = _apply_rope(k, cos, sin)
    k_all = jnp.concatenate([kc, k], axis=1)
    v_all = jnp.concatenate([vc, v], axis=1)
    nb = L // Q_BLOCK
    qb = jnp.moveaxis(q.reshape(B, nb, Q_BLOCK, DIFF_HEADS, 2, DIFF_HD), 1, 0)
    o = lax.map(lambda qi: _diff_attend(qi, k_all, v_all, lam), qb)
    o = jnp.moveaxis(o, 0, 1).reshape(B, L, DIFF_HEADS, DIFF_VD)
    y = jnp.concatenate([_fourier_mix(f), _diff_post(o, subln_g, lam_init)], axis=-1)
    if need_ctx:
        oc = _diff_attend(qc, kc, vc, lam)
        yc = jnp.concatenate([_fourier_mix(fc), _diff_post(oc, subln_g, lam_init)], axis=-1)
    else:
        yc = None
    return y, yc


def _lin_recur(left, right):
    a_l, b_l = left
    a_r, b_r = right
    return a_r * a_l, a_r * b_l + b_r


def _s5_drive(u, bbar):
    B, L, _ = u.shape
    ug = u.astype(jnp.float32).reshape(B, L, S5_GROUPS, S5_GC).astype(jnp.complex64)
    return jnp.einsum('bsgi,gpi->bsgp', ug, bbar)


def _s5_scan(bu, lam_bar, h0, reverse):
    if h0 is not None:
        idx = -1 if reverse else 0
        bu = bu.at[:, idx].add(lam_bar * h0)
    a = jnp.broadcast_to(lam_bar, bu.shape)
    _, hs = lax.associative_scan(_lin_recur, (a, bu), reverse=reverse, axis=1)
    return hs


def _s5_mixer(s, sc, lam_re, lam_im, log_dt, b_re, b_im, c_re, c_im, d_skip, w_glu, b_glu, need_ctx):
    f32 = jnp.float32
    B, L, _ = s.shape
    Lc = sc.shape[1]
    lam = lax.complex(lam_re.astype(f32), lam_im.astype(f32))
    dt = jnp.exp(log_dt.astype(f32))[..., None]
    lam_bar = jnp.exp(lam * dt)
    bbar = ((lam_bar - 1) / lam)[..., None] * lax.complex(b_re.astype(f32), b_im.astype(f32))
    cmat = lax.complex(c_re.astype(f32), c_im.astype(f32))
    d = d_skip.astype(f32)
    y = d * s.astype(f32)
    yc = d * sc.astype(f32) if need_ctx else None
    for r, rev in enumerate((False, True)):
        hs_c = _s5_scan(_s5_drive(sc, bbar[r]), lam_bar[r], None, rev)
        h0 = hs_c[:, 0] if rev else hs_c[:, -1]
        hs = _s5_scan(_s5_drive(s, bbar[r]), lam_bar[r], h0, rev)
        y = y + jnp.einsum('bsgp,gip->bsgi', hs, cmat[r]).real.reshape(B, L, S5_W)
        if need_ctx:
            yc = yc + jnp.einsum('bsgp,gip->bsgi', hs_c, cmat[r]).real.reshape(B, Lc, S5_W)
    wg = w_glu.astype(f32)
    bg = b_glu.astype(f32)

    def glu(t):
        g = jax.nn.gelu(t)
        return (g * jax.nn.sigmoid(g @ wg + bg)).astype(s.dtype)

    return glu(y), (glu(yc) if need_ctx else None)


def _chunk_gmlp(u, v, w_sp, b_sp):
    B, L, _ = u.shape
    n = L // CHUNK
    vg = _layernorm(v.reshape(B, n, CHUNK, GMLP_GROUPS, GMLP_GC))
    sp = jnp.einsum('gpq,bnqgc->bnpgc', w_sp, vg) + jnp.swapaxes(b_sp, 0, 1)[:, :, None]
    return (u.reshape(B, n, CHUNK, GMLP_GROUPS, GMLP_GC) * sp).reshape(B, L, GMLP_W)


def _odd_mixer(h, hc, w_in, lam_re, lam_im, log_dt, b_re, b_im, c_re, c_im, d_skip, w_glu, b_glu,
               w_sp, b_sp, need_ctx):
    z = h @ w_in
    s = z[..., :S5_W]
    u = z[..., S5_W:S5_W + GMLP_W]
    v = z[..., S5_W + GMLP_W:]
    if need_ctx:
        zc = hc @ w_in
        sc = zc[..., :S5_W]
        uc = zc[..., S5_W:S5_W + GMLP_W]
        vc = zc[..., S5_W + GMLP_W:]
    else:
        sc = hc @ w_in[:, :S5_W]
    ys, ysc = _s5_mixer(s, sc, lam_re, lam_im, log_dt, b_re, b_im, c_re, c_im, d_skip, w_glu, b_glu, need_ctx)
    y = jnp.concatenate([ys, _chunk_gmlp(u, v, w_sp, b_sp)], axis=-1)
    yc = jnp.concatenate([ysc, _chunk_gmlp(uc, vc, w_sp, b_sp)], axis=-1) if need_ctx else None
    return y, yc


def _sq_relu_mlp(h, w1, b1, w2, b2):
    a = jax.nn.relu(h @ w1 + b1)
    return (a * a) @ w2 + b2


def setup_inputs(seed: int = 0) -> dict:
    key = jax.random.key(seed)
    ks = iter(jax.random.split(key, 40))
    f32 = jnp.float32
    D = D_MODEL

    def nrm(shape, s):
        return jax.random.normal(next(ks), shape, f32) * s

    lam_im_base = jnp.broadcast_to(jnp.pi * jnp.arange(S5_STATE, dtype=f32), (N_ODD, 2, S5_GROUPS, S5_STATE))
    return {
        'x': nrm((BATCH, SEQ, D), 1.0),
        'c': nrm((BATCH, D), 1.0),
        'ctx': nrm((BATCH, CTX_LEN, D), 1.0),
        'c_ctx': nrm((D,), 1.0),
        'w_mod': nrm((DEPTH, D, 6 * D), D ** -0.5),
        'b_mod': nrm((DEPTH, 6 * D), 0.01),
        'w_out': nrm((DEPTH, MIX_W, D), BETA * MIX_W ** -0.5),
        'b_out': nrm((DEPTH, D), 0.01),
        'ln_mix_g': 1.0 + nrm((DEPTH, D), 0.01),
        'ln_mix_b': nrm((DEPTH, D), 0.01),
        'w_ffn1': nrm((DEPTH, D, FFN_W), D ** -0.5),
        'b_ffn1': nrm((DEPTH, FFN_W), 0.01),
        'w_ffn2': nrm((DEPTH, FFN_W, D), BETA * FFN_W ** -0.5),
        'b_ffn2': nrm((DEPTH, D), 0.01),
        'ln_ffn_g': 1.0 + nrm((DEPTH, D), 0.01),
        'ln_ffn_b': nrm((DEPTH, D), 0.01),
        'w_in_ab': nrm((N_EVEN, D, EVEN_IN_W), D ** -0.5),
        'lam_q1': nrm((N_EVEN, DIFF_HD), 0.1),
        'lam_k1': nrm((N_EVEN, DIFF_HD), 0.1),
        'lam_q2': nrm((N_EVEN, DIFF_HD), 0.1),
        'lam_k2': nrm((N_EVEN, DIFF_HD), 0.1),
        'subln_g': 1.0 + nrm((N_EVEN, DIFF_VD), 0.01),
        'w_in_cd': nrm((N_ODD, D, ODD_IN_W), D ** -0.5),
        's5_lam_re': -0.5 + nrm((N_ODD, 2, S5_GROUPS, S5_STATE), 0.01),
        's5_lam_im': lam_im_base + nrm((N_ODD, 2, S5_GROUPS, S5_STATE), 0.01),
        's5_log_dt': jax.random.uniform(next(ks), (N_ODD, 2, S5_GROUPS), f32, math.log(1e-3), math.log(1e-1)),
        's5_b_re': nrm((N_ODD, 2, S5_GROUPS, S5_STATE, S5_GC), (2 * S5_GC) ** -0.5),
        's5_b_im': nrm((N_ODD, 2, S5_GROUPS, S5_STATE, S5_GC), (2 * S5_GC) ** -0.5),
        's5_c_re': nrm((N_ODD, 2, S5_GROUPS, S5_GC, S5_STATE), S5_STATE ** -0.5),
        's5_c_im': nrm((N_ODD, 2, S5_GROUPS, S5_GC, S5_STATE), S5_STATE ** -0.5),
        's5_d': nrm((N_ODD, S5_W), 1.0),
        'w_glu': nrm((N_ODD, S5_W, S5_W), S5_W ** -0.5),
        'b_glu': nrm((N_ODD, S5_W), 0.01),
        'w_sp': nrm((N_ODD, GMLP_GROUPS, CHUNK, CHUNK), CHUNK ** -0.5),
        'b_sp': 1.0 + nrm((N_ODD, GMLP_GROUPS, CHUNK), 0.01),
    }


def reference(x, c, ctx, c_ctx, w_mod, b_mod, w_out, b_out, ln_mix_g, ln_mix_b, w_ffn1, b_ffn1, w_ffn2,
              b_ffn2, ln_ffn_g, ln_ffn_b, w_in_ab, lam_q1, lam_k1, lam_q2, lam_k2, subln_g, w_in_cd,
              s5_lam_re, s5_lam_im, s5_log_dt, s5_b_re, s5_b_im, s5_c_re, s5_c_im, s5_d, w_glu, b_glu,
              w_sp, b_sp):
    L = x.shape[1]
    cos, sin = _axial_rope_tables(L)
    xc = ctx
    for l in range(DEPTH):
        last = l == DEPTH - 1
        e = l // 2
        mod = jax.nn.silu(c) @ w_mod[l] + b_mod[l]
        modc = jax.nn.silu(c_ctx) @ w_mod[l] + b_mod[l]
        sh1, sc1, g1, sh2, sc2, g2 = jnp.split(mod[:, None, :], 6, axis=-1)
        shc1, scc1, gc1, shc2, scc2, gc2 = jnp.split(modc, 6, axis=-1)
        h = _modulate(x, sh1, sc1)
        hc = _modulate(xc, shc1, scc1)
        if l % 2 == 0:
            y, yc = _even_mixer(h, hc, w_in_ab[e], lam_q1[e], lam_k1[e], lam_q2[e], lam_k2[e], subln_g[e],
                                cos, sin, l, not last)
        else:
            y, yc = _odd_mixer(h, hc, w_in_cd[e], s5_lam_re[e], s5_lam_im[e], s5_log_dt[e], s5_b_re[e],
                               s5_b_im[e], s5_c_re[e], s5_c_im[e], s5_d[e], w_glu[e], b_glu[e],
                               w_sp[e], b_sp[e], not last)
        x = _post_norm(x, y @ w_out[l] + b_out[l], g1, ln_mix_g[l], ln_mix_b[l])
        x = _post_norm(x, _sq_relu_mlp(_modulate(x, sh2, sc2), w_ffn1[l], b_ffn1[l], w_ffn2[l], b_ffn2[l]),
                       g2, ln_ffn_g[l], ln_ffn_b[l])
        if not last:
            xc = _post_norm(xc, yc @ w_out[l] + b_out[l], gc1, ln_mix_g[l], ln_mix_b[l])
            xc = _post_norm(xc, _sq_relu_mlp(_modulate(xc, shc2, scc2), w_ffn1[l], b_ffn1[l], w_ffn2[l],
                                             b_ffn2[l]), gc2, ln_ffn_g[l], ln_ffn_b[l])
    return x
```

```python
from contextlib import ExitStack
import math
import numpy as np
import ml_dtypes
import concourse.bass as bass
import concourse.mybir as mybir
from concourse.bass_utils import run_bass_kernel_spmd

F32 = mybir.dt.float32
BF16 = mybir.dt.bfloat16
AF = mybir.ActivationFunctionType
ALU = mybir.AluOpType
AX = mybir.AxisListType
ENGS = ("pe", "act", "dve", "pool", "sp")
NDS = 48

D = 1024
SEQ = 8192
NCORE = 8
TPC = 2048
NT = 16
NCT = 2
NTT = 18
ROWS = NTT * 128
ALPHA = 4 ** 0.25
LN_EPS = 1e-5
DIFF_SCALE = 0.125


class Buf:
    __slots__ = ("name", "lw", "rd", "excl")

    def __init__(self, name):
        self.name = name
        self.lw = None
        self.rd = []
        self.excl = False


class Prog:
    def __init__(self):
        self.nc = bass.Bass("TRN2", target_bir_lowering=False)
        nc = self.nc
        self.es = ExitStack()
        self.q = {e: [] for e in ENGS}
        self.cnt = {e: 0 for e in ENGS}
        self.known = {e: {} for e in ENGS}
        self.esem = {e: self.es.enter_context(nc.semaphore("s_" + e)) for e in ENGS}
        self.dsem = [self.es.enter_context(nc.semaphore("d%d" % i)) for i in range(NDS)]
        self.dcnt = [0] * NDS
        self.dnext = 0
        self.nbuf = 0
        self.ncc = 0
        self.ccsems = []

    def sb(self, name, shape, dt):
        return self.es.enter_context(self.nc.sbuf_tensor(name, list(shape), dt))

    def ps(self, name, shape, dt):
        return self.es.enter_context(self.nc.psum_tensor(name, list(shape), dt))

    def dram(self, name, shape, dt, kind=None):
        if kind is None:
            return self.nc.dram_tensor(name, list(shape), dt)
        return self.nc.dram_tensor(name, list(shape), dt, kind=kind)

    def buf(self, name=None):
        self.nbuf += 1
        return Buf(name or ("b%d" % self.nbuf))

    def bufs(self, n, name="b"):
        return [self.buf("%s%d" % (name, i)) for i in range(n)]

    def _deps(self, e, reads, writes):
        deps = []
        for b in reads:
            if b.lw is not None:
                deps.append(b.lw)
        for b in writes:
            if b.lw is not None:
                deps.append(b.lw)
            deps.extend(b.rd)
        kn = self.known[e]
        best = {}
        for (sem, val, eng) in deps:
            if eng == e and e == "pe":
                continue
            if kn.get(sem, 0) >= val:
                continue
            if best.get(sem, (None, 0))[1] < val:
                best[sem] = (sem, val)
        waits = []
        for sem, (s, val) in best.items():
            kn[sem] = val
            waits.append((s, val))
        return waits

    def _commit(self, tok, reads, writes):
        for b in writes:
            b.lw = tok
            b.rd = []
        for b in reads:
            if b not in writes:
                b.rd.append(tok)

    def op(self, e, fn, reads=(), writes=()):
        reads = list(reads)
        writes = list(writes)
        if e != "pe":
            writes = writes + [b for b in reads if b.excl and b not in writes]
        waits = self._deps(e, reads, writes)
        self.cnt[e] += 1
        tok = (self.esem[e], self.cnt[e], e)
        self.q[e].append((waits, fn, (self.esem[e], 1)))
        self._commit(tok, reads, writes)
        return tok

    def dma(self, e, out, in_, reads=(), writes=(), **kw):
        reads = list(reads)
        writes = list(writes)
        waits = self._deps(e, reads, writes)
        i = self.dnext
        self.dnext = (self.dnext + 1) % NDS
        sem = self.dsem[i]
        if self.dcnt[i] > 0 and self.known[e].get(sem, 0) < self.dcnt[i]:
            waits.append((sem, self.dcnt[i]))
            self.known[e][sem] = self.dcnt[i]
        self.dcnt[i] += 16
        tok = (sem, self.dcnt[i], "dma")
        self.q[e].append((waits, (lambda eng: eng.dma_start(out=out, in_=in_, **kw)), (sem, 16)))
        self._commit(tok, reads, writes)
        return tok

    def collective(self, kind, groups, in_ap, out_ap, reads=(), writes=()):
        e = "pool"
        reads = list(reads)
        writes = list(writes)
        waits = self._deps(e, reads, writes)
        sem = self.es.enter_context(self.nc.semaphore("cc%d" % self.ncc))
        self.ncc += 1
        self.ccsems.append(sem)
        tok = (sem, 1, "cc")

        def fn(eng):
            return eng.collective_compute(kind, ALU.bypass, replica_groups=groups,
                                          ins=[in_ap], outs=[out_ap])
        self.q[e].append((waits, fn, (sem, None)))
        self._commit(tok, reads, writes)
        return tok

    def barrier(self):
        fin = []
        for i in range(NDS):
            if self.dcnt[i] > 0:
                fin.append((self.dsem[i], self.dcnt[i]))
        for s in self.ccsems:
            fin.append((s, 1))
        for e in ENGS:
            if self.cnt[e] > 0:
                fin.append((self.esem[e], self.cnt[e]))
        for e in ENGS:
            w = []
            for (s, v) in fin:
                if self.known[e].get(s, 0) < v and not (s is self.esem[e]):
                    w.append((s, v))
                    self.known[e][s] = v
            if w:
                self.q[e].append((w, None, None))

    def finish(self):
        self.barrier()
        nc = self.nc

        def mk(e):
            def body(eng):
                for waits, fn, inc in self.q[e]:
                    for (sem, val) in waits:
                        eng.wait_ge(sem, val)
                    if fn is None:
                        continue
                    ins = fn(eng)
                    if inc is not None:
                        if inc[1] is None:
                            ins.then_inc(inc[0])
                        else:
                            ins.then_inc(inc[0], inc[1])
            return body
        with nc.Block() as block:
            block.tensor(mk("pe"))
            block.scalar(mk("act"))
            block.vector(mk("dve"))
            block.gpsimd(mk("pool"))
            block.sync(mk("sp"))
        self.es.close()
        return nc


def _rope_tables():
    pos = np.arange(SEQ)
    row = (pos // 64).astype(np.float32)
    col = (pos % 64).astype(np.float32)
    inv = np.power(np.float32(10000.0), -np.arange(16, dtype=np.float32) / np.float32(16)).astype(np.float32)
    ang = np.stack([row[:, None] * inv, col[:, None] * inv], axis=1).astype(np.float32)
    return np.cos(ang).astype(np.float32).reshape(SEQ, 32), np.sin(ang).astype(np.float32).reshape(SEQ, 32)


def _fourier_consts():
    c = np.arange(64)
    ang = 2 * np.pi * np.outer(c, c) / 64.0
    cc = np.zeros((256, 512), np.float64)
    for g in range(4):
        cc[g * 64:(g + 1) * 64, g * 64:(g + 1) * 64] = np.cos(ang)
        cc[g * 64:(g + 1) * 64, 256 + g * 64:256 + (g + 1) * 64] = np.sin(ang)
    l1 = np.arange(64)
    m1 = np.zeros((128, 128, 128), np.float64)
    for l2 in range(128):
        ph = 2 * np.pi * (l2 * l1[:, None] / 8192.0 + np.outer(l1, l1) / 64.0)
        mr, mi = np.cos(ph), np.sin(ph)
        mfull = np.block([[mr, -mi], [mi, mr]])
        perm = np.array([pt * 64 + (16 * rk + 4 * c_ + q_) for pt in range(2) for c_ in range(4) for rk in range(4) for q_ in range(4)])
        m1[l2] = mfull.T[perm, :]
    l2 = np.arange(128)
    ph3 = 2 * np.pi * np.outer(l2, l2) / 128.0
    sc = 1.0 / math.sqrt(8192.0 * 64.0)
    w3re = (np.cos(ph3) * sc).T
    w3im = (-np.sin(ph3) * sc).T
    lc = np.arange(256)
    phc = 2 * np.pi * np.outer(lc, lc) / 256.0
    scc = 1.0 / math.sqrt(256.0 * 64.0)
    cosc = (np.cos(phc) * scc).T
    sinc = (-np.sin(phc) * scc).T
    return cc.astype(np.float32), m1, w3re, w3im, cosc, sinc


def _bf(a):
    return np.asarray(a, dtype=np.float32).astype(ml_dtypes.bfloat16)


class K:
    pass


def _pieces(start, n):
    out = []
    f = start
    while f < start + n:
        r = f // 768
        o = f % 768
        ln = min(768 - o, start + n - f)
        out.append((r, o, f - start, ln))
        f += ln
    return out


def build(stage=99, dbg=False):
    P = Prog()
    nc = P.nc
    S = K()
    S.P = P
    ein = lambda name, shape, dt=F32: P.dram(name, shape, dt, kind="ExternalInput")

    xin = ein("xin", [ROWS, D])
    cT_d = ein("cT", [128, 8, 2])
    wmod_d = ein("wmod", [2, D, 1536])
    bmod_d = ein("bmod", [2, 1536])
    vec_d = ein("vecs", [2, 6, D])
    b1T_d = ein("b1T", [2, 128, 32])
    win0_d = ein("win0", [D, 2560])
    wfT_d = ein("wfT", [256, D])
    cc_d = ein("ccs", [256, 512])
    wout_d = ein("wout", [2, D, D])
    import os
    KSMALL = 'K_SMALL' in os.environ
    w1_d = ein("w1", [2, D, 4096]) if not KSMALL else None
    w2_d = ein("w2", [2, 4096, D]) if not KSMALL else None
    wcd_d = ein("wcd", [D, 1536])
    s5lam_d = ein("s5lam", [128, 2, 64])
    s5dt_d = ein("s5dt", [128, 64])
    s5bT_d = ein("s5bT", [128, 2, 2, 4, 64])
    s5cT_d = ein("s5cT", [128, 2, 32, 16])
    s5cst_d = ein("s5cst", [128, 140])
    s5d_d = ein("s5d", [128, 4])
    wglu_d = ein("wglu", [512, 512])
    bgluT_d = ein("bgluT", [128, 4])
    wspT_d = ein("wspT", [4, 128, 128])
    bspT_d = ein("bspT", [128, 4])
    rope_d = ein("rope", [TPC, 64])
    ident_d = ein("ident", [128, 128])
    lamv_d = ein("lamv", [4, 64])
    subg_d = ein("subg", [1, 128])
    m1_d = ein("m1c", [128, 128, 128], BF16)
    w3_d = ein("w3c", [128, 64], BF16)
    dftc_d = ein("dftc", [256, 512], BF16)
    yout = P.dram("yout", [TPC, D], F32, kind="ExternalOutput")
    if dbg:
        S.dbg = {}

    xres = P.dram("xres", [ROWS, D], F32)
    ymix = P.dram("ymix", [ROWS, D], BF16)
    S.b_xres = P.bufs(NTT, "xres")
    S.b_ymix = P.bufs(NTT, "ymix")
    cc_mod_in = P.dram("cc_mod_in", [2, 3072], F32)
    cc_mod_out = P.dram("cc_mod_out", [8, 3072], F32)
    b_ccmi, b_ccmo = P.buf(), P.buf()
    wout_b = P.dram("wout_b", [2, D, D], BF16)
    w1_b = P.dram("w1_b", [2, D, 4096], BF16)
    w2_b = P.dram("w2_b", [2, 4096, D], BF16)
    b_woutb, b_w1b, b_w2b = P.bufs(2, "woutb"), P.bufs(2, "w1b"), P.bufs(2, "w2b")

    ident_f = P.sb("ident_f", [128, 128], F32)
    ident_b = P.sb("ident_b", [128, 128], BF16)
    b_ident = P.buf()
    epsc = P.sb("epsc", [128, 1], F32)
    b_eps = P.buf()
    P.op("pool", lambda e: e.memset(epsc[:, :], LN_EPS), writes=[b_eps])
    P.dma("sp", ident_f[:, :], ident_d[:, :], writes=[b_ident])
    P.op("dve", lambda e: e.tensor_copy(out=ident_b[:, :], in_=ident_f[:, :]), reads=[b_ident], writes=[b_ident])

    psum_all = P.ps("psum_all", [128, 4096], F32)
    pb = [psum_all[:, 512 * i:512 * (i + 1)] for i in range(8)]
    S.psum_all = psum_all
    b_pb = P.bufs(8, "pb")
    for b_ in b_pb:
        b_.excl = True
    S.pb, S.b_pb = pb, b_pb

    ARENA_EL = 49152
    arena = P.sb("arena", [128, ARENA_EL], BF16)
    S.arena = arena

    def av(off, shape, dt=BF16):
        n = 1
        for s_ in shape[1:]:
            n *= s_
        el = n * (2 if dt == F32 else 1)
        assert off + el <= ARENA_EL, (off, el)
        ap = arena[0:shape[0], off:off + el]
        if dt == F32:
            ap = ap.bitcast(F32)
        if len(shape) == 3:
            ap = ap.rearrange("p (a b) -> p a b", a=shape[1])
        elif len(shape) == 4:
            ap = ap.rearrange("p (a b c) -> p a b c", a=shape[1], b=shape[2])
        return ap
    S.av = av

    for l in range(2):
        P.dma("pool", wout_b[l, :, :], wout_d[l, :, :], writes=[b_woutb[l]])
    for l in range(0 if KSMALL else 2):
        for hh in range(4):
            P.dma("pool", w1_b[l, :, hh * 1024:(hh + 1) * 1024], w1_d[l, :, hh * 1024:(hh + 1) * 1024], writes=[b_w1b[l]])
            P.dma("pool", w2_b[l, hh * 1024:(hh + 1) * 1024, :], w2_d[l, hh * 1024:(hh + 1) * 1024, :], writes=[b_w2b[l]])

    cT = P.sb("cT_s", [128, 8, 2], F32)
    scT = P.sb("scT", [128, 8, 2], F32)
    b_cT = P.buf()
    P.dma("sp", cT[:, :, :], cT_d[:, :, :], writes=[b_cT])
    P.op("act", lambda e: e.activation(out=scT[:, :, :], in_=cT[:, :, :], func=AF.Silu), reads=[b_cT], writes=[b_cT])
    wst = [av(s_ * 3072, [128, 1536], F32) for s_ in range(2)]
    b_wst = P.bufs(2, "wst")
    bm3 = av(6144, [2, 3072], F32)
    mod3 = av(6144 + 6144, [2, 3072], F32)
    b_bm3, b_mod3 = P.buf(), P.buf()
    P.dma("sp", bm3, bmod_d.ap().rearrange("l n -> (l n)").partition_broadcast(2), writes=[b_bm3])
    ci = 0
    for l in range(2):
        for kc in range(8):
            s_ = ci % 2
            ci += 1
            P.dma("sp", wst[s_], wmod_d[l, kc * 128:(kc + 1) * 128, :], writes=[b_wst[s_]])
            for c3 in range(3):
                bank = l * 3 + c3
                P.op("pe", (lambda l, kc, c3, bank, s_: lambda e: e.matmul(
                    psum_all[0:2, 512 * bank:512 * bank + 512], lhsT=scT[:, kc, :], rhs=wst[s_][:, c3 * 512:(c3 + 1) * 512],
                    start=(kc == 0), stop=(kc == 7)))(l, kc, c3, bank, s_),
                    reads=[b_cT, b_wst[s_]], writes=[b_pb[bank]])
    for l in range(2):
        for c3 in range(3):
            bank = l * 3 + c3
            c0 = l * 1536 + c3 * 512
            P.op("dve", (lambda bank, c0: lambda e: e.tensor_tensor(
                out=mod3[:, c0:c0 + 512], in0=psum_all[0:2, 512 * bank:512 * bank + 512], in1=bm3[:, c0:c0 + 512], op=ALU.add))(bank, c0),
                reads=[b_pb[bank], b_bm3], writes=[b_mod3])
    P.dma("sp", cc_mod_in[:, :], mod3, reads=[b_mod3], writes=[b_ccmi])
    P.collective("AllGather", [[0, 1, 2, 3], [4, 5, 6, 7]], cc_mod_in.ap().opt(), cc_mod_out.ap().opt(), reads=[b_ccmi], writes=[b_ccmo])

    modv = P.dram("modv", [2, 2, 6144], F32)
    b_modv = P.buf()
    ccmo = cc_mod_out.ap().rearrange("(r w) n -> r w n", w=2)
    for l in range(2):
        for who in range(2):
            P.dma("sp", modv[who, l, :].rearrange("(r o) -> r o", r=4), ccmo[:, who, l * 1536:(l + 1) * 1536],
                  reads=[b_ccmo], writes=[b_modv])
    modT = P.sb("modT", [128, 2, 2, 48], F32)
    b_modT = P.buf()
    for l in range(2):
        for who in range(2):
            P.dma("sp", modT[:, l, who, :], modv[who, l, :].rearrange("(q p) -> p q", p=128), reads=[b_modv], writes=[b_modT],
                  allow_slow_non_contiguous=True)
    for kind in (1, 4):
        P.op("dve", (lambda kind: lambda e: e.tensor_scalar(
            out=modT[:, :, :, kind * 8:(kind + 1) * 8], in0=modT[:, :, :, kind * 8:(kind + 1) * 8],
            scalar1=1.0, scalar2=None, op0=ALU.add))(kind), reads=[b_modT], writes=[b_modT])
    S.modT, S.b_modT, S.modv, S.b_modv = modT, b_modT, modv, b_modv
    if dbg:
        d = P.dram("dbg_modT", [128, 192], F32, kind="ExternalOutput")
        P.dma("sp", d[:, :], modT[:, :, :, :].rearrange("p a b c -> p (a b c)"), reads=[b_modT])
    if stage == 0:
        return P.finish()

    xt = [P.sb("xt%d" % i, [128, D], F32) for i in range(2)]
    b_xt = P.bufs(2, "xt")
    xn = [P.sb("xn%d" % i, [128, D], BF16) for i in range(2)]
    b_xn = P.bufs(2, "xn")
    stt = [P.sb("stt%d" % i, [128, 16], F32) for i in range(2)]
    b_stt = P.bufs(2, "stt")

    def ln_stats(src, b_src, slot, eng_rs="act"):
        st = stt[slot]
        P.op("dve", lambda e: e.bn_stats(out=st[:, 0:6], in_=src[:, 0:512]), reads=[b_src], writes=[b_stt[slot]])
        P.op("dve", lambda e: e.bn_stats(out=st[:, 6:12], in_=src[:, 512:1024]), reads=[b_src], writes=[b_stt[slot]])
        P.op("dve", lambda e: e.bn_aggr(out=st[:, 12:14], in_=st[:, 0:12]), reads=[b_stt[slot]], writes=[b_stt[slot]])
        P.op("act", lambda e: e.activation(out=st[:, 14:15], in_=st[:, 13:14], func=AF.Sqrt, bias=epsc[:, 0:1], scale=1.0),
             reads=[b_stt[slot], b_eps], writes=[b_stt[slot]])
        P.op("dve", lambda e: e.reciprocal(out=st[:, 14:15], in_=st[:, 14:15]), reads=[b_stt[slot]], writes=[b_stt[slot]])
        return st[:, 12:13], st[:, 14:15]

    def ln_to_hT(t, x_src_ap, b_xsrc, slot, l, kinds, hT_ap_fn, b_hT, tbank):
        who = 0 if t < NT else 1
        P.dma("sp", xt[slot][:, :], x_src_ap, reads=[b_xsrc], writes=[b_xt[slot]])
        mean, rstd = ln_stats(xt[slot], b_xt[slot], slot)
        P.op("dve", lambda e: e.tensor_scalar(out=xn[slot][:, :], in0=xt[slot][:, :], scalar1=mean, scalar2=rstd,
                                              op0=ALU.subtract, op1=ALU.mult),
             reads=[b_xt[slot], b_stt[slot]], writes=[b_xn[slot]])
        ptb = psum_all[:, tbank * 512:tbank * 512 + 1024]
        for kc in range(8):
            P.op("pe", (lambda kc: lambda e: e.matmul(ptb[:, kc * 128:(kc + 1) * 128], lhsT=xn[slot][:, kc * 128:(kc + 1) * 128],
                                                      rhs=ident_b[:, :], start=True, stop=True))(kc),
                 reads=[b_xn[slot], b_ident], writes=[b_pb[tbank + kc // 4]])
        ksh, ksc = kinds
        for kc in range(8):
            sc_ap = modT[:, l, who, ksc * 8 + kc:ksc * 8 + kc + 1]
            sh_ap = modT[:, l, who, ksh * 8 + kc:ksh * 8 + kc + 1]
            if kc % 2 == 0:
                P.op("act", (lambda kc, sc_ap, sh_ap: lambda e: e.activation(
                    out=hT_ap_fn(kc), in_=ptb[:, kc * 128:(kc + 1) * 128], func=AF.Identity, bias=sh_ap, scale=sc_ap))(kc, sc_ap, sh_ap),
                    reads=[b_pb[tbank + kc // 4], b_modT], writes=[b_hT[0]])
            else:
                P.op("dve", (lambda kc, sc_ap, sh_ap: lambda e: e.tensor_scalar(
                    out=hT_ap_fn(kc), in0=ptb[:, kc * 128:(kc + 1) * 128], scalar1=sc_ap, scalar2=sh_ap,
                    op0=ALU.mult, op1=ALU.add))(kc, sc_ap, sh_ap),
                    reads=[b_pb[tbank + kc // 4], b_modT], writes=[b_hT[1]])

    S.ln_stats, S.ln_to_hT = ln_stats, ln_to_hT

    def x_src(t, first_layer):
        return (xin[t * 128:(t + 1) * 128, :] if first_layer else xres[t * 128:(t + 1) * 128, :])

    WIN_N = 2816
    win = arena[:, 0:8 * WIN_N].rearrange("p (k n) -> p k n", k=8)
    b_win = P.buf()
    P.barrier()
    P.dma("pool", win[:, :, 512:WIN_N], win0_d.ap()[:, 256:2560].rearrange("(k p) n -> p k n", p=128), writes=[b_win])
    wfT = av(8 * WIN_N, [128, 2, D], F32)
    ccs = av(8 * WIN_N + 4096, [128, 2, 512], F32)
    b_wfT = P.buf()
    P.dma("sp", wfT, wfT_d.ap().rearrange("(c p) k -> p c k", p=128), writes=[b_wfT])
    P.dma("sp", ccs, cc_d.ap().rearrange("(c p) n -> p c n", p=128), writes=[b_wfT])
    for kc in range(8):
        bank = kc % 4
        for c in range(2):
            P.op("pe", (lambda kc, c, bank: lambda e: e.matmul(pb[bank], lhsT=wfT[:, c, kc * 128:(kc + 1) * 128], rhs=ccs[:, c, :],
                                                               start=(c == 0), stop=(c == 1)))(kc, c, bank),
                 reads=[b_wfT], writes=[b_pb[bank]])
        P.op("act" if kc % 2 else "dve",
             (lambda kc, bank: (lambda e: e.copy(out=win[:, kc, 0:512], in_=pb[bank])) if kc % 2 else
              (lambda e: e.tensor_copy(out=win[:, kc, 0:512], in_=pb[bank])))(kc, bank),
             reads=[b_pb[bank]], writes=[b_win])

    import os
    KCUT = int(os.environ.get('K_CUT', '99'))
    if KCUT == 1:
        return P.finish()
    cc_f_in = [P.dram("cc_f_in%d" % c, [512, 512], BF16) for c in range(4)]
    cc_f_out = [P.dram("cc_f_out%d" % c, [2048, 512], BF16) for c in range(4)]
    cc_k_in = [P.dram("cc_k_in%d" % h, [128, TPC], BF16) for h in range(6)]
    cc_k_out = [P.dram("cc_k_out%d" % h, [512, TPC], BF16) for h in range(6)]
    ccval_i = [P.dram("ccval_i%d" % h, [TPC, 128], BF16) for h in range(6)]
    ccval_o = [P.dram("ccval_o%d" % h, [4 * TPC, 128], BF16) for h in range(6)]
    b_ccf_in, b_ccf_out = P.bufs(4, "ccfi"), P.bufs(4, "ccfo")
    b_cck_in, b_cck_out = P.bufs(6, "ccki"), P.bufs(6, "ccko")
    b_ccvali, b_ccvalo = P.bufs(6, "ccvi"), P.bufs(6, "ccvo")
    G4 = [[0, 1, 2, 3], [4, 5, 6, 7]]

    rope_sb = P.sb("rope_sb", [128, NT, 64], F32)
    b_rope = P.buf()
    P.dma("sp", rope_sb[:, :, :], rope_d.ap().rearrange("(t p) c -> p t c", p=128), writes=[b_rope])
    qT_all = P.sb("qT_all", [128, 6, ROWS], BF16)
    b_qT = P.buf()
    kTc = P.sb("kTc", [128, 6, 256], BF16)
    vc = P.sb("vc", [128, 2, 6, 129], BF16)
    abc = P.sb("abc", [128, 2, 512], BF16)
    b_kTc, b_vc, b_abc = P.buf(), P.buf(), P.buf()
    P.op("pool", lambda e: e.memset(vc[:, :, :, 128:129], 1.0), writes=[b_vc])
    hT1 = [P.sb("hT1_%d" % i, [128, 8, 128], BF16) for i in range(2)]
    b_hT1 = [P.bufs(2, "hT1_%d_" % i) for i in range(2)]
    zqk = av(28672, [128, 1536], F32)
    b_zqk = P.buf()
    rtmp = [av(28672 + 3072 + i * 1536, [128, 768], F32) for i in range(4)]
    b_rtmp = P.bufs(4, "rtmp")
    qkb = P.sb("qkb", [128, 1536], BF16)
    b_qkb = P.buf()
    ab_st = [P.sb("ab_st%d" % i, [128, 512], BF16) for i in range(2)]
    v_st = [P.sb("v_st%d" % i, [128, 768], BF16) for i in range(2)]
    kT_st = [P.sb("kT_st%d" % i, [128, 6, 128], BF16) for i in range(2)]
    b_ab, b_v, b_kTst = P.bufs(2, "ab"), P.bufs(2, "vst"), P.bufs(2, "kTst")

    import os
    def l0_tile(t):
        slot = t % 2
        main = t < NT
        ln_to_hT(t, x_src(t, True), Buf("xin"), slot, 0, (0, 1), (lambda kc, slot=slot: hT1[slot][:, kc, :]), b_hT1[slot], 6)
        if KCUT == 2:
            return P.finish()
        for cg in range(6):
            n0 = cg * 512
            n1 = min(WIN_N, n0 + 512)
            for kc in range(8):
                P.op("pe", (lambda cg, kc, n0, n1: lambda e: e.matmul(
                    psum_all[:, n0:n1], lhsT=hT1[slot][:, kc, :], rhs=win[:, kc, n0:n1], start=(kc == 0), stop=(kc == 7)))(cg, kc, n0, n1),
                    reads=b_hT1[slot] + [b_win], writes=[b_pb[cg]])
        if KCUT == 3:
            return P.finish()
        if main:
            P.op("act", lambda e, slot=slot: e.copy(out=ab_st[slot][:, :], in_=pb[0]), reads=[b_pb[0]], writes=[b_ab[slot]])
            if 'a' not in os.environ.get('K_NODMA', ''):
                P.dma("sp", cc_f_in[t // 4][(t % 4) * 128:(t % 4 + 1) * 128, :], ab_st[slot][:, :], reads=[b_ab[slot]], writes=[b_ccf_in[t // 4]])
        else:
            P.op("act", lambda e, t=t: e.copy(out=abc[:, t - NT, :], in_=pb[0]), reads=[b_pb[0]], writes=[b_abc])
        if main:
            for bk in range(3):
                P.op("act", lambda e, bk=bk: e.copy(out=zqk[:, bk * 512:(bk + 1) * 512], in_=pb[1 + bk]),
                     reads=[b_pb[1 + bk]], writes=[b_zqk])
            zv = zqk.rearrange("p (u a h f) -> p u a h f", u=24, a=2, h=2)
            ov = qkb[:, :].rearrange("p (u a h f) -> p u a h f", u=24, a=2, h=2)
            t1, t2 = zv[:, :, :, 0, :], zv[:, :, :, 1, :]
            cs = rope_sb[:, t, 0:32].rearrange("p (a f) -> p a f", a=2).unsqueeze(1).broadcast_to([128, 24, 2, 16])
            sn = rope_sb[:, t, 32:64].rearrange("p (a f) -> p a f", a=2).unsqueeze(1).broadcast_to([128, 24, 2, 16])
            rv = [r.rearrange("p (u a f) -> p u a f", u=24, a=2) for r in rtmp]
            P.op("dve", lambda e: e.tensor_tensor(out=rv[0], in0=t1, in1=cs, op=ALU.mult), reads=[b_zqk, b_rope], writes=[b_rtmp[0]])
            P.op("pool", lambda e: e.tensor_tensor(out=rv[1], in0=t2, in1=sn, op=ALU.mult), reads=[b_zqk, b_rope], writes=[b_rtmp[1]])
            P.op("dve", lambda e: e.tensor_tensor(out=ov[:, :, :, 0, :], in0=rv[0], in1=rv[1], op=ALU.subtract),
                 reads=[b_rtmp[0], b_rtmp[1]], writes=[b_qkb])
            P.op("pool", lambda e: e.tensor_tensor(out=rv[2], in0=t2, in1=cs, op=ALU.mult), reads=[b_zqk, b_rope], writes=[b_rtmp[2]])
            P.op("dve", lambda e: e.tensor_tensor(out=rv[3], in0=t1, in1=sn, op=ALU.mult), reads=[b_zqk, b_rope], writes=[b_rtmp[3]])
            P.op("pool", lambda e: e.tensor_tensor(out=ov[:, :, :, 1, :], in0=rv[2], in1=rv[3], op=ALU.add),
                 reads=[b_rtmp[2], b_rtmp[3]], writes=[b_qkb])
        else:
            for bk in range(3):
                P.op("act" if bk % 2 else "dve",
                     (lambda bk: (lambda e: e.copy(out=qkb[:, bk * 512:(bk + 1) * 512], in_=pb[1 + bk])) if bk % 2 else
                      (lambda e: e.tensor_copy(out=qkb[:, bk * 512:(bk + 1) * 512], in_=pb[1 + bk])))(bk),
                     reads=[b_pb[1 + bk]], writes=[b_qkb])
        if KCUT == 4:
            return P.finish()
        if main:
            P.op("dve", lambda e, slot=slot: e.tensor_copy(out=v_st[slot][:, 0:512], in_=pb[4]), reads=[b_pb[4]], writes=[b_v[slot]])
            P.op("act", lambda e, slot=slot: e.copy(out=v_st[slot][:, 512:768], in_=psum_all[:, 2560:2816]), reads=[b_pb[5]], writes=[b_v[slot]])
            if 'v' not in os.environ.get('K_NODMA', ''):
                for h_ in range(6):
                    P.dma("sp", ccval_i[h_][t * 128:(t + 1) * 128, :], v_st[slot][:, h_ * 128:(h_ + 1) * 128], reads=[b_v[slot]], writes=[b_ccvali[h_]])
        else:
            ci = t - NT
            P.op("dve", lambda e, ci=ci: e.tensor_copy(out=vc[:, ci, 0:4, 0:128], in_=pb[4].rearrange("p (h e) -> p h e", h=4)),
                 reads=[b_pb[4]], writes=[b_vc])
            P.op("act", lambda e, ci=ci: e.copy(out=vc[:, ci, 4:6, 0:128], in_=psum_all[:, 2560:2816].rearrange("p (h e) -> p h e", h=2)),
                 reads=[b_pb[5]], writes=[b_vc])
        if KCUT == 5:
            return P.finish()
        if KCUT == 7:
            return None
        pq = psum_all[:, 3072:4096]
        for u in range(8):
            P.op("pe", (lambda u: lambda e: e.matmul(pq[:, u * 128:(u + 1) * 128], lhsT=qkb[:, u * 128:(u + 1) * 128], rhs=ident_b[:, :],
                                                     start=True, stop=True))(u),
                 reads=[b_qkb, b_ident], writes=[b_pb[6 + u // 4]])
        P.op("act", lambda e: e.copy(out=qT_all[:, :, t * 128:(t + 1) * 128], in_=pq[:, 0:768].rearrange("p (h k) -> p h k", h=6)),
             reads=[b_pb[6], b_pb[7]], writes=[b_qT])
        kdst = kT_st[slot][:, :, :] if main else kTc[:, :, (t - NT) * 128:(t - NT + 1) * 128]
        b_kd = b_kTst[slot] if main else b_kTc
        P.op("dve", lambda e: e.tensor_copy(out=kdst[:, 0:2, :], in_=pq[:, 768:1024].rearrange("p (h k) -> p h k", h=2)),
             reads=[b_pb[7]], writes=[b_kd])
        for u in range(8, 12):
            P.op("pe", (lambda u: lambda e: e.matmul(pq[:, (u - 8) * 128:(u - 7) * 128], lhsT=qkb[:, u * 128:(u + 1) * 128], rhs=ident_b[:, :],
                                                     start=True, stop=True))(u),
                 reads=[b_qkb, b_ident], writes=[b_pb[6]])
        P.op("dve", lambda e: e.tensor_copy(out=kdst[:, 2:6, :], in_=pq[:, 0:512].rearrange("p (h k) -> p h k", h=4)),
             reads=[b_pb[6]], writes=[b_kd])
        if main and 'k' not in os.environ.get('K_NODMA', ''):
            for h_ in range(6):
                P.dma("sp", cc_k_in[h_][:, t * 128:(t + 1) * 128], kT_st[slot][:, h_, :], reads=[b_kTst[slot]], writes=[b_cck_in[h_]])
        return None
    for t_ in ([int(v) for v in os.environ['K_TILES'].split(',')] if 'K_TILES' in os.environ else range(NTT)):
        r_ = l0_tile(t_)
        if r_ is not None:
            return r_
    if 'K_NOCC' in os.environ:
        return P.finish()
    for h_ in range(6):
        P.collective("AllGather", G4, cc_k_in[h_].ap().opt(), cc_k_out[h_].ap().opt(), reads=[b_cck_in[h_]], writes=[b_cck_out[h_]])
        P.collective("AllGather", G4, ccval_i[h_].ap().opt(), ccval_o[h_].ap().opt(), reads=[b_ccvali[h_]], writes=[b_ccvalo[h_]])
    for c_ in range(4):
        P.collective("AllGather", G4, cc_f_in[c_].ap().opt(), cc_f_out[c_].ap().opt(), reads=[b_ccf_in[c_]], writes=[b_ccf_out[c_]])
    if dbg:
        d = P.dram("dbg_qT", [128, 6 * ROWS], BF16, kind="ExternalOutput")
        P.dma("sp", d[:, :], qT_all[:, :, :].rearrange("p h t -> p (h t)"), reads=[b_qT])
        d2 = P.dram("dbg_k", [512, TPC], BF16, kind="ExternalOutput")
        P.dma("sp", d2[:, :], cc_k_out[3][:, :], reads=[b_cck_out[3]])
        d3 = P.dram("dbg_v", [4 * TPC, 128], BF16, kind="ExternalOutput")
        P.dma("sp", d3[:, :], ccval_o[2][:, :], reads=[b_ccvalo[2]])
        d4 = P.dram("dbg_f", [2048, 512], BF16, kind="ExternalOutput")
        P.dma("sp", d4[:, :], cc_f_out[1][:, :], reads=[b_ccf_out[1]])
    if stage == 1:
        return P.finish()
    P.barrier()
    SLOT_EL = 8448 + 66 * 129
    kT_h = [av(s * SLOT_EL, [128, 8448]) for s in range(2)]
    Vp = [av(s * SLOT_EL + 8448, [128, 66, 129]) for s in range(2)]
    PT = [av(2 * SLOT_EL + s * 1024, [128, 1024]) for s in range(2)]
    b_kT, b_Vp, b_PT = P.bufs(2, "kTh"), P.bufs(2, "Vp"), P.bufs(2, "PT")
    for s in range(2):
        P.op("pool", lambda e, s=s: e.memset(Vp[s][:, :, 128:129], 1.0), writes=[b_Vp[s]])
    lamb = P.sb("lamb", [128, 4, 64], F32)
    lsm = P.sb("lsm", [128, 8], F32)
    gsub = P.sb("gsub", [128, 128], F32)
    b_lam = P.buf()
    P.dma("sp", lamb[:, :, :].rearrange("p a b -> p (a b)"), lamv_d.ap().rearrange("a b -> (a b)").partition_broadcast(128), writes=[b_lam])
    P.dma("sp", gsub[:, :], subg_d.ap().rearrange("a b -> (a b)").partition_broadcast(128), writes=[b_lam])
    for i in range(2):
        P.op("dve", lambda e, i=i: e.tensor_tensor(out=lamb[:, 2 * i, :], in0=lamb[:, 2 * i, :], in1=lamb[:, 2 * i + 1, :], op=ALU.mult),
             reads=[b_lam], writes=[b_lam])
        P.op("dve", lambda e, i=i: e.reduce_sum(out=lsm[:, i:i + 1], in_=lamb[:, 2 * i, :], axis=AX.X), reads=[b_lam], writes=[b_lam])
    P.op("act", lambda e: e.activation(out=lsm[:, 2:4], in_=lsm[:, 0:2], func=AF.Exp), reads=[b_lam], writes=[b_lam])
    LAM_INIT = 0.8 - 0.6 * math.exp(-0.3 * 0)
    P.op("dve", lambda e: e.tensor_tensor(out=lsm[:, 4:5], in0=lsm[:, 3:4], in1=lsm[:, 2:3], op=ALU.subtract), reads=[b_lam], writes=[b_lam])
    P.op("dve", lambda e: e.tensor_scalar(out=lsm[:, 4:5], in0=lsm[:, 4:5], scalar1=-LAM_INIT, scalar2=None, op0=ALU.add), reads=[b_lam], writes=[b_lam])
    P.op("dve", lambda e: e.tensor_scalar(out=gsub[:, :], in0=gsub[:, :], scalar1=1.0 - LAM_INIT, scalar2=None, op0=ALU.mult), reads=[b_lam], writes=[b_lam])
    neglam = lsm[:, 4:5]
    ep_r = [P.sb("ep_r%d" % i, [128, 8], F32) for i in range(2)]
    ep_o = [P.sb("ep_o%d" % i, [128, 128], F32) for i in range(2)]
    ep_j = [P.sb("ep_j%d" % i, [128, 128], F32) for i in range(2)]
    b_ep = P.bufs(2, "ep")
    yst = [P.sb("yst%d" % i, [128, 4, 128], BF16) for i in range(2)]
    b_yst = P.bufs(2, "yst")
    epi = 0
    ATT_H = int(os.environ.get('K_HEADS', '6'))
    def load_head(h, s):
        for j in range(4):
            P.dma("sp", kT_h[s][:, 256 + 2048 * j:256 + 2048 * (j + 1)], cc_k_out[h][j * 128:(j + 1) * 128, :],
                  reads=[b_cck_out[h]], writes=[b_kT[s]])
            P.dma("sp", Vp[s][:, 2 + 16 * j:2 + 16 * (j + 1), 0:128],
                  ccval_o[h][j * 2048:(j + 1) * 2048, :].rearrange("(kb p) e -> p kb e", p=128),
                  reads=[b_ccvalo[h]], writes=[b_Vp[s]])
        P.op("pool", lambda e: e.tensor_copy(out=kT_h[s][:, 0:256], in_=kTc[:, h, :]), reads=[b_kTc], writes=[b_kT[s]])
        P.op("pool", lambda e: e.tensor_copy(out=Vp[s][:, 0:2, 0:128], in_=vc[:, :, h, 0:128]), reads=[b_vc], writes=[b_Vp[s]])

    def att_block(h, s, q0, nq, nqb, i, kb, last):
        sp_i = i % 2
        for m in range(2):
            P.op("pe", (lambda m: lambda e: e.matmul(
                psum_all[:, (2 * sp_i + m) * 512:(2 * sp_i + m) * 512 + nq],
                lhsT=kT_h[s][64 * m:64 * m + 64, kb * 128:(kb + 1) * 128],
                rhs=qT_all[64 * m:64 * m + 64, h, q0:q0 + nq], start=True, stop=True))(m),
                reads=[b_kT[s], b_qT], writes=[b_pb[2 * sp_i + m]])
        P.op("act", lambda e: e.activation(
            out=PT[sp_i][:, :].rearrange("p (m q) -> p m q", m=2)[:, :, 0:nq],
            in_=psum_all[:, 2 * sp_i * 512:2 * sp_i * 512 + 1024].rearrange("p (m q) -> p m q", m=2)[:, :, 0:nq],
            func=AF.Exp, scale=DIFF_SCALE),
            reads=[b_pb[2 * sp_i], b_pb[2 * sp_i + 1]], writes=[b_PT[sp_i]])
        for qb in range(nqb):
            for m in range(2):
                P.op("pe", (lambda m, qb: lambda e: e.matmul(
                    psum_all[:, (4 + qb) * 512 + m * 129:(4 + qb) * 512 + m * 129 + 129],
                    lhsT=PT[sp_i][:, m * 512 + qb * 128:m * 512 + (qb + 1) * 128],
                    rhs=Vp[s][:, kb, 0:129], start=(i == 0 and m == 0), stop=last,
                    skip_group_check=True))(m, qb),
                    reads=[b_PT[sp_i], b_Vp[s]], writes=[b_pb[4 + qb]])

    def att_epi(h, ys, qb):
        es = qb % 2
        O = psum_all[:, (4 + qb) * 512:(4 + qb) * 512 + 258]
        r = ep_r[es]
        P.op("dve", lambda e: e.reciprocal(out=r[:, 0:2], in_=O.rearrange("p (m c) -> p m c", m=2)[:, :, 128]),
             reads=[b_pb[4 + qb]], writes=[b_ep[es]])
        P.op("dve", lambda e: e.tensor_tensor(out=r[:, 2:3], in0=r[:, 1:2], in1=neglam, op=ALU.mult),
             reads=[b_ep[es], b_lam], writes=[b_ep[es]])
        P.op("dve", lambda e: e.tensor_scalar(out=ep_o[es][:, :], in0=O[:, 0:128], scalar1=r[:, 0:1], scalar2=None, op0=ALU.mult),
             reads=[b_pb[4 + qb], b_ep[es]], writes=[b_ep[es]])
        P.op("dve", lambda e: e.scalar_tensor_tensor(out=ep_o[es][:, :], in0=O[:, 129:257], scalar=r[:, 2:3], in1=ep_o[es][:, :],
                                                     op0=ALU.mult, op1=ALU.add),
             reads=[b_pb[4 + qb], b_ep[es]], writes=[b_ep[es]])
        P.op("act", lambda e: e.activation(out=ep_j[es][:, :], in_=ep_o[es][:, :], func=AF.Square, accum_out=r[:, 3:4]),
             reads=[b_ep[es]], writes=[b_ep[es]])
        P.op("act", lambda e: e.activation(out=r[:, 4:5], in_=r[:, 3:4], func=AF.Sqrt, bias=epsc[:, 0:1], scale=1.0 / 128.0),
             reads=[b_ep[es], b_eps], writes=[b_ep[es]])
        P.op("dve", lambda e: e.reciprocal(out=r[:, 5:6], in_=r[:, 4:5]), reads=[b_ep[es]], writes=[b_ep[es]])
        P.op("dve", lambda e: e.scalar_tensor_tensor(
            out=yst[ys][:, qb, :], in0=ep_o[es][:, :], scalar=r[:, 5:6], in1=gsub[:, :], op0=ALU.mult, op1=ALU.mult),
            reads=[b_ep[es], b_lam], writes=[b_yst[ys]])

    def att_store(h, ys, q0, nq, nqb):
        t0 = q0 // 128
        P.dma("sp", ymix[q0:q0 + nq, 256 + h * 128:256 + (h + 1) * 128].rearrange("(qb p) e -> p qb e", p=128),
              yst[ys][:, 0:nqb, :], reads=[b_yst[ys]], writes=[S.b_ymix[t0 + i_] for i_ in range(nqb)])

    GROUPS = [int(v) for v in os.environ['K_GROUPS'].split(',')] if 'K_GROUPS' in os.environ else list(range(5))
    for h in range(ATT_H):
        s = h % 2
        load_head(h, s)
        for g in GROUPS:
            if g < 4:
                nq, nqb, q0, kbs = 512, 4, g * 512, list(range(66))
            else:
                nq, nqb, q0, kbs = 256, 2, 2048, [0, 1]
            for i, kb in enumerate(kbs):
                att_block(h, s, q0, nq, nqb, i, kb, i == len(kbs) - 1)
            ys = epi % 2
            epi += 1
            for qb in range(nqb):
                att_epi(h, ys, qb)
            att_store(h, ys, q0, nq, nqb)
    if dbg:
        d = P.dram("dbg_ymix", [ROWS, D], BF16, kind="ExternalOutput")
        P.dma("sp", d[:, :], ymix[:, :], reads=S.b_ymix)
    if stage == 2:
        return P.finish()
    P.barrier()
    Zd = P.dram("Zd", [128, 128, 256], BF16)
    b_Zd = P.bufs(4, "Zd")
    w3 = P.sb("w3_s", [128, 64], BF16)
    dftc = P.sb("dftc_s", [128, 2, 512], BF16)
    b_fc = P.buf()
    P.dma("sp", w3[:, :], w3_d[:, :], writes=[b_fc])
    P.dma("sp", dftc[:, :, :], dftc_d.ap().rearrange("(b p) n -> p b n", p=128), writes=[b_fc])
    Gb = [av(s_ * 8192, [128, 32, 256]) for s_ in range(1)]
    M1b = av(8192, [128, 32, 128])
    Zsb = av(8192 + 4096, [128, 32, 256])
    b_Gb, b_M1b, b_Zsb = P.buf(), P.buf(), P.buf()

    def f_stage1(blk):
        for part in range(2):
            for c_ in range(4):
                p0 = part * 64 + c_ * 16
                src = cc_f_out[c_][:, part * 256:(part + 1) * 256].rearrange("(g l) n -> g l n", l=128)[:, blk * 32:(blk + 1) * 32, :]
                P.dma("sp", Gb[0][p0:p0 + 16, :, :], src, reads=[b_ccf_out[c_]], writes=[b_Gb])
        P.dma("sp", M1b, m1_d.ap()[blk * 32:(blk + 1) * 32, :, :].rearrange("l k m -> k l m"), writes=[b_M1b])
        for l2 in range(32):
            bank = (l2 // 2) % 4
            o0 = bank * 512 + (l2 % 2) * 256
            P.op("pe", (lambda l2, o0: lambda e: e.matmul(psum_all[:, o0:o0 + 256], lhsT=M1b[:, l2, :], rhs=Gb[0][:, l2, :],
                                                          start=True, stop=True, skip_group_check=True))(l2, o0),
                 reads=[b_Gb, b_M1b], writes=[b_pb[bank]])
            if l2 % 2 == 1:
                eng = "act" if (l2 // 2) % 2 else "dve"
                P.op(eng, (lambda l2, bank, eng: (lambda e: e.copy(out=Zsb[:, l2 - 1:l2 + 1, :], in_=pb[bank].rearrange("p (a n) -> p a n", a=2)))
                           if eng == "act" else (lambda e: e.tensor_copy(out=Zsb[:, l2 - 1:l2 + 1, :], in_=pb[bank].rearrange("p (a n) -> p a n", a=2))))(l2, bank, eng),
                     reads=[b_pb[bank]], writes=[b_Zsb])
        P.dma("sp", Zd[:, blk * 32:(blk + 1) * 32, :], Zsb, reads=[b_Zsb], writes=[b_Zd[blk]])
    for blk in range(4):
        f_stage1(blk)

    Zt = [av(s_ * 2048, [128, 8, 256]) for s_ in range(2)]
    Ysb = av(4096, [32, 8, 256])
    b_Zt, b_Ysb = P.buf(), P.buf()
    P.barrier()

    def f_stage3(bt):
        for half in range(2):
            P.dma("sp", Zt[half], Zd[half * 64 + bt * 8:half * 64 + (bt + 1) * 8, :, :].rearrange("a l n -> l a n"),
                  reads=b_Zd, writes=[b_Zt])
        for pr in range(4):
            bank = pr % 2
            for half in range(2):
                P.op("pe", (lambda pr, half, bank: lambda e: e.matmul(
                    psum_all[0:32, bank * 512:(bank + 1) * 512], lhsT=w3[:, half * 32:(half + 1) * 32],
                    rhs=Zt[half][:, 2 * pr:2 * pr + 2, :], start=(half == 0), stop=(half == 1)))(pr, half, bank),
                    reads=[b_Zt, b_fc], writes=[b_pb[bank]])
            P.op("act", (lambda pr, bank: lambda e: e.copy(out=Ysb[:, 2 * pr:2 * pr + 2, :],
                                                           in_=psum_all[0:32, bank * 512:(bank + 1) * 512].rearrange("p (a n) -> p a n", a=2)))(pr, bank),
                 reads=[b_pb[bank]], writes=[b_Ysb])
        dst = ymix[0:TPC, 0:256].rearrange("(a b) n -> a b n", b=64)[:, bt * 8:(bt + 1) * 8, :]
        P.dma("sp", dst, Ysb, reads=[b_Ysb], writes=[S.b_ymix[t_] for t_ in range(NT)])
    for bt in range(8):
        f_stage3(bt)
    yc = av(8192, [128, 256])
    b_yc = P.buf()
    for lt in range(2):
        k_ = 0
        for lb in range(2):
            for part in range(2):
                P.op("pe", (lambda lt, lb, part, k_: lambda e: e.matmul(
                    psum_all[:, 1024:1280], lhsT=dftc[:, lb, part * 256 + lt * 128:part * 256 + (lt + 1) * 128],
                    rhs=abc[:, lb, part * 256:(part + 1) * 256], start=(k_ == 0), stop=(k_ == 3)))(lt, lb, part, k_),
                    reads=[b_fc, b_abc], writes=[b_pb[2]])
                k_ += 1
        P.op("dve", lambda e: e.tensor_copy(out=yc, in_=psum_all[:, 1024:1280]), reads=[b_pb[2]], writes=[b_yc])
        P.dma("sp", ymix[TPC + lt * 128:TPC + (lt + 1) * 128, 0:256], yc, reads=[b_yc], writes=[S.b_ymix[NT + lt]])
    if dbg:
        d = P.dram("dbg_yf", [ROWS, 256], BF16, kind="ExternalOutput")
        P.dma("sp", d[:, :], ymix[:, 0:256], reads=S.b_ymix)
    pnb = P.sb("pnb", [128, 5, D], F32)
    b_pnb = P.buf()
    yt = [P.sb("yt%d" % i, [128, D], BF16) for i in range(2)]
    b_yt = P.bufs(2, "yt")
    pt1 = [P.sb("pt1_%d" % i, [128, D], F32) for i in range(2)]
    b_pt1 = P.bufs(2, "pt1")
    b1T = P.sb("b1T_s", [128, 32], F32)
    b_b1T = P.buf()

    def load_pn(l, kind_gate, vi_bias, vi_g, vi_b):
        for who in range(2):
            P.dma("sp", pnb[:, who, :], modv[who, l, kind_gate * 1024:(kind_gate + 1) * 1024].partition_broadcast(128),
                  reads=[b_modv], writes=[b_pnb])
        for k_, vi in enumerate((vi_bias, vi_g, vi_b)):
            P.dma("sp", pnb[:, 2 + k_, :], vec_d[l, vi, :].partition_broadcast(128), writes=[b_pnb])

    def post_norm(t, ypsum, ybanks, x_ap, b_x, out_ap, b_out_list, slot):
        who = 0 if t < NT else 1
        p1 = pt1[slot]
        P.dma("sp", xt[slot][:, :], x_ap, reads=[b_x], writes=[b_xt[slot]])
        P.op("dve", lambda e: e.tensor_tensor(out=p1[:, :], in0=ypsum, in1=pnb[:, 2, :], op=ALU.add),
             reads=[b_pb[ybanks[0]], b_pb[ybanks[1]], b_pnb], writes=[b_pt1[slot]])
        P.op("pool", lambda e: e.tensor_tensor(out=p1[:, :], in0=p1[:, :], in1=pnb[:, who, :], op=ALU.mult),
             reads=[b_pnb], writes=[b_pt1[slot]])
        P.op("dve", lambda e: e.scalar_tensor_tensor(out=p1[:, :], in0=xt[slot][:, :], scalar=ALPHA, in1=p1[:, :],
                                                     op0=ALU.mult, op1=ALU.add),
             reads=[b_xt[slot]], writes=[b_pt1[slot]])
        mean, rstd = ln_stats(p1, b_pt1[slot], slot)
        P.op("dve", lambda e: e.tensor_scalar(out=p1[:, :], in0=p1[:, :], scalar1=mean, scalar2=rstd, op0=ALU.subtract, op1=ALU.mult),
             reads=[b_stt[slot]], writes=[b_pt1[slot]])
        P.op("pool", lambda e: e.tensor_tensor(out=p1[:, :], in0=p1[:, :], in1=pnb[:, 3, :], op=ALU.mult), reads=[b_pnb], writes=[b_pt1[slot]])
        P.op("pool", lambda e: e.tensor_tensor(out=p1[:, :], in0=p1[:, :], in1=pnb[:, 4, :], op=ALU.add), reads=[b_pnb], writes=[b_pt1[slot]])
        P.dma("sp", out_ap, p1[:, :], reads=[b_pt1[slot]], writes=b_out_list)

    def out_proj_phase(l, first_layer):
        P.barrier()
        wout_sb = av(0, [128, 8, D])
        b_wo = P.buf()
        P.dma("sp", wout_sb, wout_b[l, :, :].rearrange("(k p) n -> p k n", p=128), reads=[b_woutb[l]], writes=[b_wo])
        load_pn(l, 2, 0, 1, 2)
        yT = [av(8192 + s_ * 1024, [128, 8, 128]) for s_ in range(2)]
        b_yT = P.bufs(2, "yT")

        def tile(t):
            slot = t % 2
            P.dma("sp", yt[slot][:, :], ymix[t * 128:(t + 1) * 128, :], reads=[S.b_ymix[t]], writes=[b_yt[slot]])
            pq = psum_all[:, 3072:4096]
            for kc in range(8):
                P.op("pe", (lambda kc: lambda e: e.matmul(pq[:, kc * 128:(kc + 1) * 128], lhsT=yt[slot][:, kc * 128:(kc + 1) * 128],
                                                          rhs=ident_b[:, :], start=True, stop=True))(kc),
                     reads=[b_yt[slot], b_ident], writes=[b_pb[6 + kc // 4]])
            P.op("act", lambda e: e.copy(out=yT[slot][:, 0:4, :], in_=pq[:, 0:512].rearrange("p (k t) -> p k t", k=4)),
                 reads=[b_pb[6]], writes=[b_yT[slot]])
            P.op("dve", lambda e: e.tensor_copy(out=yT[slot][:, 4:8, :], in_=pq[:, 512:1024].rearrange("p (k t) -> p k t", k=4)),
                 reads=[b_pb[7]], writes=[b_yT[slot]])
            for hf in range(2):
                for kc in range(8):
                    P.op("pe", (lambda hf, kc: lambda e: e.matmul(pb[hf], lhsT=yT[slot][:, kc, :], rhs=wout_sb[:, kc, hf * 512:(hf + 1) * 512],
                                                                  start=(kc == 0), stop=(kc == 7)))(hf, kc),
                         reads=[b_yT[slot], b_wo], writes=[b_pb[hf]])
            x_ap = x_src(t, first_layer)
            post_norm(t, psum_all[:, 0:1024], (0, 1), x_ap, (Buf("xin") if first_layer else S.b_xres[t]),
                      xres[t * 128:(t + 1) * 128, :], [S.b_xres[t]], slot)
        for t in range(NTT):
            tile(t)

    def ffn_phase(l, last_layer):
        P.barrier()
        w2_sb = av(0, [128, 32, D])
        b_w2 = P.buf()
        for q4 in range(4):
            P.dma("sp", w2_sb[:, q4 * 8:(q4 + 1) * 8, :], w2_b[l, q4 * 1024:(q4 + 1) * 1024, :].rearrange("(k p) n -> p k n", p=128),
                  reads=[b_w2b[l]], writes=[b_w2])
        w1blk = [av(32768 + s_ * 4096, [128, 8, 512]) for s_ in range(2)]
        b_w1blk = P.bufs(2, "w1blk")
        hTg = av(32768 + 8192, [128, 8, 256])
        b_hTg = P.bufs(2, "hTg")
        aT = qT_all[:, :, :].rearrange("p h t -> p (h t)")[:, 0:8192].rearrange("p (c t) -> p c t", c=32)
        b_aT = P.buf()
        P.dma("sp", b1T[:, :], b1T_d[l, :, :], writes=[b_b1T])
        load_pn(l, 5, 3, 4, 5)
        cnt = [0]

        def group(t0):
            for i_ in range(2):
                t = t0 + i_
                ln_to_hT(t, xres[t * 128:(t + 1) * 128, :], S.b_xres[t], t % 2, l, (3, 4),
                         (lambda kc, i_=i_: hTg[:, kc, i_ * 128:(i_ + 1) * 128]), b_hTg, 6)
            for hb in range(8):
                s_ = cnt[0] % 2
                cnt[0] += 1
                P.dma("sp", w1blk[s_], w1_b[l, :, hb * 512:(hb + 1) * 512].rearrange("(k p) n -> p k n", p=128),
                      reads=[b_w1b[l]], writes=[b_w1blk[s_]])
                for hc in range(4):
                    bank = 4 + (hb * 4 + hc) % 2
                    for kc in range(8):
                        P.op("pe", (lambda hc, kc, bank, s_: lambda e: e.matmul(
                            psum_all[:, bank * 512:bank * 512 + 256], lhsT=w1blk[s_][:, kc, hc * 128:(hc + 1) * 128], rhs=hTg[:, kc, :],
                            start=(kc == 0), stop=(kc == 7)))(hc, kc, bank, s_),
                            reads=[b_w1blk[s_]] + b_hTg, writes=[b_pb[bank]])
                    c_ = hb * 4 + hc
                    P.op("act", (lambda c_, bank: lambda e: e.activation(out=aT[:, c_, :], in_=psum_all[:, bank * 512:bank * 512 + 256],
                                                                         func=AF.Relu, bias=b1T[:, c_:c_ + 1], scale=1.0))(c_, bank),
                         reads=[b_pb[bank], b_b1T], writes=[b_aT])
                    P.op("pool", (lambda c_: lambda e: e.tensor_tensor(out=aT[:, c_, :], in0=aT[:, c_, :], in1=aT[:, c_, :], op=ALU.mult))(c_),
                         reads=[b_aT], writes=[b_aT])
            for i_ in range(2):
                t = t0 + i_
                for hf in range(2):
                    bank = 2 * i_ + hf
                    for c_ in range(32):
                        P.op("pe", (lambda c_, hf, bank, i_: lambda e: e.matmul(
                            pb[bank], lhsT=aT[:, c_, i_ * 128:(i_ + 1) * 128], rhs=w2_sb[:, c_, hf * 512:(hf + 1) * 512],
                            start=(c_ == 0), stop=(c_ == 31)))(c_, hf, bank, i_),
                            reads=[b_aT, b_w2], writes=[b_pb[bank]])
                if last_layer and t < NT:
                    out_ap, bl = yout[t * 128:(t + 1) * 128, :], []
                elif last_layer:
                    continue
                else:
                    out_ap, bl = xres[t * 128:(t + 1) * 128, :], [S.b_xres[t]]
                post_norm(t, psum_all[:, 2 * i_ * 512:2 * i_ * 512 + 1024], (2 * i_, 2 * i_ + 1),
                          xres[t * 128:(t + 1) * 128, :], S.b_xres[t], out_ap, bl, t % 2)
        for t0 in range(0, NTT, 2):
            group(t0)

    out_proj_phase(0, True)
    if dbg:
        d = P.dram("dbg_x1a", [ROWS, D], F32, kind="ExternalOutput")
        P.dma("sp", d[:, :], xres[:, :], reads=S.b_xres)
    if stage == 3:
        return P.finish()
    ffn_phase(0, False)
    if dbg:
        d = P.dram("dbg_x1", [ROWS, D], F32, kind="ExternalOutput")
        P.dma("sp", d[:, :], xres[:, :], reads=S.b_xres)
    if stage == 4:
        return P.finish()
    P.barrier()
    wcd = av(0, [128, 8, 1536])
    b_wcd = P.buf()
    P.dma("pool", wcd, wcd_d.ap().rearrange("(k p) n -> p k n", p=128), writes=[b_wcd])
    wspT = av(12288, [128, 4, 128])
    bspT = P.sb("bspT_s", [128, 4], F32)
    b_wsp = P.buf()
    P.dma("pool", wspT, wspT_d.ap().rearrange("g q p -> q g p"), writes=[b_wsp])
    P.dma("sp", bspT[:, :], bspT_d[:, :], writes=[b_wsp])
    hT2 = [av(12288 + 512 + s_ * 1024, [128, 8, 128]) for s_ in range(2)]
    b_hT2 = [P.bufs(2, "hT2_%d_" % i) for i in range(2)]
    vgb = [av(12288 + 512 + 2048 + s_ * 512, [128, 512]) for s_ in range(2)]
    b_vg = P.bufs(2, "vg")
    usb = [P.sb("usb%d" % i, [128, 512], F32) for i in range(1)]
    b_usb = P.buf()
    gst = P.sb("gst", [128, 4, 8], F32)
    b_gst = P.buf()
    yg = [av(12288 + 512 + 2048 + 1024 + s_ * 1024, [128, 1024]) for s_ in range(2)]
    b_yg = P.bufs(2, "yg")

    S.sT = qT_all[:, :, :].rearrange("p h t -> p (h t)")[:, 0:4 * ROWS].rearrange("p (c t) -> p c t", c=4)
    S.b_sT = P.buf()

    def l1_tile(t):
        slot = t % 2
        ln_to_hT(t, xres[t * 128:(t + 1) * 128, :], S.b_xres[t], slot, 1, (0, 1), (lambda kc: hT2[slot][:, kc, :]), b_hT2[slot], 6)
        for ct in range(4):
            for kc in range(8):
                P.op("pe", (lambda ct, kc: lambda e: e.matmul(psum_all[:, 4 * 512 + ct * 128:4 * 512 + (ct + 1) * 128], lhsT=wcd[:, kc, ct * 128:(ct + 1) * 128],
                                                              rhs=hT2[slot][:, kc, :], start=(kc == 0 and ct == 0), stop=(kc == 7),
                                                              skip_group_check=True))(ct, kc),
                     reads=b_hT2[slot] + [b_wcd], writes=[b_pb[4]])
        P.op("act", lambda e: e.copy(out=S.sT[:, :, t * 128:(t + 1) * 128], in_=pb[4].rearrange("p (c t) -> p c t", c=4)),
             reads=[b_pb[4]], writes=[S.b_sT])
        if t >= NT:
            return
        for cg in range(3):
            for kc in range(8):
                P.op("pe", (lambda cg, kc: lambda e: e.matmul(pb[cg], lhsT=hT2[slot][:, kc, :], rhs=wcd[:, kc, cg * 512:(cg + 1) * 512],
                                                              start=(kc == 0), stop=(kc == 7)))(cg, kc),
                     reads=b_hT2[slot] + [b_wcd], writes=[b_pb[cg]])
        for g in range(4):
            vsl = pb[2][:, g * 128:(g + 1) * 128]
            P.op("dve", (lambda g, vsl: lambda e: e.bn_stats(out=gst[:, g, 0:6], in_=vsl))(g, vsl), reads=[b_pb[2]], writes=[b_gst])
            P.op("dve", (lambda g: lambda e: e.bn_aggr(out=gst[:, g, 6:8], in_=gst[:, g, 0:6]))(g), reads=[b_gst], writes=[b_gst])
        P.op("act", lambda e: e.activation(out=gst[:, :, 0], in_=gst[:, :, 7], func=AF.Sqrt, bias=epsc[:, 0:1], scale=1.0),
             reads=[b_gst, b_eps], writes=[b_gst])
        P.op("dve", lambda e: e.reciprocal(out=gst[:, :, 0], in_=gst[:, :, 0]), reads=[b_gst], writes=[b_gst])
        for g in range(4):
            P.op("dve", (lambda g: lambda e: e.tensor_scalar(out=vgb[slot][:, g * 128:(g + 1) * 128], in0=pb[2][:, g * 128:(g + 1) * 128],
                                                             scalar1=gst[:, g, 6:7], scalar2=gst[:, g, 0:1], op0=ALU.subtract, op1=ALU.mult))(g),
                 reads=[b_pb[2], b_gst], writes=[b_vg[slot]])
        for g in range(4):
            P.op("pe", (lambda g: lambda e: e.matmul(pb[3][:, g * 128:(g + 1) * 128], lhsT=wspT[:, g, :], rhs=vgb[slot][:, g * 128:(g + 1) * 128],
                                                     start=(g == 0), stop=True, skip_group_check=True))(g),
                 reads=[b_vg[slot], b_wsp], writes=[b_pb[3]])
        P.op("act", lambda e: e.copy(out=usb[0][:, :], in_=pb[1]), reads=[b_pb[1]], writes=[b_usb])
        for g in range(4):
            P.op("dve", (lambda g: lambda e: e.scalar_tensor_tensor(out=yg[slot][:, 512 + g * 128:512 + (g + 1) * 128], in0=pb[3][:, g * 128:(g + 1) * 128],
                                                                    scalar=bspT[:, g:g + 1], in1=usb[0][:, g * 128:(g + 1) * 128],
                                                                    op0=ALU.add, op1=ALU.mult))(g),
                 reads=[b_pb[3], b_wsp, b_usb], writes=[b_yg[slot]])
        P.op("pool", lambda e: e.memset(yg[slot][:, 0:512], 0.0), writes=[b_yg[slot]])
        P.dma("sp", ymix[t * 128:(t + 1) * 128, :], yg[slot], reads=[b_yg[slot]], writes=[S.b_ymix[t]])
    for t in range(NTT):
        l1_tile(t)
    zt2 = P.sb("zt2", [128, D], BF16)
    b_zt = P.buf()
    P.op("pool", lambda e: e.memset(zt2[:, :], 0.0), writes=[b_zt])
    for t in range(NT, NTT):
        P.dma("sp", ymix[t * 128:(t + 1) * 128, :], zt2[:, :], reads=[b_zt], writes=[S.b_ymix[t]])
    if dbg:
        d = P.dram("dbg_y1", [ROWS, D], BF16, kind="ExternalOutput")
        P.dma("sp", d[:, :], ymix[:, :], reads=S.b_ymix)
    if stage == 5:
        return P.finish()
    def s5_branch():
        P.barrier()
        T = TPC
        o = [0]

        def alloc(shape, dt=F32):
            n = 1
            for s_ in shape[1:]:
                n *= s_
            el = n * (2 if dt == F32 else 1)
            if len(shape) == 5:
                ap = av(o[0], [shape[0], shape[1] * shape[2], shape[3], shape[4]], dt).rearrange("p (a b) c d -> p a b c d", a=shape[1])
            else:
                ap = av(o[0], shape, dt)
            o[0] += el
            return ap
        lamT = alloc([128, 2, 64])
        dtT = alloc([128, 64])
        b_pp = P.buf()
        P.dma("sp", lamT, s5lam_d[:, :, :], writes=[b_pp])
        P.dma("sp", dtT, s5dt_d[:, :], writes=[b_pp])
        wk = [alloc([128, 64]) for _ in range(10)]
        pw = alloc([128, 12, 2, 64])
        b_pw = P.buf()

        def V_(fn, rd=(), wr=()):
            P.op("dve", fn, reads=[b_pp] + list(rd), writes=[b_pp] + list(wr))
        P.op("act", lambda e: e.activation(out=dtT, in_=dtT, func=AF.Exp), reads=[b_pp], writes=[b_pp])
        a_, th, mag, s8, c8, t0_, t1_, den, qr, qi = wk
        lr, li = lamT[:, 0, :], lamT[:, 1, :]
        V_(lambda e: e.tensor_tensor(out=a_, in0=lr, in1=dtT, op=ALU.mult))
        V_(lambda e: e.tensor_tensor(out=th, in0=li, in1=dtT, op=ALU.mult))
        P.op("act", lambda e: e.activation(out=mag, in_=a_, func=AF.Exp), reads=[b_pp], writes=[b_pp])
        P.op("act", lambda e: e.activation(out=s8, in_=th, func=AF.Sin, scale=1.0 / 8.0), reads=[b_pp], writes=[b_pp])
        P.op("act", lambda e: e.activation(out=t0_, in_=th, func=AF.Sin, scale=1.0 / 16.0), reads=[b_pp], writes=[b_pp])
        V_(lambda e: e.tensor_tensor(out=t0_, in0=t0_, in1=t0_, op=ALU.mult))
        V_(lambda e: e.tensor_scalar(out=c8, in0=t0_, scalar1=-2.0, scalar2=1.0, op0=ALU.mult, op1=ALU.add))
        for _ in range(3):
            V_(lambda e: e.tensor_tensor(out=t0_, in0=c8, in1=c8, op=ALU.mult))
            V_(lambda e: e.tensor_tensor(out=t1_, in0=s8, in1=s8, op=ALU.mult))
            V_(lambda e: e.scalar_tensor_tensor(out=s8, in0=s8, scalar=2.0, in1=c8, op0=ALU.mult, op1=ALU.mult))
            V_(lambda e: e.tensor_tensor(out=c8, in0=t0_, in1=t1_, op=ALU.subtract))
        V_(lambda e: e.tensor_tensor(out=pw[:, 0, 0, :], in0=mag, in1=c8, op=ALU.mult), wr=[b_pw])
        V_(lambda e: e.tensor_tensor(out=pw[:, 0, 1, :], in0=mag, in1=s8, op=ALU.mult), wr=[b_pw])
        V_(lambda e: e.tensor_scalar(out=t0_, in0=pw[:, 0, 0, :], scalar1=-1.0, scalar2=None, op0=ALU.add))
        V_(lambda e: e.tensor_tensor(out=den, in0=lr, in1=lr, op=ALU.mult))
        V_(lambda e: e.tensor_tensor(out=t1_, in0=li, in1=li, op=ALU.mult))
        V_(lambda e: e.tensor_tensor(out=den, in0=den, in1=t1_, op=ALU.add))
        V_(lambda e: e.reciprocal(out=den, in_=den))
        V_(lambda e: e.tensor_tensor(out=qr, in0=t0_, in1=lr, op=ALU.mult))
        V_(lambda e: e.tensor_tensor(out=t1_, in0=pw[:, 0, 1, :], in1=li, op=ALU.mult))
        V_(lambda e: e.tensor_tensor(out=qr, in0=qr, in1=t1_, op=ALU.add))
        V_(lambda e: e.tensor_tensor(out=qr, in0=qr, in1=den, op=ALU.mult))
        V_(lambda e: e.tensor_tensor(out=qi, in0=pw[:, 0, 1, :], in1=lr, op=ALU.mult))
        V_(lambda e: e.tensor_tensor(out=t1_, in0=t0_, in1=li, op=ALU.mult))
        V_(lambda e: e.tensor_tensor(out=qi, in0=qi, in1=t1_, op=ALU.subtract))
        V_(lambda e: e.tensor_tensor(out=qi, in0=qi, in1=den, op=ALU.mult))
        for k in range(11):
            V_((lambda k: lambda e: e.tensor_tensor(out=t0_, in0=pw[:, k, 0, :], in1=pw[:, k, 0, :], op=ALU.mult))(k))
            V_((lambda k: lambda e: e.tensor_tensor(out=t1_, in0=pw[:, k, 1, :], in1=pw[:, k, 1, :], op=ALU.mult))(k))
            V_((lambda k: lambda e: e.scalar_tensor_tensor(out=pw[:, k + 1, 1, :], in0=pw[:, k, 0, :], scalar=2.0, in1=pw[:, k, 1, :],
                                                           op0=ALU.mult, op1=ALU.mult))(k), wr=[b_pw])
            V_((lambda k: lambda e: e.tensor_tensor(out=pw[:, k + 1, 0, :], in0=t0_, in1=t1_, op=ALU.subtract))(k), wr=[b_pw])
        qd = P.dram("s5_qd", [2, 8, 8, 64], F32)
        b_qd = P.buf()
        P.dma("sp", qd[0, :, :, :].rearrange("g rc p -> p (g rc)"), qr[0:64, :], reads=[b_pp], writes=[b_qd], allow_slow_non_contiguous=True)
        P.dma("sp", qd[1, :, :, :].rearrange("g rc p -> p (g rc)"), qi[0:64, :], reads=[b_pp], writes=[b_qd], allow_slow_non_contiguous=True)
        o_reuse = o[0]
        BT = alloc([128, 2, 2, 4, 64])
        QB = alloc([128, 2, 2, 4, 64])
        WBf = alloc([128, 2, 4, 128])
        b_B = P.buf()
        P.dma("sp", BT, s5bT_d[:, :, :, :, :], writes=[b_B])
        for ri in range(2):
            for g8 in range(8):
                P.dma("sp", QB[16 * g8:16 * g8 + 16, ri, :, :, :].rearrange("p r c q -> p (r c q)"),
                      qd[ri, g8, :, :].rearrange("rc p -> (rc p)").partition_broadcast(16),
                      reads=[b_qd], writes=[b_B], allow_slow_non_contiguous=True)
        tB = [alloc([128, 2, 4, 64]) for _ in range(2)]

        def B_(fn):
            P.op("dve", fn, reads=[b_B], writes=[b_B])
        WBv = WBf.rearrange("p r c (h q) -> p r c h q", h=2)
        B_(lambda e: e.tensor_tensor(out=tB[0], in0=QB[:, 0], in1=BT[:, 0], op=ALU.mult))
        B_(lambda e: e.tensor_tensor(out=tB[1], in0=QB[:, 1], in1=BT[:, 1], op=ALU.mult))
        B_(lambda e: e.tensor_tensor(out=WBv[:, :, :, 0, :], in0=tB[0], in1=tB[1], op=ALU.subtract))
        B_(lambda e: e.tensor_tensor(out=tB[0], in0=QB[:, 0], in1=BT[:, 1], op=ALU.mult))
        B_(lambda e: e.tensor_tensor(out=tB[1], in0=QB[:, 1], in1=BT[:, 0], op=ALU.mult))
        B_(lambda e: e.tensor_tensor(out=WBv[:, :, :, 1, :], in0=tB[0], in1=tB[1], op=ALU.add))
        CT = alloc([128, 2, 32, 16])
        cst = alloc([128, 128 + 8 + 4 + 4])
        b_C = P.buf()
        P.dma("sp", CT, s5cT_d[:, :, :, :], writes=[b_C])
        P.dma("sp", cst[:, 0:140], s5cst_d[:, :], writes=[b_C])
        P.dma("sp", cst[:, 140:144], s5d_d[:, :], writes=[b_C])
        P.op("dve", lambda e: e.tensor_scalar(out=CT[64:128], in0=CT[64:128], scalar1=-1.0, scalar2=None, op0=ALU.mult), reads=[b_C], writes=[b_C])
        Smat, rowmask, onehot, dcol = cst[:, 0:128], cst[:, 128:136], cst[:, 136:140], cst[:, 140:144]
        AT = alloc([128, 12, 128])
        Bm = alloc([128, 128], BF16)
        CZ = alloc([128, 128])
        X = [alloc([128, T]) for _ in range(2)]
        Vb = alloc([128, T])
        Xc = [Vb[:, 0:256], Vb[:, 256:512]]
        Fall = alloc([128, 64, 2])
        yacc = alloc([128, 4, T])
        b_AT, b_Bm, b_CZ, b_Fall, b_yacc = P.buf(), P.buf(), P.buf(), P.buf(), P.buf()
        b_X = [P.bufs(2, "X%d_" % i) for i in range(2)]
        b_Xc = [P.bufs(2, "Xc%d_" % i) for i in range(2)]
        P.op("pool", lambda e: e.memset(CZ, 0.0), writes=[b_CZ])
        sT = S.sT

        def build_AT(col):
            for k in range(12):
                P.op("dve", (lambda k: lambda e: e.tensor_scalar(out=AT[:, k, :], in0=ident_f[:, :], scalar1=pw[:, k, 0, col:col + 1],
                                                                 scalar2=None, op0=ALU.mult))(k), reads=[b_pw, b_ident], writes=[b_AT])
                P.op("dve", (lambda k: lambda e: e.scalar_tensor_tensor(out=AT[:, k, :], in0=Smat, scalar=pw[:, k, 1, col:col + 1], in1=AT[:, k, :],
                                                                        op0=ALU.mult, op1=ALU.add))(k), reads=[b_pw, b_C, b_AT], writes=[b_AT])

        def scan_level(Xb, b_Xb, n, r, k, cw, cur, ev):
            sh = 1 << k
            lo_all, hi_all = (sh, n) if r == 0 else (0, n - sh)
            if r == 0:
                ca, cb = 0, min(sh, n)
            else:
                ca, cb = max(n - sh, 0), n
            P.op("act", lambda e: e.copy(out=Xb[1 - cur][:, ca:cb], in_=Xb[cur][:, ca:cb]), reads=b_Xb[cur], writes=[b_Xb[1 - cur][0]])
            for c0 in range(lo_all, hi_all, cw):
                c1 = min(hi_all, c0 + cw)
                bank = 5 + (ev[0] % 2)
                ev[0] += 1
                s0 = c0 - sh if r == 0 else c0 + sh
                P.op("pe", (lambda c0, c1, bank, s0: lambda e: e.matmul(
                    psum_all[:, bank * 512:bank * 512 + c1 - c0], lhsT=AT[:, k, :], rhs=Xb[cur][:, s0:s0 + c1 - c0],
                    start=True, stop=True))(c0, c1, bank, s0),
                    reads=b_Xb[cur] + [b_AT], writes=[b_pb[bank]])
                P.op("dve", (lambda c0, c1, bank: lambda e: e.tensor_tensor(
                    out=Xb[1 - cur][:, c0:c1], in0=psum_all[:, bank * 512:bank * 512 + c1 - c0], in1=Xb[cur][:, c0:c1], op=ALU.add))(c0, c1, bank),
                    reads=[b_pb[bank]] + b_Xb[cur], writes=[b_Xb[1 - cur][1]])

        def scan2(r):
            ev = [0]
            cur, curc = 0, 0
            for k in range(11):
                scan_level(X, b_X, T, r, k, 512, cur, ev)
                cur = 1 - cur
                if k < 8:
                    scan_level(Xc, b_Xc, 256, r, k, 256, curc, ev)
                    curc = 1 - curc
            return cur, curc

        def drive(dst, b_dst, ct, tok0, n, cw):
            for c0 in range(0, n, cw):
                c1 = min(n, c0 + cw)
                P.op("pe", (lambda c0, c1: lambda e: e.matmul(psum_all[:, 4 * 512:4 * 512 + c1 - c0], lhsT=Bm, rhs=sT[:, ct, tok0 + c0:tok0 + c1],
                                                              start=True, stop=True))(c0, c1), reads=[b_Bm, S.b_sT], writes=[b_pb[4]])
                P.op("act", (lambda c0, c1: lambda e: e.copy(out=dst[:, c0:c1], in_=psum_all[:, 4 * 512:4 * 512 + c1 - c0]))(c0, c1),
                     reads=[b_pb[4]], writes=b_dst)

        def out_contrib(src, b_src, first, last):
            for c in range(4):
                P.op("pe", (lambda c: lambda e: e.matmul(pb[c], lhsT=CZ, rhs=src[:, c * 512:(c + 1) * 512], start=first, stop=last,
                                                         skip_group_check=True))(c), reads=[b_CZ] + list(b_src), writes=[b_pb[c]])

        def set_group(ct, g8, r):
            g = ct * 8 + g8
            col = g8 * 8 + r * 4 + ct
            build_AT(col)
            P.op("dve", lambda e: e.tensor_scalar(out=Bm, in0=WBf[:, r, ct, :], scalar1=rowmask[:, g8:g8 + 1], scalar2=None, op0=ALU.mult),
                 reads=[b_B, b_C], writes=[b_Bm])
            P.op("act", lambda e: e.copy(out=CZ[:, 16 * g8:16 * g8 + 16], in_=CT[:, r, g, :]), reads=[b_C], writes=[b_CZ])
            return g, col

        def clear_group(g8):
            P.op("pool", lambda e: e.memset(CZ[:, 16 * g8:16 * g8 + 16], 0.0), writes=[b_CZ])

        for ct in range(4):
            n_acc = 0
            for g8 in range(8):
                for r in range(2):
                    g, col = set_group(ct, g8, r)
                    drive(X[0], b_X[0], ct, 0, T, 512)
                    drive(Xc[0], b_Xc[0], ct, T, 256, 256)
                    cur, curc = scan2(r)
                    fcol = T - 1 if r == 0 else 0
                    fcc = 255 if r == 0 else 0
                    P.op("act", (lambda cur, fcol, col: lambda e: e.copy(out=Fall[:, col, 0:1], in_=X[cur][:, fcol:fcol + 1]))(cur, fcol, col),
                         reads=b_X[cur], writes=[b_Fall])
                    P.op("act", (lambda curc, fcc, col: lambda e: e.copy(out=Fall[:, col, 1:2], in_=Xc[curc][:, fcc:fcc + 1]))(curc, fcc, col),
                         reads=b_Xc[curc], writes=[b_Fall])
                    out_contrib(X[cur], b_X[cur], n_acc == 0, n_acc == 15)
                    n_acc += 1
                clear_group(g8)
            for c in range(4):
                P.op("dve" if c % 2 else "act",
                     (lambda c, ct: (lambda e: e.tensor_copy(out=yacc[:, ct, c * 512:(c + 1) * 512], in_=pb[c])) if c % 2 else
                      (lambda e: e.copy(out=yacc[:, ct, c * 512:(c + 1) * 512], in_=pb[c])))(c, ct),
                     reads=[b_pb[c]], writes=[b_yacc])
        ccs_i = P.dram("ccs5_i", [128, 64], F32)
        ccs_o = P.dram("ccs5_o", [512, 64], F32)
        b_ci, b_co = P.buf(), P.buf()
        P.dma("sp", ccs_i[:, :], Fall[:, :, 0], reads=[b_Fall], writes=[b_ci], allow_slow_non_contiguous=True)
        P.collective("AllGather", G4, ccs_i.ap().opt(), ccs_o.ap().opt(), reads=[b_ci], writes=[b_co])
        Fg = alloc([128, 4, 64])
        b_Fg = P.buf()
        P.dma("sp", Fg, ccs_o.ap().rearrange("(k p) n -> p k n", p=128), reads=[b_co], writes=[b_Fg])
        Sk = alloc([128, 8])
        b_Sk, b_Vb = P.buf(), P.buf()

        for ct in range(4):
            n_acc = 0
            for g8 in range(8):
                for r in range(2):
                    g, col = set_group(ct, g8, r)
                    order = [0, 1, 2, 3] if r == 0 else [3, 2, 1, 0]
                    P.op("dve", (lambda col, k0: lambda e: e.tensor_copy(out=Sk[:, k0:k0 + 1], in_=Fall[:, col, 1:2]))(col, order[0]),
                         reads=[b_Fall], writes=[b_Sk])
                    for a_i in range(3):
                        kp, kn = order[a_i], order[a_i + 1]
                        P.op("pe", (lambda kp: lambda e: e.matmul(psum_all[:, 4 * 512:4 * 512 + 1], lhsT=AT[:, 11, :], rhs=Sk[:, kp:kp + 1], start=True, stop=True))(kp),
                             reads=[b_AT, b_Sk], writes=[b_pb[4]])
                        P.op("dve", (lambda kp, kn, col: lambda e: e.tensor_tensor(out=Sk[:, kn:kn + 1], in0=psum_all[:, 4 * 512:4 * 512 + 1],
                                                                                   in1=Fg[:, kp, col:col + 1], op=ALU.add))(kp, kn, col),
                             reads=[b_pb[4], b_Fg], writes=[b_Sk])
                    P.op("dve", lambda e: e.tensor_tensor(out=Sk[:, 4:8], in0=Sk[:, 0:4], in1=onehot, op=ALU.mult), reads=[b_Sk, b_C], writes=[b_Sk])
                    P.op("dve", lambda e: e.reduce_sum(out=Sk[:, 4:5], in_=Sk[:, 4:8], axis=AX.X), reads=[b_Sk], writes=[b_Sk])
                    def vcol(a, b, r=r):
                        return (Vb[:, a:b] if r == 0 else Vb[:, T - b:T - a])
                    P.op("pe", lambda e: e.matmul(psum_all[:, 4 * 512:4 * 512 + 1], lhsT=AT[:, 0, :], rhs=Sk[:, 4:5], start=True, stop=True),
                         reads=[b_AT, b_Sk], writes=[b_pb[4]])
                    P.op("dve", (lambda dst: lambda e: e.tensor_copy(out=dst, in_=psum_all[:, 4 * 512:4 * 512 + 1]))(vcol(0, 1)),
                         reads=[b_pb[4]], writes=[b_Vb])
                    for k in range(11):
                        sh = 1 << k
                        for c0 in range(0, sh, 512):
                            c1 = min(sh, c0 + 512)
                            bank = 5 + (k % 2)
                            P.op("pe", (lambda k, c0, c1, bank, src: lambda e: e.matmul(psum_all[:, bank * 512:bank * 512 + c1 - c0], lhsT=AT[:, k, :],
                                                                                        rhs=src, start=True, stop=True))(k, c0, c1, bank, vcol(c0, c1)),
                                 reads=[b_AT, b_Vb], writes=[b_pb[bank]])
                            P.op("act", (lambda c0, c1, bank, dst: lambda e: e.copy(out=dst,
                                                                                    in_=psum_all[:, bank * 512:bank * 512 + c1 - c0]))(c0, c1, bank, vcol(sh + c0, sh + c1)),
                                 reads=[b_pb[bank]], writes=[b_Vb])
                    out_contrib(Vb, [b_Vb], n_acc == 0, n_acc == 15)
                    n_acc += 1
                clear_group(g8)
            for c in range(4):
                P.op("dve", (lambda c, ct: lambda e: e.tensor_tensor(out=yacc[:, ct, c * 512:(c + 1) * 512], in0=pb[c], in1=yacc[:, ct, c * 512:(c + 1) * 512],
                                                                     op=ALU.add))(c, ct), reads=[b_pb[c]], writes=[b_yacc])
        P.barrier()
        o[0] = o_reuse
        wg = alloc([128, 4, 512])
        bgc = alloc([128, 4])
        b_wg = P.buf()
        P.dma("sp", wg, wglu_d.ap().rearrange("(k p) n -> p k n", p=128), writes=[b_wg])
        P.dma("sp", bgc, bgluT_d[:, :], writes=[b_wg])
        tmp = [X[0], X[1]]
        for ct in range(4):
            ya = yacc[:, ct, :]
            P.op("dve", (lambda ct, ya: lambda e: e.scalar_tensor_tensor(out=ya, in0=sT[:, ct, 0:T], scalar=dcol[:, ct:ct + 1], in1=ya,
                                                                         op0=ALU.mult, op1=ALU.add))(ct, ya), reads=[S.b_sT, b_C, b_yacc], writes=[b_yacc])
            P.op("dve", (lambda ya: lambda e: e.tensor_tensor(out=tmp[0], in0=ya, in1=ya, op=ALU.mult))(ya), reads=[b_yacc], writes=b_X[0])
            P.op("dve", lambda e: e.tensor_scalar(out=tmp[0], in0=tmp[0], scalar1=0.044715 * 0.7978845608, scalar2=0.7978845608, op0=ALU.mult, op1=ALU.add),
                 reads=b_X[0], writes=b_X[0])
            P.op("dve", (lambda ya: lambda e: e.tensor_tensor(out=tmp[0], in0=tmp[0], in1=ya, op=ALU.mult))(ya), reads=[b_yacc] + b_X[0], writes=b_X[0])
            P.op("act", lambda e: e.activation(out=tmp[0], in_=tmp[0], func=AF.Tanh), reads=b_X[0], writes=b_X[0])
            P.op("dve", lambda e: e.tensor_scalar(out=tmp[0], in0=tmp[0], scalar1=0.5, scalar2=0.5, op0=ALU.mult, op1=ALU.add), reads=b_X[0], writes=b_X[0])
            P.op("dve", (lambda ct, ya: lambda e: e.tensor_tensor(out=ya, in0=tmp[0], in1=ya, op=ALU.mult))(ct, ya), reads=b_X[0] + [b_yacc], writes=[b_yacc])
        ysT = alloc([128, 4, 512], BF16)
        b_ysT = P.buf()
        yts = alloc([128, 512], BF16)
        b_yts = P.buf()
        for c in range(4):
            for co in range(4):
                for k in range(4):
                    P.op("pe", (lambda c, co, k: lambda e: e.matmul(pb[co], lhsT=wg[:, k, co * 128:(co + 1) * 128], rhs=yacc[:, k, c * 512:(c + 1) * 512],
                                                                    start=(k == 0), stop=(k == 3)))(c, co, k), reads=[b_wg, b_yacc], writes=[b_pb[co]])
                P.op("act", (lambda co: lambda e: e.activation(out=tmp[1][:, co * 512:(co + 1) * 512], in_=pb[co], func=AF.Sigmoid,
                                                               bias=bgc[:, co:co + 1], scale=1.0))(co), reads=[b_pb[co], b_wg], writes=b_X[1])
                P.op("dve", (lambda c, co: lambda e: e.tensor_tensor(out=ysT[:, co, :], in0=tmp[1][:, co * 512:(co + 1) * 512],
                                                                     in1=yacc[:, co, c * 512:(c + 1) * 512], op=ALU.mult))(c, co),
                     reads=b_X[1] + [b_yacc], writes=[b_ysT])
            for tt in range(4):
                t = c * 4 + tt
                for co in range(4):
                    P.op("pe", (lambda co, tt: lambda e: e.matmul(psum_all[:, 4 * 512 + co * 128:4 * 512 + (co + 1) * 128], lhsT=ysT[:, co, tt * 128:(tt + 1) * 128],
                                                                  rhs=ident_b[:, :], start=True, stop=True, skip_group_check=True))(co, tt),
                         reads=[b_ysT, b_ident], writes=[b_pb[4]])
                P.op("act", lambda e: e.copy(out=yts, in_=pb[4]), reads=[b_pb[4]], writes=[b_yts])
                P.dma("sp", ymix[t * 128:(t + 1) * 128, 0:512], yts, reads=[b_yts], writes=[S.b_ymix[t]])
    s5_branch()
    out_proj_phase(1, False)
    ffn_phase(1, True)
    return P.finish()


def _prep_inputs(inp):
    f32 = np.float32
    cos, sin = _rope_tables()
    ccs, m1, w3re, w3im, cosc, sinc = _fourier_consts()
    vecs = np.stack([np.stack([inp["b_out"][l], inp["ln_mix_g"][l], inp["ln_mix_b"][l],
                               inp["b_ffn2"][l], inp["ln_ffn_g"][l], inp["ln_ffn_b"][l]], 0) for l in range(2)], 0)
    b1T = np.ascontiguousarray(inp["b_ffn1"].reshape(2, 32, 128).transpose(0, 2, 1))
    lamv = np.stack([inp["lam_q1"][0], inp["lam_k1"][0], inp["lam_q2"][0], inp["lam_k2"][0]], 0)
    common = {
        "vecs": np.ascontiguousarray(vecs.astype(f32)), "b1T": b1T.astype(f32),
        "win0": np.ascontiguousarray(inp["w_in_ab"][0]), "wfT": np.ascontiguousarray(inp["w_in_ab"][0][:, :256].T),
        "ccs": ccs, "wout": inp["w_out"], "w1": inp["w_ffn1"], "w2": inp["w_ffn2"],
        "ident": np.eye(128, dtype=f32), "lamv": lamv.astype(f32), "subg": inp["subln_g"].astype(f32),
        "wcd": np.ascontiguousarray(inp["w_in_cd"][0]),
        "wspT": np.ascontiguousarray(inp["w_sp"][0].transpose(0, 2, 1)),
        "bspT": np.ascontiguousarray(inp["b_sp"][0].T),
        "s5lam": np.ascontiguousarray(np.tile(np.stack([inp["s5_lam_re"][0].reshape(2, 4, 8, 64).transpose(3, 2, 0, 1).reshape(64, 64), inp["s5_lam_im"][0].reshape(2, 4, 8, 64).transpose(3, 2, 0, 1).reshape(64, 64)], 1), (2, 1, 1)).astype(f32)),
        "s5dt": np.ascontiguousarray(np.broadcast_to(inp["s5_log_dt"][0].reshape(2, 4, 8).transpose(2, 0, 1).reshape(1, 64), (128, 64)).astype(f32)),
        "s5bT": np.ascontiguousarray(np.stack([inp["s5_b_re"][0], inp["s5_b_im"][0]], 0).reshape(2, 2, 4, 8, 64, 16).transpose(3, 5, 0, 1, 2, 4).reshape(128, 2, 2, 4, 64).astype(f32)),
        "s5cT": np.ascontiguousarray(np.concatenate([inp["s5_c_re"][0].transpose(3, 0, 1, 2), inp["s5_c_im"][0].transpose(3, 0, 1, 2)], 0).astype(f32)),
        "s5d": np.ascontiguousarray(inp["s5_d"][0].reshape(4, 128).T.astype(f32)),
        "wglu": np.ascontiguousarray(inp["w_glu"][0]), "bgluT": np.ascontiguousarray(inp["b_glu"][0].reshape(4, 128).T.astype(f32)),
        "m1c": _bf(m1), "dftc": _bf(np.concatenate([cosc, sinc], 1)),
    }
    maps = []
    for r in range(NCORE):
        b, j = r // 4, r % 4
        m = dict(common)
        m["xin"] = np.ascontiguousarray(np.concatenate([inp["x"][b, TPC * j:TPC * (j + 1)], inp["ctx"][b]], 0))
        c_all = np.stack([inp["c"][b], inp["c_ctx"]], 0).astype(f32)
        m["cT"] = np.ascontiguousarray(c_all.reshape(2, 8, 128).transpose(2, 1, 0))
        m["wmod"] = np.ascontiguousarray(inp["w_mod"][:, :, 1536 * j:1536 * (j + 1)])
        m["bmod"] = np.ascontiguousarray(inp["b_mod"][:, 1536 * j:1536 * (j + 1)])
        m["rope"] = np.ascontiguousarray(np.concatenate([cos[b * 0 + TPC * j:TPC * (j + 1)], sin[TPC * j:TPC * (j + 1)]], 1))
        cst = np.zeros((128, 140), f32)
        for k_ in range(64):
            cst[k_, k_ + 64] = 1.0
            cst[k_ + 64, k_] = -1.0
        for p_ in range(128):
            cst[p_, 128 + p_ // 16] = 1.0
        cst[:, 136 + j] = 1.0
        m["s5cst"] = cst
        m["w3c"] = _bf(np.concatenate([w3re[:, 32 * j:32 * (j + 1)], w3im[:, 32 * j:32 * (j + 1)]], 1))
        maps.append(m)
    return maps


def kernel(**inputs):
    inp = {k: np.asarray(v) for k, v in inputs.items()}
    maps = _prep_inputs(inp)
    nc = build()
    res = run_bass_kernel_spmd(nc, maps, core_ids=list(range(NCORE)))
    out = np.zeros((2, SEQ, D), np.float32)
    for r in range(NCORE):
        b, j = r // 4, r % 4
        out[b, TPC * j:TPC * (j + 1)] = res.results[r]["yout"]
    return out
```

```python
from contextlib import ExitStack
import math
import numpy as np
import ml_dtypes
import concourse.bass as bass
import concourse.mybir as mybir
from concourse.bass_utils import run_bass_kernel_spmd

F32 = mybir.dt.float32
BF16 = mybir.dt.bfloat16
AF = mybir.ActivationFunctionType
ALU = mybir.AluOpType
AX = mybir.AxisListType
ENGS = ("pe", "act", "dve", "pool", "sp")
NDS = 48

D = 1024
SEQ = 8192
NCORE = 8
TPC = 2048
NT = 16
NCT = 2
NTT = 18
ROWS = NTT * 128
ALPHA = 4 ** 0.25
LN_EPS = 1e-5
DIFF_SCALE = 0.125


class Buf:
    __slots__ = ("name", "lw", "rd", "excl")

    def __init__(self, name):
        self.name = name
        self.lw = None
        self.rd = []
        self.excl = False


class Prog:
    def __init__(self):
        self.nc = bass.Bass("TRN2", target_bir_lowering=False)
        nc = self.nc
        self.es = ExitStack()
        self.q = {e: [] for e in ENGS}
        self.cnt = {e: 0 for e in ENGS}
        self.known = {e: {} for e in ENGS}
        self.esem = {e: self.es.enter_context(nc.semaphore("s_" + e)) for e in ENGS}
        self.dsem = [self.es.enter_context(nc.semaphore("d%d" % i)) for i in range(NDS)]
        self.dcnt = [0] * NDS
        self.dnext = 0
        self.nbuf = 0
        self.ncc = 0
        self.ccsems = []

    def sb(self, name, shape, dt):
        return self.es.enter_context(self.nc.sbuf_tensor(name, list(shape), dt))

    def ps(self, name, shape, dt):
        return self.es.enter_context(self.nc.psum_tensor(name, list(shape), dt))

    def dram(self, name, shape, dt, kind=None):
        if kind is None:
            return self.nc.dram_tensor(name, list(shape), dt)
        return self.nc.dram_tensor(name, list(shape), dt, kind=kind)

    def buf(self, name=None):
        self.nbuf += 1
        return Buf(name or ("b%d" % self.nbuf))

    def bufs(self, n, name="b"):
        return [self.buf("%s%d" % (name, i)) for i in range(n)]

    def _deps(self, e, reads, writes):
        deps = []
        for b in reads:
            if b.lw is not None:
                deps.append(b.lw)
        for b in writes:
            if b.lw is not None:
                deps.append(b.lw)
            deps.extend(b.rd)
        kn = self.known[e]
        best = {}
        for (sem, val, eng) in deps:
            if eng == e and e == "pe":
                continue
            if kn.get(sem, 0) >= val:
                continue
            if best.get(sem, (None, 0))[1] < val:
                best[sem] = (sem, val)
        waits = []
        for sem, (s, val) in best.items():
            kn[sem] = val
            waits.append((s, val))
        return waits

    def _commit(self, tok, reads, writes):
        for b in writes:
            b.lw = tok
            b.rd = []
        for b in reads:
            if b not in writes:
                b.rd.append(tok)

    def op(self, e, fn, reads=(), writes=()):
        reads = list(reads)
        writes = list(writes)
        if e != "pe":
            writes = writes + [b for b in reads if b.excl and b not in writes]
        waits = self._deps(e, reads, writes)
        self.cnt[e] += 1
        tok = (self.esem[e], self.cnt[e], e)
        self.q[e].append((waits, fn, (self.esem[e], 1)))
        self._commit(tok, reads, writes)
        return tok

    def dma(self, e, out, in_, reads=(), writes=(), **kw):
        reads = list(reads)
        writes = list(writes)
        waits = self._deps(e, reads, writes)
        i = self.dnext
        self.dnext = (self.dnext + 1) % NDS
        sem = self.dsem[i]
        if self.dcnt[i] > 0 and self.known[e].get(sem, 0) < self.dcnt[i]:
            waits.append((sem, self.dcnt[i]))
            self.known[e][sem] = self.dcnt[i]
        self.dcnt[i] += 16
        tok = (sem, self.dcnt[i], "dma")
        self.q[e].append((waits, (lambda eng: eng.dma_start(out=out, in_=in_, **kw)), (sem, 16)))
        self._commit(tok, reads, writes)
        return tok

    def collective(self, kind, groups, in_ap, out_ap, reads=(), writes=()):
        e = "pool"
        reads = list(reads)
        writes = list(writes)
        waits = self._deps(e, reads, writes)
        sem = self.es.enter_context(self.nc.semaphore("cc%d" % self.ncc))
        self.ncc += 1
        self.ccsems.append(sem)
        tok = (sem, 1, "cc")

        def fn(eng):
            return eng.collective_compute(kind, ALU.bypass, replica_groups=groups,
                                          ins=[in_ap], outs=[out_ap])
        self.q[e].append((waits, fn, (sem, None)))
        self._commit(tok, reads, writes)
        return tok

    def barrier(self):
        fin = []
        for i in range(NDS):
            if self.dcnt[i] > 0:
                fin.append((self.dsem[i], self.dcnt[i]))
        for s in self.ccsems:
            fin.append((s, 1))
        for e in ENGS:
            if self.cnt[e] > 0:
                fin.append((self.esem[e], self.cnt[e]))
        for e in ENGS:
            w = []
            for (s, v) in fin:
                if self.known[e].get(s, 0) < v and not (s is self.esem[e]):
                    w.append((s, v))
                    self.known[e][s] = v
            if w:
                self.q[e].append((w, None, None))

    def finish(self):
        self.barrier()
        nc = self.nc

        def mk(e):
            def body(eng):
                for waits, fn, inc in self.q[e]:
                    for (sem, val) in waits:
                        eng.wait_ge(sem, val)
                    if fn is None:
                        continue
                    ins = fn(eng)
                    if inc is not None:
                        if inc[1] is None:
                            ins.then_inc(inc[0])
                        else:
                            ins.then_inc(inc[0], inc[1])
            return body
        with nc.Block() as block:
            block.tensor(mk("pe"))
            block.scalar(mk("act"))
            block.vector(mk("dve"))
            block.gpsimd(mk("pool"))
            block.sync(mk("sp"))
        self.es.close()
        return nc


def _rope_tables():
    pos = np.arange(SEQ)
    row = (pos // 64).astype(np.float32)
    col = (pos % 64).astype(np.float32)
    inv = np.power(np.float32(10000.0), -np.arange(16, dtype=np.float32) / np.float32(16)).astype(np.float32)
    ang = np.stack([row[:, None] * inv, col[:, None] * inv], axis=1).astype(np.float32)
    return np.cos(ang).astype(np.float32).reshape(SEQ, 32), np.sin(ang).astype(np.float32).reshape(SEQ, 32)


def _fourier_consts():
    c = np.arange(64)
    ang = 2 * np.pi * np.outer(c, c) / 64.0
    cc = np.zeros((256, 512), np.float64)
    for g in range(4):
        cc[g * 64:(g + 1) * 64, g * 64:(g + 1) * 64] = np.cos(ang)
        cc[g * 64:(g + 1) * 64, 256 + g * 64:256 + (g + 1) * 64] = np.sin(ang)
    l1 = np.arange(64)
    m1 = np.zeros((128, 128, 128), np.float64)
    for l2 in range(128):
        ph = 2 * np.pi * (l2 * l1[:, None] / 8192.0 + np.outer(l1, l1) / 64.0)
        mr, mi = np.cos(ph), np.sin(ph)
        mfull = np.block([[mr, -mi], [mi, mr]])
        perm = np.array([pt * 64 + (16 * rk + 4 * c_ + q_) for pt in range(2) for c_ in range(4) for rk in range(4) for q_ in range(4)])
        m1[l2] = mfull.T[perm, :]
    l2 = np.arange(128)
    ph3 = 2 * np.pi * np.outer(l2, l2) / 128.0
    sc = 1.0 / math.sqrt(8192.0 * 64.0)
    w3re = (np.cos(ph3) * sc).T
    w3im = (-np.sin(ph3) * sc).T
    lc = np.arange(256)
    phc = 2 * np.pi * np.outer(lc, lc) / 256.0
    scc = 1.0 / math.sqrt(256.0 * 64.0)
    cosc = (np.cos(phc) * scc).T
    sinc = (-np.sin(phc) * scc).T
    return cc.astype(np.float32), m1, w3re, w3im, cosc, sinc


def _bf(a):
    return np.asarray(a, dtype=np.float32).astype(ml_dtypes.bfloat16)


class K:
    pass


def _pieces(start, n):
    out = []
    f = start
    while f < start + n:
        r = f // 768
        o = f % 768
        ln = min(768 - o, start + n - f)
        out.append((r, o, f - start, ln))
        f += ln
    return out


def build(stage=99, dbg=False):
    P = Prog()
    nc = P.nc
    S = K()
    S.P = P
    ein = lambda name, shape, dt=F32: P.dram(name, shape, dt, kind="ExternalInput")

    xin = ein("xin", [ROWS, D])
    cT_d = ein("cT", [128, 8, 2])
    wmod_d = ein("wmod", [2, D, 1536])
    bmod_d = ein("bmod", [2, 1536])
    vec_d = ein("vecs", [2, 6, D])
    b1T_d = ein("b1T", [2, 128, 32])
    win0_d = ein("win0", [D, 2560])
    wfT_d = ein("wfT", [256, D])
    cc_d = ein("ccs", [256, 512])
    wout_d = ein("wout", [2, D, D])
    import os
    KSMALL = 'K_SMALL' in os.environ
    w1_d = ein("w1", [2, D, 4096]) if not KSMALL else None
    w2_d = ein("w2", [2, 4096, D]) if not KSMALL else None
    wcd_d = ein("wcd", [D, 1536])
    s5lam_d = ein("s5lam", [128, 2, 64])
    s5dt_d = ein("s5dt", [128, 64])
    s5bT_d = ein("s5bT", [128, 2, 2, 4, 64])
    s5cT_d = ein("s5cT", [128, 2, 32, 16])
    s5cst_d = ein("s5cst", [128, 140])
    s5d_d = ein("s5d", [128, 4])
    wglu_d = ein("wglu", [512, 512])
    bgluT_d = ein("bgluT", [128, 4])
    wspT_d = ein("wspT", [4, 128, 128])
    bspT_d = ein("bspT", [128, 4])
    rope_d = ein("rope", [TPC, 64])
    ident_d = ein("ident", [128, 128])
    lamv_d = ein("lamv", [4, 64])
    subg_d = ein("subg", [1, 128])
    m1_d = ein("m1c", [128, 128, 128], BF16)
    w3_d = ein("w3c", [128, 64], BF16)
    dftc_d = ein("dftc", [256, 512], BF16)
    yout = P.dram("yout", [TPC, D], F32, kind="ExternalOutput")
    if dbg:
        S.dbg = {}

    xres = P.dram("xres", [ROWS, D], F32)
    ymix = P.dram("ymix", [ROWS, D], BF16)
    S.b_xres = P.bufs(NTT, "xres")
    S.b_ymix = P.bufs(NTT, "ymix")
    cc_mod_in = P.dram("cc_mod_in", [2, 3072], F32)
    cc_mod_out = P.dram("cc_mod_out", [8, 3072], F32)
    b_ccmi, b_ccmo = P.buf(), P.buf()
    wout_b = P.dram("wout_b", [2, D, D], BF16)
    w1_b = P.dram("w1_b", [2, D, 4096], BF16)
    w2_b = P.dram("w2_b", [2, 4096, D], BF16)
    b_woutb, b_w1b, b_w2b = P.bufs(2, "woutb"), P.bufs(2, "w1b"), P.bufs(2, "w2b")

    ident_f = P.sb("ident_f", [128, 128], F32)
    ident_b = P.sb("ident_b", [128, 128], BF16)
    b_ident = P.buf()
    epsc = P.sb("epsc", [128, 1], F32)
    b_eps = P.buf()
    P.op("pool", lambda e: e.memset(epsc[:, :], LN_EPS), writes=[b_eps])
    P.dma("sp", ident_f[:, :], ident_d[:, :], writes=[b_ident])
    P.op("dve", lambda e: e.tensor_copy(out=ident_b[:, :], in_=ident_f[:, :]), reads=[b_ident], writes=[b_ident])

    psum_all = P.ps("psum_all", [128, 4096], F32)
    pb = [psum_all[:, 512 * i:512 * (i + 1)] for i in range(8)]
    S.psum_all = psum_all
    b_pb = P.bufs(8, "pb")
    for b_ in b_pb:
        b_.excl = True
    S.pb, S.b_pb = pb, b_pb

    ARENA_EL = 49152
    arena = P.sb("arena", [128, ARENA_EL], BF16)
    S.arena = arena

    def av(off, shape, dt=BF16):
        n = 1
        for s_ in shape[1:]:
            n *= s_
        el = n * (2 if dt == F32 else 1)
        assert off + el <= ARENA_EL, (off, el)
        ap = arena[0:shape[0], off:off + el]
        if dt == F32:
            ap = ap.bitcast(F32)
        if len(shape) == 3:
            ap = ap.rearrange("p (a b) -> p a b", a=shape[1])
        elif len(shape) == 4:
            ap = ap.rearrange("p (a b c) -> p a b c", a=shape[1], b=shape[2])
        return ap
    S.av = av

    for l in range(2):
        P.dma("pool", wout_b[l, :, :], wout_d[l, :, :], writes=[b_woutb[l]])
    for l in range(0 if KSMALL else 2):
        for hh in range(4):
            P.dma("pool", w1_b[l, :, hh * 1024:(hh + 1) * 1024], w1_d[l, :, hh * 1024:(hh + 1) * 1024], writes=[b_w1b[l]])
            P.dma("pool", w2_b[l, hh * 1024:(hh + 1) * 1024, :], w2_d[l, hh * 1024:(hh + 1) * 1024, :], writes=[b_w2b[l]])

    cT = P.sb("cT_s", [128, 8, 2], F32)
    scT = P.sb("scT", [128, 8, 2], F32)
    b_cT = P.buf()
    P.dma("sp", cT[:, :, :], cT_d[:, :, :], writes=[b_cT])
    P.op("act", lambda e: e.activation(out=scT[:, :, :], in_=cT[:, :, :], func=AF.Silu), reads=[b_cT], writes=[b_cT])
    wst = [av(s_ * 3072, [128, 1536], F32) for s_ in range(2)]
    b_wst = P.bufs(2, "wst")
    bm3 = av(6144, [2, 3072], F32)
    mod3 = av(6144 + 6144, [2, 3072], F32)
    b_bm3, b_mod3 = P.buf(), P.buf()
    P.dma("sp", bm3, bmod_d.ap().rearrange("l n -> (l n)").partition_broadcast(2), writes=[b_bm3])
    ci = 0
    for l in range(2):
        for kc in range(8):
            s_ = ci % 2
            ci += 1
            P.dma("sp", wst[s_], wmod_d[l, kc * 128:(kc + 1) * 128, :], writes=[b_wst[s_]])
            for c3 in range(3):
                bank = l * 3 + c3
                P.op("pe", (lambda l, kc, c3, bank, s_: lambda e: e.matmul(
                    psum_all[0:2, 512 * bank:512 * bank + 512], lhsT=scT[:, kc, :], rhs=wst[s_][:, c3 * 512:(c3 + 1) * 512],
                    start=(kc == 0), stop=(kc == 7)))(l, kc, c3, bank, s_),
                    reads=[b_cT, b_wst[s_]], writes=[b_pb[bank]])
    for l in range(2):
        for c3 in range(3):
            bank = l * 3 + c3
            c0 = l * 1536 + c3 * 512
            P.op("dve", (lambda bank, c0: lambda e: e.tensor_tensor(
                out=mod3[:, c0:c0 + 512], in0=psum_all[0:2, 512 * bank:512 * bank + 512], in1=bm3[:, c0:c0 + 512], op=ALU.add))(bank, c0),
                reads=[b_pb[bank], b_bm3], writes=[b_mod3])
    P.dma("sp", cc_mod_in[:, :], mod3, reads=[b_mod3], writes=[b_ccmi])
    P.collective("AllGather", [[0, 1, 2, 3], [4, 5, 6, 7]], cc_mod_in.ap().opt(), cc_mod_out.ap().opt(), reads=[b_ccmi], writes=[b_ccmo])

    modv = P.dram("modv", [2, 2, 6144], F32)
    b_modv = P.buf()
    ccmo = cc_mod_out.ap().rearrange("(r w) n -> r w n", w=2)
    for l in range(2):
        for who in range(2):
            P.dma("sp", modv[who, l, :].rearrange("(r o) -> r o", r=4), ccmo[:, who, l * 1536:(l + 1) * 1536],
                  reads=[b_ccmo], writes=[b_modv])
    modT = P.sb("modT", [128, 2, 2, 48], F32)
    b_modT = P.buf()
    for l in range(2):
        for who in range(2):
            P.dma("sp", modT[:, l, who, :], modv[who, l, :].rearrange("(q p) -> p q", p=128), reads=[b_modv], writes=[b_modT],
                  allow_slow_non_contiguous=True)
    for kind in (1, 4):
        P.op("dve", (lambda kind: lambda e: e.tensor_scalar(
            out=modT[:, :, :, kind * 8:(kind + 1) * 8], in0=modT[:, :, :, kind * 8:(kind + 1) * 8],
            scalar1=1.0, scalar2=None, op0=ALU.add))(kind), reads=[b_modT], writes=[b_modT])
    S.modT, S.b_modT, S.modv, S.b_modv = modT, b_modT, modv, b_modv
    if dbg:
        d = P.dram("dbg_modT", [128, 192], F32, kind="ExternalOutput")
        P.dma("sp", d[:, :], modT[:, :, :, :].rearrange("p a b c -> p (a b c)"), reads=[b_modT])
    if stage == 0:
        return P.finish()

    xt = [P.sb("xt%d" % i, [128, D], F32) for i in range(2)]
    b_xt = P.bufs(2, "xt")
    xn = [P.sb("xn%d" % i, [128, D], BF16) for i in range(2)]
    b_xn = P.bufs(2, "xn")
    stt = [P.sb("stt%d" % i, [128, 16], F32) for i in range(2)]
    b_stt = P.bufs(2, "stt")

    def ln_stats(src, b_src, slot, eng_rs="act"):
        st = stt[slot]
        P.op("dve", lambda e: e.bn_stats(out=st[:, 0:6], in_=src[:, 0:512]), reads=[b_src], writes=[b_stt[slot]])
        P.op("dve", lambda e: e.bn_stats(out=st[:, 6:12], in_=src[:, 512:1024]), reads=[b_src], writes=[b_stt[slot]])
        P.op("dve", lambda e: e.bn_aggr(out=st[:, 12:14], in_=st[:, 0:12]), reads=[b_stt[slot]], writes=[b_stt[slot]])
        P.op("act", lambda e: e.activation(out=st[:, 14:15], in_=st[:, 13:14], func=AF.Sqrt, bias=epsc[:, 0:1], scale=1.0),
             reads=[b_stt[slot], b_eps], writes=[b_stt[slot]])
        P.op("dve", lambda e: e.reciprocal(out=st[:, 14:15], in_=st[:, 14:15]), reads=[b_stt[slot]], writes=[b_stt[slot]])
        return st[:, 12:13], st[:, 14:15]

    def ln_to_hT(t, x_src_ap, b_xsrc, slot, l, kinds, hT_ap_fn, b_hT, tbank):
        who = 0 if t < NT else 1
        P.dma("sp", xt[slot][:, :], x_src_ap, reads=[b_xsrc], writes=[b_xt[slot]])
        mean, rstd = ln_stats(xt[slot], b_xt[slot], slot)
        P.op("dve", lambda e: e.tensor_scalar(out=xn[slot][:, :], in0=xt[slot][:, :], scalar1=mean, scalar2=rstd,
                                              op0=ALU.subtract, op1=ALU.mult),
             reads=[b_xt[slot], b_stt[slot]], writes=[b_xn[slot]])
        ptb = psum_all[:, tbank * 512:tbank * 512 + 1024]
        for kc in range(8):
            P.op("pe", (lambda kc: lambda e: e.matmul(ptb[:, kc * 128:(kc + 1) * 128], lhsT=xn[slot][:, kc * 128:(kc + 1) * 128],
                                                      rhs=ident_b[:, :], start=True, stop=True))(kc),
                 reads=[b_xn[slot], b_ident], writes=[b_pb[tbank + kc // 4]])
        ksh, ksc = kinds
        for kc in range(8):
            sc_ap = modT[:, l, who, ksc * 8 + kc:ksc * 8 + kc + 1]
            sh_ap = modT[:, l, who, ksh * 8 + kc:ksh * 8 + kc + 1]
            if kc % 2 == 0:
                P.op("act", (lambda kc, sc_ap, sh_ap: lambda e: e.activation(
                    out=hT_ap_fn(kc), in_=ptb[:, kc * 128:(kc + 1) * 128], func=AF.Identity, bias=sh_ap, scale=sc_ap))(kc, sc_ap, sh_ap),
                    reads=[b_pb[tbank + kc // 4], b_modT], writes=[b_hT[0]])
            else:
                P.op("dve", (lambda kc, sc_ap, sh_ap: lambda e: e.tensor_scalar(
                    out=hT_ap_fn(kc), in0=ptb[:, kc * 128:(kc + 1) * 128], scalar1=sc_ap, scalar2=sh_ap,
                    op0=ALU.mult, op1=ALU.add))(kc, sc_ap, sh_ap),
                    reads=[b_pb[tbank + kc // 4], b_modT], writes=[b_hT[1]])

    S.ln_stats, S.ln_to_hT = ln_stats, ln_to_hT

    def x_src(t, first_layer):
        return (xin[t * 128:(t + 1) * 128, :] if first_layer else xres[t * 128:(t + 1) * 128, :])

    WIN_N = 2816
    win = arena[:, 0:8 * WIN_N].rearrange("p (k n) -> p k n", k=8)
    b_win = P.buf()
    P.barrier()
    P.dma("pool", win[:, :, 512:WIN_N], win0_d.ap()[:, 256:2560].rearrange("(k p) n -> p k n", p=128), writes=[b_win])
    wfT = av(8 * WIN_N, [128, 2, D], F32)
    ccs = av(8 * WIN_N + 4096, [128, 2, 512], F32)
    b_wfT = P.buf()
    P.dma("sp", wfT, wfT_d.ap().rearrange("(c p) k -> p c k", p=128), writes=[b_wfT])
    P.dma("sp", ccs, cc_d.ap().rearrange("(c p) n -> p c n", p=128), writes=[b_wfT])
    for kc in range(8):
        bank = kc % 4
        for c in range(2):
            P.op("pe", (lambda kc, c, bank: lambda e: e.matmul(pb[bank], lhsT=wfT[:, c, kc * 128:(kc + 1) * 128], rhs=ccs[:, c, :],
                                                               start=(c == 0), stop=(c == 1)))(kc, c, bank),
                 reads=[b_wfT], writes=[b_pb[bank]])
        P.op("act" if kc % 2 else "dve",
             (lambda kc, bank: (lambda e: e.copy(out=win[:, kc, 0:512], in_=pb[bank])) if kc % 2 else
              (lambda e: e.tensor_copy(out=win[:, kc, 0:512], in_=pb[bank])))(kc, bank),
             reads=[b_pb[bank]], writes=[b_win])

    import os
    KCUT = int(os.environ.get('K_CUT', '99'))
    if KCUT == 1:
        return P.finish()
    cc_f_in = [P.dram("cc_f_in%d" % c, [512, 512], BF16) for c in range(4)]
    cc_f_out = [P.dram("cc_f_out%d" % c, [2048, 512], BF16) for c in range(4)]
    cc_k_in = [P.dram("cc_k_in%d" % h, [128, TPC], BF16) for h in range(6)]
    cc_k_out = [P.dram("cc_k_out%d" % h, [512, TPC], BF16) for h in range(6)]
    ccval_i = [P.dram("ccval_i%d" % h, [TPC, 128], BF16) for h in range(6)]
    ccval_o = [P.dram("ccval_o%d" % h, [4 * TPC, 128], BF16) for h in range(6)]
    b_ccf_in, b_ccf_out = P.bufs(4, "ccfi"), P.bufs(4, "ccfo")
    b_cck_in, b_cck_out = P.bufs(6, "ccki"), P.bufs(6, "ccko")
    b_ccvali, b_ccvalo = P.bufs(6, "ccvi"), P.bufs(6, "ccvo")
    G4 = [[0, 1, 2, 3], [4, 5, 6, 7]]

    rope_sb = P.sb("rope_sb", [128, NT, 64], F32)
    b_rope = P.buf()
    P.dma("sp", rope_sb[:, :, :], rope_d.ap().rearrange("(t p) c -> p t c", p=128), writes=[b_rope])
    qT_all = P.sb("qT_all", [128, 6, ROWS], BF16)
    b_qT = P.buf()
    kTc = P.sb("kTc", [128, 6, 256], BF16)
    vc = P.sb("vc", [128, 2, 6, 129], BF16)
    abc = P.sb("abc", [128, 2, 512], BF16)
    b_kTc, b_vc, b_abc = P.buf(), P.buf(), P.buf()
    P.op("pool", lambda e: e.memset(vc[:, :, :, 128:129], 1.0), writes=[b_vc])
    hT1 = [P.sb("hT1_%d" % i, [128, 8, 128], BF16) for i in range(2)]
    b_hT1 = [P.bufs(2, "hT1_%d_" % i) for i in range(2)]
    zqk = av(28672, [128, 1536], F32)
    b_zqk = P.buf()
    rtmp = [av(28672 + 3072 + i * 1536, [128, 768], F32) for i in range(4)]
    b_rtmp = P.bufs(4, "rtmp")
    qkb = P.sb("qkb", [128, 1536], BF16)
    b_qkb = P.buf()
    ab_st = [P.sb("ab_st%d" % i, [128, 512], BF16) for i in range(2)]
    v_st = [P.sb("v_st%d" % i, [128, 768], BF16) for i in range(2)]
    kT_st = [P.sb("kT_st%d" % i, [128, 6, 128], BF16) for i in range(2)]
    b_ab, b_v, b_kTst = P.bufs(2, "ab"), P.bufs(2, "vst"), P.bufs(2, "kTst")

    import os
    def l0_tile(t):
        slot = t % 2
        main = t < NT
        ln_to_hT(t, x_src(t, True), Buf("xin"), slot, 0, (0, 1), (lambda kc, slot=slot: hT1[slot][:, kc, :]), b_hT1[slot], 6)
        if KCUT == 2:
            return P.finish()
        for cg in range(6):
            n0 = cg * 512
            n1 = min(WIN_N, n0 + 512)
            for kc in range(8):
                P.op("pe", (lambda cg, kc, n0, n1: lambda e: e.matmul(
                    psum_all[:, n0:n1], lhsT=hT1[slot][:, kc, :], rhs=win[:, kc, n0:n1], start=(kc == 0), stop=(kc == 7)))(cg, kc, n0, n1),
                    reads=b_hT1[slot] + [b_win], writes=[b_pb[cg]])
        if KCUT == 3:
            return P.finish()
        if main:
            P.op("act", lambda e, slot=slot: e.copy(out=ab_st[slot][:, :], in_=pb[0]), reads=[b_pb[0]], writes=[b_ab[slot]])
            if 'a' not in os.environ.get('K_NODMA', ''):
                P.dma("sp", cc_f_in[t // 4][(t % 4) * 128:(t % 4 + 1) * 128, :], ab_st[slot][:, :], reads=[b_ab[slot]], writes=[b_ccf_in[t // 4]])
        else:
            P.op("act", lambda e, t=t: e.copy(out=abc[:, t - NT, :], in_=pb[0]), reads=[b_pb[0]], writes=[b_abc])
        if main:
            for bk in range(3):
                P.op("act", lambda e, bk=bk: e.copy(out=zqk[:, bk * 512:(bk + 1) * 512], in_=pb[1 + bk]),
                     reads=[b_pb[1 + bk]], writes=[b_zqk])
            zv = zqk.rearrange("p (u a h f) -> p u a h f", u=24, a=2, h=2)
            ov = qkb[:, :].rearrange("p (u a h f) -> p u a h f", u=24, a=2, h=2)
            t1, t2 = zv[:, :, :, 0, :], zv[:, :, :, 1, :]
            cs = rope_sb[:, t, 0:32].rearrange("p (a f) -> p a f", a=2).unsqueeze(1).broadcast_to([128, 24, 2, 16])
            sn = rope_sb[:, t, 32:64].rearrange("p (a f) -> p a f", a=2).unsqueeze(1).broadcast_to([128, 24, 2, 16])
            rv = [r.rearrange("p (u a f) -> p u a f", u=24, a=2) for r in rtmp]
            P.op("dve", lambda e: e.tensor_tensor(out=rv[0], in0=t1, in1=cs, op=ALU.mult), reads=[b_zqk, b_rope], writes=[b_rtmp[0]])
            P.op("pool", lambda e: e.tensor_tensor(out=rv[1], in0=t2, in1=sn, op=ALU.mult), reads=[b_zqk, b_rope], writes=[b_rtmp[1]])
            P.op("dve", lambda e: e.tensor_tensor(out=ov[:, :, :, 0, :], in0=rv[0], in1=rv[1], op=ALU.subtract),
                 reads=[b_rtmp[0], b_rtmp[1]], writes=[b_qkb])
            P.op("pool", lambda e: e.tensor_tensor(out=rv[2], in0=t2, in1=cs, op=ALU.mult), reads=[b_zqk, b_rope], writes=[b_rtmp[2]])
            P.op("dve", lambda e: e.tensor_tensor(out=rv[3], in0=t1, in1=sn, op=ALU.mult), reads=[b_zqk, b_rope], writes=[b_rtmp[3]])
            P.op("pool", lambda e: e.tensor_tensor(out=ov[:, :, :, 1, :], in0=rv[2], in1=rv[3], op=ALU.add),
                 reads=[b_rtmp[2], b_rtmp[3]], writes=[b_qkb])
        else:
            for bk in range(3):
                P.op("act" if bk % 2 else "dve",
                     (lambda bk: (lambda e: e.copy(out=qkb[:, bk * 512:(bk + 1) * 512], in_=pb[1 + bk])) if bk % 2 else
                      (lambda e: e.tensor_copy(out=qkb[:, bk * 512:(bk + 1) * 512], in_=pb[1 + bk])))(bk),
                     reads=[b_pb[1 + bk]], writes=[b_qkb])
        if KCUT == 4:
            return P.finish()
        if main:
            P.op("dve", lambda e, slot=slot: e.tensor_copy(out=v_st[slot][:, 0:512], in_=pb[4]), reads=[b_pb[4]], writes=[b_v[slot]])
            P.op("act", lambda e, slot=slot: e.copy(out=v_st[slot][:, 512:768], in_=psum_all[:, 2560:2816]), reads=[b_pb[5]], writes=[b_v[slot]])
            if 'v' not in os.environ.get('K_NODMA', ''):
                for h_ in range(6):
                    P.dma("sp", ccval_i[h_][t * 128:(t + 1) * 128, :], v_st[slot][:, h_ * 128:(h_ + 1) * 128], reads=[b_v[slot]], writes=[b_ccvali[h_]])
        else:
            ci = t - NT
            P.op("dve", lambda e, ci=ci: e.tensor_copy(out=vc[:, ci, 0:4, 0:128], in_=pb[4].rearrange("p (h e) -> p h e", h=4)),
                 reads=[b_pb[4]], writes=[b_vc])
            P.op("act", lambda e, ci=ci: e.copy(out=vc[:, ci, 4:6, 0:128], in_=psum_all[:, 2560:2816].rearrange("p (h e) -> p h e", h=2)),
                 reads=[b_pb[5]], writes=[b_vc])
        if KCUT == 5:
            return P.finish()
        if KCUT == 7:
            return None
        pq = psum_all[:, 3072:4096]
        for u in range(8):
            P.op("pe", (lambda u: lambda e: e.matmul(pq[:, u * 128:(u + 1) * 128], lhsT=qkb[:, u * 128:(u + 1) * 128], rhs=ident_b[:, :],
                                                     start=True, stop=True))(u),
                 reads=[b_qkb, b_ident], writes=[b_pb[6 + u // 4]])
        P.op("act", lambda e: e.copy(out=qT_all[:, :, t * 128:(t + 1) * 128], in_=pq[:, 0:768].rearrange("p (h k) -> p h k", h=6)),
             reads=[b_pb[6], b_pb[7]], writes=[b_qT])
        kdst = kT_st[slot][:, :, :] if main else kTc[:, :, (t - NT) * 128:(t - NT + 1) * 128]
        b_kd = b_kTst[slot] if main else b_kTc
        P.op("dve", lambda e: e.tensor_copy(out=kdst[:, 0:2, :], in_=pq[:, 768:1024].rearrange("p (h k) -> p h k", h=2)),
             reads=[b_pb[7]], writes=[b_kd])
        for u in range(8, 12):
            P.op("pe", (lambda u: lambda e: e.matmul(pq[:, (u - 8) * 128:(u - 7) * 128], lhsT=qkb[:, u * 128:(u + 1) * 128], rhs=ident_b[:, :],
                                                     start=True, stop=True))(u),
                 reads=[b_qkb, b_ident], writes=[b_pb[6]])
        P.op("dve", lambda e: e.tensor_copy(out=kdst[:, 2:6, :], in_=pq[:, 0:512].rearrange("p (h k) -> p h k", h=4)),
             reads=[b_pb[6]], writes=[b_kd])
        if main and 'k' not in os.environ.get('K_NODMA', ''):
            for h_ in range(6):
                P.dma("sp", cc_k_in[h_][:, t * 128:(t + 1) * 128], kT_st[slot][:, h_, :], reads=[b_kTst[slot]], writes=[b_cck_in[h_]])
        return None
    for t_ in ([int(v) for v in os.environ['K_TILES'].split(',')] if 'K_TILES' in os.environ else range(NTT)):
        r_ = l0_tile(t_)
        if r_ is not None:
            return r_
    if 'K_NOCC' in os.environ:
        return P.finish()
    for h_ in range(6):
        P.collective("AllGather", G4, cc_k_in[h_].ap().opt(), cc_k_out[h_].ap().opt(), reads=[b_cck_in[h_]], writes=[b_cck_out[h_]])
        P.collective("AllGather", G4, ccval_i[h_].ap().opt(), ccval_o[h_].ap().opt(), reads=[b_ccvali[h_]], writes=[b_ccvalo[h_]])
    for c_ in range(4):
        P.collective("AllGather", G4, cc_f_in[c_].ap().opt(), cc_f_out[c_].ap().opt(), reads=[b_ccf_in[c_]], writes=[b_ccf_out[c_]])
    if dbg:
        d = P.dram("dbg_qT", [128, 6 * ROWS], BF16, kind="ExternalOutput")
        P.dma("sp", d[:, :], qT_all[:, :, :].rearrange("p h t -> p (h t)"), reads=[b_qT])
        d2 = P.dram("dbg_k", [512, TPC], BF16, kind="ExternalOutput")
        P.dma("sp", d2[:, :], cc_k_out[3][:, :], reads=[b_cck_out[3]])
        d3 = P.dram("dbg_v", [4 * TPC, 128], BF16, kind="ExternalOutput")
        P.dma("sp", d3[:, :], ccval_o[2][:, :], reads=[b_ccvalo[2]])
        d4 = P.dram("dbg_f", [2048, 512], BF16, kind="ExternalOutput")
        P.dma("sp", d4[:, :], cc_f_out[1][:, :], reads=[b_ccf_out[1]])
    if stage == 1:
        return P.finish()
    P.barrier()
    SLOT_EL = 8448 + 66 * 129
    kT_h = [av(s * SLOT_EL, [128, 8448]) for s in range(2)]
    Vp = [av(s * SLOT_EL + 8448, [128, 66, 129]) for s in range(2)]
    PT = [av(2 * SLOT_EL + s * 1024, [128, 1024]) for s in range(2)]
    b_kT, b_Vp, b_PT = P.bufs(2, "kTh"), P.bufs(2, "Vp"), P.bufs(2, "PT")
    for s in range(2):
        P.op("pool", lambda e, s=s: e.memset(Vp[s][:, :, 128:129], 1.0), writes=[b_Vp[s]])
    lamb = P.sb("lamb", [128, 4, 64], F32)
    lsm = P.sb("lsm", [128, 8], F32)
    gsub = P.sb("gsub", [128, 128], F32)
    b_lam = P.buf()
    P.dma("sp", lamb[:, :, :].rearrange("p a b -> p (a b)"), lamv_d.ap().rearrange("a b -> (a b)").partition_broadcast(128), writes=[b_lam])
    P.dma("sp", gsub[:, :], subg_d.ap().rearrange("a b -> (a b)").partition_broadcast(128), writes=[b_lam])
    for i in range(2):
        P.op("dve", lambda e, i=i: e.tensor_tensor(out=lamb[:, 2 * i, :], in0=lamb[:, 2 * i, :], in1=lamb[:, 2 * i + 1, :], op=ALU.mult),
             reads=[b_lam], writes=[b_lam])
        P.op("dve", lambda e, i=i: e.reduce_sum(out=lsm[:, i:i + 1], in_=lamb[:, 2 * i, :], axis=AX.X), reads=[b_lam], writes=[b_lam])
    P.op("act", lambda e: e.activation(out=lsm[:, 2:4], in_=lsm[:, 0:2], func=AF.Exp), reads=[b_lam], writes=[b_lam])
    LAM_INIT = 0.8 - 0.6 * math.exp(-0.3 * 0)
    P.op("dve", lambda e: e.tensor_tensor(out=lsm[:, 4:5], in0=lsm[:, 3:4], in1=lsm[:, 2:3], op=ALU.subtract), reads=[b_lam], writes=[b_lam])
    P.op("dve", lambda e: e.tensor_scalar(out=lsm[:, 4:5], in0=lsm[:, 4:5], scalar1=-LAM_INIT, scalar2=None, op0=ALU.add), reads=[b_lam], writes=[b_lam])
    P.op("dve", lambda e: e.tensor_scalar(out=gsub[:, :], in0=gsub[:, :], scalar1=1.0 - LAM_INIT, scalar2=None, op0=ALU.mult), reads=[b_lam], writes=[b_lam])
    neglam = lsm[:, 4:5]
    ep_r = [P.sb("ep_r%d" % i, [128, 8], F32) for i in range(2)]
    ep_o = [P.sb("ep_o%d" % i, [128, 128], F32) for i in range(2)]
    ep_j = [P.sb("ep_j%d" % i, [128, 128], F32) for i in range(2)]
    b_ep = P.bufs(2, "ep")
    yst = [P.sb("yst%d" % i, [128, 4, 128], BF16) for i in range(2)]
    b_yst = P.bufs(2, "yst")
    epi = 0
    ATT_H = int(os.environ.get('K_HEADS', '6'))
    def load_head(h, s):
        for j in range(4):
            P.dma("sp", kT_h[s][:, 256 + 2048 * j:256 + 2048 * (j + 1)], cc_k_out[h][j * 128:(j + 1) * 128, :],
                  reads=[b_cck_out[h]], writes=[b_kT[s]])
            P.dma("sp", Vp[s][:, 2 + 16 * j:2 + 16 * (j + 1), 0:128],
                  ccval_o[h][j * 2048:(j + 1) * 2048, :].rearrange("(kb p) e -> p kb e", p=128),
                  reads=[b_ccvalo[h]], writes=[b_Vp[s]])
        P.op("pool", lambda e: e.tensor_copy(out=kT_h[s][:, 0:256], in_=kTc[:, h, :]), reads=[b_kTc], writes=[b_kT[s]])
        P.op("pool", lambda e: e.tensor_copy(out=Vp[s][:, 0:2, 0:128], in_=vc[:, :, h, 0:128]), reads=[b_vc], writes=[b_Vp[s]])

    def att_S(h, s, q0, nq, i, kb):
        sp_i = i % 2
        for m in range(2):
            P.op("pe", (lambda m: lambda e: e.matmul(
                psum_all[:, (2 * sp_i + m) * 512:(2 * sp_i + m) * 512 + nq],
                lhsT=kT_h[s][64 * m:64 * m + 64, kb * 128:(kb + 1) * 128],
                rhs=qT_all[64 * m:64 * m + 64, h, q0:q0 + nq], start=True, stop=True))(m),
                reads=[b_kT[s], b_qT], writes=[b_pb[2 * sp_i + m]])
        P.op("act", lambda e: e.activation(
            out=PT[sp_i][:, :].rearrange("p (m q) -> p m q", m=2)[:, :, 0:nq],
            in_=psum_all[:, 2 * sp_i * 512:2 * sp_i * 512 + 1024].rearrange("p (m q) -> p m q", m=2)[:, :, 0:nq],
            func=AF.Exp, scale=DIFF_SCALE),
            reads=[b_pb[2 * sp_i], b_pb[2 * sp_i + 1]], writes=[b_PT[sp_i]])

    def att_PV(h, s, nqb, i, kb, last):
        sp_i = i % 2
        for qb in range(nqb):
            for m in range(2):
                P.op("pe", (lambda m, qb: lambda e: e.matmul(
                    psum_all[:, (4 + qb) * 512 + m * 129:(4 + qb) * 512 + m * 129 + 129],
                    lhsT=PT[sp_i][:, m * 512 + qb * 128:m * 512 + (qb + 1) * 128],
                    rhs=Vp[s][:, kb, 0:129], start=(i == 0 and m == 0), stop=last,
                    skip_group_check=True))(m, qb),
                    reads=[b_PT[sp_i], b_Vp[s]], writes=[b_pb[4 + qb]])

    def att_epi(h, ys, qb):
        es = qb % 2
        O = psum_all[:, (4 + qb) * 512:(4 + qb) * 512 + 258]
        r = ep_r[es]
        P.op("dve", lambda e: e.reciprocal(out=r[:, 0:2], in_=O.rearrange("p (m c) -> p m c", m=2)[:, :, 128]),
             reads=[b_pb[4 + qb]], writes=[b_ep[es]])
        P.op("dve", lambda e: e.tensor_tensor(out=r[:, 2:3], in0=r[:, 1:2], in1=neglam, op=ALU.mult),
             reads=[b_ep[es], b_lam], writes=[b_ep[es]])
        P.op("dve", lambda e: e.tensor_scalar(out=ep_o[es][:, :], in0=O[:, 0:128], scalar1=r[:, 0:1], scalar2=None, op0=ALU.mult),
             reads=[b_pb[4 + qb], b_ep[es]], writes=[b_ep[es]])
        P.op("dve", lambda e: e.scalar_tensor_tensor(out=ep_o[es][:, :], in0=O[:, 129:257], scalar=r[:, 2:3], in1=ep_o[es][:, :],
                                                     op0=ALU.mult, op1=ALU.add),
             reads=[b_pb[4 + qb], b_ep[es]], writes=[b_ep[es]])
        P.op("act", lambda e: e.activation(out=ep_j[es][:, :], in_=ep_o[es][:, :], func=AF.Square, accum_out=r[:, 3:4]),
             reads=[b_ep[es]], writes=[b_ep[es]])
        P.op("act", lambda e: e.activation(out=r[:, 4:5], in_=r[:, 3:4], func=AF.Sqrt, bias=epsc[:, 0:1], scale=1.0 / 128.0),
             reads=[b_ep[es], b_eps], writes=[b_ep[es]])
        P.op("dve", lambda e: e.reciprocal(out=r[:, 5:6], in_=r[:, 4:5]), reads=[b_ep[es]], writes=[b_ep[es]])
        P.op("dve", lambda e: e.scalar_tensor_tensor(
            out=yst[ys][:, qb, :], in0=ep_o[es][:, :], scalar=r[:, 5:6], in1=gsub[:, :], op0=ALU.mult, op1=ALU.mult),
            reads=[b_ep[es], b_lam], writes=[b_yst[ys]])

    def att_store(h, ys, q0, nq, nqb):
        t0 = q0 // 128
        P.dma("sp", ymix[q0:q0 + nq, 256 + h * 128:256 + (h + 1) * 128].rearrange("(qb p) e -> p qb e", p=128),
              yst[ys][:, 0:nqb, :], reads=[b_yst[ys]], writes=[S.b_ymix[t0 + i_] for i_ in range(nqb)])

    GROUPS = [int(v) for v in os.environ['K_GROUPS'].split(',')] if 'K_GROUPS' in os.environ else list(range(5))
    for h in range(ATT_H):
        s = h % 2
        load_head(h, s)
        for g in GROUPS:
            if g < 4:
                nq, nqb, q0, kbs = 512, 4, g * 512, list(range(66))
            else:
                nq, nqb, q0, kbs = 256, 2, 2048, [0, 1]
            att_S(h, s, q0, nq, 0, kbs[0])
            for i, kb in enumerate(kbs):
                if i + 1 < len(kbs):
                    att_S(h, s, q0, nq, i + 1, kbs[i + 1])
                att_PV(h, s, nqb, i, kb, i == len(kbs) - 1)
            ys = epi % 2
            epi += 1
            for qb in range(nqb):
                att_epi(h, ys, qb)
            att_store(h, ys, q0, nq, nqb)
    if dbg:
        d = P.dram("dbg_ymix", [ROWS, D], BF16, kind="ExternalOutput")
        P.dma("sp", d[:, :], ymix[:, :], reads=S.b_ymix)
    if stage == 2:
        return P.finish()
    P.barrier()
    Zd = P.dram("Zd", [128, 128, 256], BF16)
    b_Zd = P.bufs(4, "Zd")
    w3 = P.sb("w3_s", [128, 64], BF16)
    dftc = P.sb("dftc_s", [128, 2, 512], BF16)
    b_fc = P.buf()
    P.dma("sp", w3[:, :], w3_d[:, :], writes=[b_fc])
    P.dma("sp", dftc[:, :, :], dftc_d.ap().rearrange("(b p) n -> p b n", p=128), writes=[b_fc])
    Gb = [av(s_ * 8192, [128, 32, 256]) for s_ in range(1)]
    M1b = av(8192, [128, 32, 128])
    Zsb = av(8192 + 4096, [128, 32, 256])
    b_Gb, b_M1b, b_Zsb = P.buf(), P.buf(), P.buf()

    def f_stage1(blk):
        for part in range(2):
            for c_ in range(4):
                p0 = part * 64 + c_ * 16
                src = cc_f_out[c_][:, part * 256:(part + 1) * 256].rearrange("(g l) n -> g l n", l=128)[:, blk * 32:(blk + 1) * 32, :]
                P.dma("sp", Gb[0][p0:p0 + 16, :, :], src, reads=[b_ccf_out[c_]], writes=[b_Gb])
        P.dma("sp", M1b, m1_d.ap()[blk * 32:(blk + 1) * 32, :, :].rearrange("l k m -> k l m"), writes=[b_M1b])
        for l2 in range(32):
            bank = (l2 // 2) % 4
            o0 = bank * 512 + (l2 % 2) * 256
            P.op("pe", (lambda l2, o0: lambda e: e.matmul(psum_all[:, o0:o0 + 256], lhsT=M1b[:, l2, :], rhs=Gb[0][:, l2, :],
                                                          start=True, stop=True, skip_group_check=True))(l2, o0),
                 reads=[b_Gb, b_M1b], writes=[b_pb[bank]])
            if l2 % 2 == 1:
                eng = "act" if (l2 // 2) % 2 else "dve"
                P.op(eng, (lambda l2, bank, eng: (lambda e: e.copy(out=Zsb[:, l2 - 1:l2 + 1, :], in_=pb[bank].rearrange("p (a n) -> p a n", a=2)))
                           if eng == "act" else (lambda e: e.tensor_copy(out=Zsb[:, l2 - 1:l2 + 1, :], in_=pb[bank].rearrange("p (a n) -> p a n", a=2))))(l2, bank, eng),
                     reads=[b_pb[bank]], writes=[b_Zsb])
        P.dma("sp", Zd[:, blk * 32:(blk + 1) * 32, :], Zsb, reads=[b_Zsb], writes=[b_Zd[blk]])
    for blk in range(4):
        f_stage1(blk)

    Zt = [av(s_ * 2048, [128, 8, 256]) for s_ in range(2)]
    Ysb = av(4096, [32, 8, 256])
    b_Zt, b_Ysb = P.buf(), P.buf()
    P.barrier()

    def f_stage3(bt):
        for half in range(2):
            P.dma("sp", Zt[half], Zd[half * 64 + bt * 8:half * 64 + (bt + 1) * 8, :, :].rearrange("a l n -> l a n"),
                  reads=b_Zd, writes=[b_Zt])
        for pr in range(4):
            bank = pr % 2
            for half in range(2):
                P.op("pe", (lambda pr, half, bank: lambda e: e.matmul(
                    psum_all[0:32, bank * 512:(bank + 1) * 512], lhsT=w3[:, half * 32:(half + 1) * 32],
                    rhs=Zt[half][:, 2 * pr:2 * pr + 2, :], start=(half == 0), stop=(half == 1)))(pr, half, bank),
                    reads=[b_Zt, b_fc], writes=[b_pb[bank]])
            P.op("act", (lambda pr, bank: lambda e: e.copy(out=Ysb[:, 2 * pr:2 * pr + 2, :],
                                                           in_=psum_all[0:32, bank * 512:(bank + 1) * 512].rearrange("p (a n) -> p a n", a=2)))(pr, bank),
                 reads=[b_pb[bank]], writes=[b_Ysb])
        dst = ymix[0:TPC, 0:256].rearrange("(a b) n -> a b n", b=64)[:, bt * 8:(bt + 1) * 8, :]
        P.dma("sp", dst, Ysb, reads=[b_Ysb], writes=[S.b_ymix[t_] for t_ in range(NT)])
    for bt in range(8):
        f_stage3(bt)
    yc = av(8192, [128, 256])
    b_yc = P.buf()
    for lt in range(2):
        k_ = 0
        for lb in range(2):
            for part in range(2):
                P.op("pe", (lambda lt, lb, part, k_: lambda e: e.matmul(
                    psum_all[:, 1024:1280], lhsT=dftc[:, lb, part * 256 + lt * 128:part * 256 + (lt + 1) * 128],
                    rhs=abc[:, lb, part * 256:(part + 1) * 256], start=(k_ == 0), stop=(k_ == 3)))(lt, lb, part, k_),
                    reads=[b_fc, b_abc], writes=[b_pb[2]])
                k_ += 1
        P.op("dve", lambda e: e.tensor_copy(out=yc, in_=psum_all[:, 1024:1280]), reads=[b_pb[2]], writes=[b_yc])
        P.dma("sp", ymix[TPC + lt * 128:TPC + (lt + 1) * 128, 0:256], yc, reads=[b_yc], writes=[S.b_ymix[NT + lt]])
    if dbg:
        d = P.dram("dbg_yf", [ROWS, 256], BF16, kind="ExternalOutput")
        P.dma("sp", d[:, :], ymix[:, 0:256], reads=S.b_ymix)
    pnb = P.sb("pnb", [128, 5, D], F32)
    b_pnb = P.buf()
    yt = [P.sb("yt%d" % i, [128, D], BF16) for i in range(2)]
    b_yt = P.bufs(2, "yt")
    pt1 = [P.sb("pt1_%d" % i, [128, D], F32) for i in range(2)]
    b_pt1 = P.bufs(2, "pt1")
    b1T = P.sb("b1T_s", [128, 32], F32)
    b_b1T = P.buf()

    def load_pn(l, kind_gate, vi_bias, vi_g, vi_b):
        for who in range(2):
            P.dma("sp", pnb[:, who, :], modv[who, l, kind_gate * 1024:(kind_gate + 1) * 1024].partition_broadcast(128),
                  reads=[b_modv], writes=[b_pnb])
        for k_, vi in enumerate((vi_bias, vi_g, vi_b)):
            P.dma("sp", pnb[:, 2 + k_, :], vec_d[l, vi, :].partition_broadcast(128), writes=[b_pnb])

    def post_norm(t, ypsum, ybanks, x_ap, b_x, out_ap, b_out_list, slot):
        who = 0 if t < NT else 1
        p1 = pt1[slot]
        P.dma("sp", xt[slot][:, :], x_ap, reads=[b_x], writes=[b_xt[slot]])
        P.op("dve", lambda e: e.tensor_tensor(out=p1[:, :], in0=ypsum, in1=pnb[:, 2, :], op=ALU.add),
             reads=[b_pb[ybanks[0]], b_pb[ybanks[1]], b_pnb], writes=[b_pt1[slot]])
        P.op("pool", lambda e: e.tensor_tensor(out=p1[:, :], in0=p1[:, :], in1=pnb[:, who, :], op=ALU.mult),
             reads=[b_pnb], writes=[b_pt1[slot]])
        P.op("dve", lambda e: e.scalar_tensor_tensor(out=p1[:, :], in0=xt[slot][:, :], scalar=ALPHA, in1=p1[:, :],
                                                     op0=ALU.mult, op1=ALU.add),
             reads=[b_xt[slot]], writes=[b_pt1[slot]])
        mean, rstd = ln_stats(p1, b_pt1[slot], slot)
        P.op("dve", lambda e: e.tensor_scalar(out=p1[:, :], in0=p1[:, :], scalar1=mean, scalar2=rstd, op0=ALU.subtract, op1=ALU.mult),
             reads=[b_stt[slot]], writes=[b_pt1[slot]])
        P.op("pool", lambda e: e.tensor_tensor(out=p1[:, :], in0=p1[:, :], in1=pnb[:, 3, :], op=ALU.mult), reads=[b_pnb], writes=[b_pt1[slot]])
        P.op("pool", lambda e: e.tensor_tensor(out=p1[:, :], in0=p1[:, :], in1=pnb[:, 4, :], op=ALU.add), reads=[b_pnb], writes=[b_pt1[slot]])
        P.dma("sp", out_ap, p1[:, :], reads=[b_pt1[slot]], writes=b_out_list)

    def out_proj_phase(l, first_layer):
        P.barrier()
        wout_sb = av(0, [128, 8, D])
        b_wo = P.buf()
        P.dma("sp", wout_sb, wout_b[l, :, :].rearrange("(k p) n -> p k n", p=128), reads=[b_woutb[l]], writes=[b_wo])
        load_pn(l, 2, 0, 1, 2)
        yT = [av(8192 + s_ * 1024, [128, 8, 128]) for s_ in range(2)]
        b_yT = P.bufs(2, "yT")

        def tile(t):
            slot = t % 2
            P.dma("sp", yt[slot][:, :], ymix[t * 128:(t + 1) * 128, :], reads=[S.b_ymix[t]], writes=[b_yt[slot]])
            pq = psum_all[:, 3072:4096]
            for kc in range(8):
                P.op("pe", (lambda kc: lambda e: e.matmul(pq[:, kc * 128:(kc + 1) * 128], lhsT=yt[slot][:, kc * 128:(kc + 1) * 128],
                                                          rhs=ident_b[:, :], start=True, stop=True))(kc),
                     reads=[b_yt[slot], b_ident], writes=[b_pb[6 + kc // 4]])
            P.op("act", lambda e: e.copy(out=yT[slot][:, 0:4, :], in_=pq[:, 0:512].rearrange("p (k t) -> p k t", k=4)),
                 reads=[b_pb[6]], writes=[b_yT[slot]])
            P.op("dve", lambda e: e.tensor_copy(out=yT[slot][:, 4:8, :], in_=pq[:, 512:1024].rearrange("p (k t) -> p k t", k=4)),
                 reads=[b_pb[7]], writes=[b_yT[slot]])
            for hf in range(2):
                for kc in range(8):
                    P.op("pe", (lambda hf, kc: lambda e: e.matmul(pb[hf], lhsT=yT[slot][:, kc, :], rhs=wout_sb[:, kc, hf * 512:(hf + 1) * 512],
                                                                  start=(kc == 0), stop=(kc == 7)))(hf, kc),
                         reads=[b_yT[slot], b_wo], writes=[b_pb[hf]])
            x_ap = x_src(t, first_layer)
            post_norm(t, psum_all[:, 0:1024], (0, 1), x_ap, (Buf("xin") if first_layer else S.b_xres[t]),
                      xres[t * 128:(t + 1) * 128, :], [S.b_xres[t]], slot)
        for t in range(NTT):
            tile(t)

    def ffn_phase(l, last_layer):
        P.barrier()
        w2_sb = av(0, [128, 32, D])
        b_w2 = P.buf()
        for q4 in range(4):
            P.dma("sp", w2_sb[:, q4 * 8:(q4 + 1) * 8, :], w2_b[l, q4 * 1024:(q4 + 1) * 1024, :].rearrange("(k p) n -> p k n", p=128),
                  reads=[b_w2b[l]], writes=[b_w2])
        w1blk = [av(32768 + s_ * 4096, [128, 8, 512]) for s_ in range(2)]
        b_w1blk = P.bufs(2, "w1blk")
        hTg = av(32768 + 8192, [128, 8, 256])
        b_hTg = P.bufs(2, "hTg")
        aT = qT_all[:, :, :].rearrange("p h t -> p (h t)")[:, 0:8192].rearrange("p (c t) -> p c t", c=32)
        b_aT = P.buf()
        P.dma("sp", b1T[:, :], b1T_d[l, :, :], writes=[b_b1T])
        load_pn(l, 5, 3, 4, 5)
        cnt = [0]

        def group(t0):
            for i_ in range(2):
                t = t0 + i_
                ln_to_hT(t, xres[t * 128:(t + 1) * 128, :], S.b_xres[t], t % 2, l, (3, 4),
                         (lambda kc, i_=i_: hTg[:, kc, i_ * 128:(i_ + 1) * 128]), b_hTg, 6)
            for hb in range(8):
                s_ = cnt[0] % 2
                cnt[0] += 1
                P.dma("sp", w1blk[s_], w1_b[l, :, hb * 512:(hb + 1) * 512].rearrange("(k p) n -> p k n", p=128),
                      reads=[b_w1b[l]], writes=[b_w1blk[s_]])
                for hc in range(4):
                    bank = 4 + (hb * 4 + hc) % 2
                    for kc in range(8):
                        P.op("pe", (lambda hc, kc, bank, s_: lambda e: e.matmul(
                            psum_all[:, bank * 512:bank * 512 + 256], lhsT=w1blk[s_][:, kc, hc * 128:(hc + 1) * 128], rhs=hTg[:, kc, :],
                            start=(kc == 0), stop=(kc == 7)))(hc, kc, bank, s_),
                            reads=[b_w1blk[s_]] + b_hTg, writes=[b_pb[bank]])
                    c_ = hb * 4 + hc
                    P.op("act", (lambda c_, bank: lambda e: e.activation(out=aT[:, c_, :], in_=psum_all[:, bank * 512:bank * 512 + 256],
                                                                         func=AF.Relu, bias=b1T[:, c_:c_ + 1], scale=1.0))(c_, bank),
                         reads=[b_pb[bank], b_b1T], writes=[b_aT])
                    P.op("pool", (lambda c_: lambda e: e.tensor_tensor(out=aT[:, c_, :], in0=aT[:, c_, :], in1=aT[:, c_, :], op=ALU.mult))(c_),
                         reads=[b_aT], writes=[b_aT])
            for i_ in range(2):
                t = t0 + i_
                for hf in range(2):
                    bank = 2 * i_ + hf
                    for c_ in range(32):
                        P.op("pe", (lambda c_, hf, bank, i_: lambda e: e.matmul(
                            pb[bank], lhsT=aT[:, c_, i_ * 128:(i_ + 1) * 128], rhs=w2_sb[:, c_, hf * 512:(hf + 1) * 512],
                            start=(c_ == 0), stop=(c_ == 31)))(c_, hf, bank, i_),
                            reads=[b_aT, b_w2], writes=[b_pb[bank]])
                if last_layer and t < NT:
                    out_ap, bl = yout[t * 128:(t + 1) * 128, :], []
                elif last_layer:
                    continue
                else:
                    out_ap, bl = xres[t * 128:(t + 1) * 128, :], [S.b_xres[t]]
                post_norm(t, psum_all[:, 2 * i_ * 512:2 * i_ * 512 + 1024], (2 * i_, 2 * i_ + 1),
                          xres[t * 128:(t + 1) * 128, :], S.b_xres[t], out_ap, bl, t % 2)
        for t0 in range(0, NTT, 2):
            group(t0)

    out_proj_phase(0, True)
    if dbg:
        d = P.dram("dbg_x1a", [ROWS, D], F32, kind="ExternalOutput")
        P.dma("sp", d[:, :], xres[:, :], reads=S.b_xres)
    if stage == 3:
        return P.finish()
    ffn_phase(0, False)
    if dbg:
        d = P.dram("dbg_x1", [ROWS, D], F32, kind="ExternalOutput")
        P.dma("sp", d[:, :], xres[:, :], reads=S.b_xres)
    if stage == 4:
        return P.finish()
    P.barrier()
    wcd = av(0, [128, 8, 1536])
    b_wcd = P.buf()
    P.dma("pool", wcd, wcd_d.ap().rearrange("(k p) n -> p k n", p=128), writes=[b_wcd])
    wspT = av(12288, [128, 4, 128])
    bspT = P.sb("bspT_s", [128, 4], F32)
    b_wsp = P.buf()
    P.dma("pool", wspT, wspT_d.ap().rearrange("g q p -> q g p"), writes=[b_wsp])
    P.dma("sp", bspT[:, :], bspT_d[:, :], writes=[b_wsp])
    hT2 = [av(12288 + 512 + s_ * 1024, [128, 8, 128]) for s_ in range(2)]
    b_hT2 = [P.bufs(2, "hT2_%d_" % i) for i in range(2)]
    vgb = [av(12288 + 512 + 2048 + s_ * 512, [128, 512]) for s_ in range(2)]
    b_vg = P.bufs(2, "vg")
    usb = [P.sb("usb%d" % i, [128, 512], F32) for i in range(1)]
    b_usb = P.buf()
    gst = P.sb("gst", [128, 4, 8], F32)
    b_gst = P.buf()
    yg = [av(12288 + 512 + 2048 + 1024 + s_ * 1024, [128, 1024]) for s_ in range(2)]
    b_yg = P.bufs(2, "yg")

    S.sT = qT_all[:, :, :].rearrange("p h t -> p (h t)")[:, 0:4 * ROWS].rearrange("p (c t) -> p c t", c=4)
    S.b_sT = P.buf()

    def l1_tile(t):
        slot = t % 2
        ln_to_hT(t, xres[t * 128:(t + 1) * 128, :], S.b_xres[t], slot, 1, (0, 1), (lambda kc: hT2[slot][:, kc, :]), b_hT2[slot], 6)
        for ct in range(4):
            for kc in range(8):
                P.op("pe", (lambda ct, kc: lambda e: e.matmul(psum_all[:, 4 * 512 + ct * 128:4 * 512 + (ct + 1) * 128], lhsT=wcd[:, kc, ct * 128:(ct + 1) * 128],
                                                              rhs=hT2[slot][:, kc, :], start=(kc == 0 and ct == 0), stop=(kc == 7),
                                                              skip_group_check=True))(ct, kc),
                     reads=b_hT2[slot] + [b_wcd], writes=[b_pb[4]])
        P.op("act", lambda e: e.copy(out=S.sT[:, :, t * 128:(t + 1) * 128], in_=pb[4].rearrange("p (c t) -> p c t", c=4)),
             reads=[b_pb[4]], writes=[S.b_sT])
        if t >= NT:
            return
        for cg in range(3):
            for kc in range(8):
                P.op("pe", (lambda cg, kc: lambda e: e.matmul(pb[cg], lhsT=hT2[slot][:, kc, :], rhs=wcd[:, kc, cg * 512:(cg + 1) * 512],
                                                              start=(kc == 0), stop=(kc == 7)))(cg, kc),
                     reads=b_hT2[slot] + [b_wcd], writes=[b_pb[cg]])
        for g in range(4):
            vsl = pb[2][:, g * 128:(g + 1) * 128]
            P.op("dve", (lambda g, vsl: lambda e: e.bn_stats(out=gst[:, g, 0:6], in_=vsl))(g, vsl), reads=[b_pb[2]], writes=[b_gst])
            P.op("dve", (lambda g: lambda e: e.bn_aggr(out=gst[:, g, 6:8], in_=gst[:, g, 0:6]))(g), reads=[b_gst], writes=[b_gst])
        P.op("act", lambda e: e.activation(out=gst[:, :, 0], in_=gst[:, :, 7], func=AF.Sqrt, bias=epsc[:, 0:1], scale=1.0),
             reads=[b_gst, b_eps], writes=[b_gst])
        P.op("dve", lambda e: e.reciprocal(out=gst[:, :, 0], in_=gst[:, :, 0]), reads=[b_gst], writes=[b_gst])
        for g in range(4):
            P.op("dve", (lambda g: lambda e: e.tensor_scalar(out=vgb[slot][:, g * 128:(g + 1) * 128], in0=pb[2][:, g * 128:(g + 1) * 128],
                                                             scalar1=gst[:, g, 6:7], scalar2=gst[:, g, 0:1], op0=ALU.subtract, op1=ALU.mult))(g),
                 reads=[b_pb[2], b_gst], writes=[b_vg[slot]])
        for g in range(4):
            P.op("pe", (lambda g: lambda e: e.matmul(pb[3][:, g * 128:(g + 1) * 128], lhsT=wspT[:, g, :], rhs=vgb[slot][:, g * 128:(g + 1) * 128],
                                                     start=(g == 0), stop=True, skip_group_check=True))(g),
                 reads=[b_vg[slot], b_wsp], writes=[b_pb[3]])
        P.op("act", lambda e: e.copy(out=usb[0][:, :], in_=pb[1]), reads=[b_pb[1]], writes=[b_usb])
        for g in range(4):
            P.op("dve", (lambda g: lambda e: e.scalar_tensor_tensor(out=yg[slot][:, 512 + g * 128:512 + (g + 1) * 128], in0=pb[3][:, g * 128:(g + 1) * 128],
                                                                    scalar=bspT[:, g:g + 1], in1=usb[0][:, g * 128:(g + 1) * 128],
                                                                    op0=ALU.add, op1=ALU.mult))(g),
                 reads=[b_pb[3], b_wsp, b_usb], writes=[b_yg[slot]])
        P.op("pool", lambda e: e.memset(yg[slot][:, 0:512], 0.0), writes=[b_yg[slot]])
        P.dma("sp", ymix[t * 128:(t + 1) * 128, :], yg[slot], reads=[b_yg[slot]], writes=[S.b_ymix[t]])
    for t in range(NTT):
        l1_tile(t)
    zt2 = P.sb("zt2", [128, D], BF16)
    b_zt = P.buf()
    P.op("pool", lambda e: e.memset(zt2[:, :], 0.0), writes=[b_zt])
    for t in range(NT, NTT):
        P.dma("sp", ymix[t * 128:(t + 1) * 128, :], zt2[:, :], reads=[b_zt], writes=[S.b_ymix[t]])
    if dbg:
        d = P.dram("dbg_y1", [ROWS, D], BF16, kind="ExternalOutput")
        P.dma("sp", d[:, :], ymix[:, :], reads=S.b_ymix)
    if stage == 5:
        return P.finish()
    def s5_branch():
        P.barrier()
        T = TPC
        o = [0]

        def alloc(shape, dt=F32):
            n = 1
            for s_ in shape[1:]:
                n *= s_
            el = n * (2 if dt == F32 else 1)
            if len(shape) == 5:
                ap = av(o[0], [shape[0], shape[1] * shape[2], shape[3], shape[4]], dt).rearrange("p (a b) c d -> p a b c d", a=shape[1])
            else:
                ap = av(o[0], shape, dt)
            o[0] += el
            return ap
        lamT = alloc([128, 2, 64])
        dtT = alloc([128, 64])
        b_pp = P.buf()
        P.dma("sp", lamT, s5lam_d[:, :, :], writes=[b_pp])
        P.dma("sp", dtT, s5dt_d[:, :], writes=[b_pp])
        wk = [alloc([128, 64]) for _ in range(10)]
        pw = alloc([128, 12, 2, 64])
        b_pw = P.buf()

        def V_(fn, rd=(), wr=()):
            P.op("dve", fn, reads=[b_pp] + list(rd), writes=[b_pp] + list(wr))
        P.op("act", lambda e: e.activation(out=dtT, in_=dtT, func=AF.Exp), reads=[b_pp], writes=[b_pp])
        a_, th, mag, s8, c8, t0_, t1_, den, qr, qi = wk
        lr, li = lamT[:, 0, :], lamT[:, 1, :]
        V_(lambda e: e.tensor_tensor(out=a_, in0=lr, in1=dtT, op=ALU.mult))
        V_(lambda e: e.tensor_tensor(out=th, in0=li, in1=dtT, op=ALU.mult))
        P.op("act", lambda e: e.activation(out=mag, in_=a_, func=AF.Exp), reads=[b_pp], writes=[b_pp])
        P.op("act", lambda e: e.activation(out=s8, in_=th, func=AF.Sin, scale=1.0 / 8.0), reads=[b_pp], writes=[b_pp])
        P.op("act", lambda e: e.activation(out=t0_, in_=th, func=AF.Sin, scale=1.0 / 16.0), reads=[b_pp], writes=[b_pp])
        V_(lambda e: e.tensor_tensor(out=t0_, in0=t0_, in1=t0_, op=ALU.mult))
        V_(lambda e: e.tensor_scalar(out=c8, in0=t0_, scalar1=-2.0, scalar2=1.0, op0=ALU.mult, op1=ALU.add))
        for _ in range(3):
            V_(lambda e: e.tensor_tensor(out=t0_, in0=c8, in1=c8, op=ALU.mult))
            V_(lambda e: e.tensor_tensor(out=t1_, in0=s8, in1=s8, op=ALU.mult))
            V_(lambda e: e.scalar_tensor_tensor(out=s8, in0=s8, scalar=2.0, in1=c8, op0=ALU.mult, op1=ALU.mult))
            V_(lambda e: e.tensor_tensor(out=c8, in0=t0_, in1=t1_, op=ALU.subtract))
        V_(lambda e: e.tensor_tensor(out=pw[:, 0, 0, :], in0=mag, in1=c8, op=ALU.mult), wr=[b_pw])
        V_(lambda e: e.tensor_tensor(out=pw[:, 0, 1, :], in0=mag, in1=s8, op=ALU.mult), wr=[b_pw])
        V_(lambda e: e.tensor_scalar(out=t0_, in0=pw[:, 0, 0, :], scalar1=-1.0, scalar2=None, op0=ALU.add))
        V_(lambda e: e.tensor_tensor(out=den, in0=lr, in1=lr, op=ALU.mult))
        V_(lambda e: e.tensor_tensor(out=t1_, in0=li, in1=li, op=ALU.mult))
        V_(lambda e: e.tensor_tensor(out=den, in0=den, in1=t1_, op=ALU.add))
        V_(lambda e: e.reciprocal(out=den, in_=den))
        V_(lambda e: e.tensor_tensor(out=qr, in0=t0_, in1=lr, op=ALU.mult))
        V_(lambda e: e.tensor_tensor(out=t1_, in0=pw[:, 0, 1, :], in1=li, op=ALU.mult))
        V_(lambda e: e.tensor_tensor(out=qr, in0=qr, in1=t1_, op=ALU.add))
        V_(lambda e: e.tensor_tensor(out=qr, in0=qr, in1=den, op=ALU.mult))
        V_(lambda e: e.tensor_tensor(out=qi, in0=pw[:, 0, 1, :], in1=lr, op=ALU.mult))
        V_(lambda e: e.tensor_tensor(out=t1_, in0=t0_, in1=li, op=ALU.mult))
        V_(lambda e: e.tensor_tensor(out=qi, in0=qi, in1=t1_, op=ALU.subtract))
        V_(lambda e: e.tensor_tensor(out=qi, in0=qi, in1=den, op=ALU.mult))
        for k in range(11):
            V_((lambda k: lambda e: e.tensor_tensor(out=t0_, in0=pw[:, k, 0, :], in1=pw[:, k, 0, :], op=ALU.mult))(k))
            V_((lambda k: lambda e: e.tensor_tensor(out=t1_, in0=pw[:, k, 1, :], in1=pw[:, k, 1, :], op=ALU.mult))(k))
            V_((lambda k: lambda e: e.scalar_tensor_tensor(out=pw[:, k + 1, 1, :], in0=pw[:, k, 0, :], scalar=2.0, in1=pw[:, k, 1, :],
                                                           op0=ALU.mult, op1=ALU.mult))(k), wr=[b_pw])
            V_((lambda k: lambda e: e.tensor_tensor(out=pw[:, k + 1, 0, :], in0=t0_, in1=t1_, op=ALU.subtract))(k), wr=[b_pw])
        qd = P.dram("s5_qd", [2, 8, 8, 64], F32)
        b_qd = P.buf()
        P.dma("sp", qd[0, :, :, :].rearrange("g rc p -> p (g rc)"), qr[0:64, :], reads=[b_pp], writes=[b_qd], allow_slow_non_contiguous=True)
        P.dma("sp", qd[1, :, :, :].rearrange("g rc p -> p (g rc)"), qi[0:64, :], reads=[b_pp], writes=[b_qd], allow_slow_non_contiguous=True)
        o_reuse = o[0]
        BT = alloc([128, 2, 2, 4, 64])
        QB = alloc([128, 2, 2, 4, 64])
        WBf = alloc([128, 2, 4, 128])
        b_B = P.buf()
        P.dma("sp", BT, s5bT_d[:, :, :, :, :], writes=[b_B])
        for ri in range(2):
            for g8 in range(8):
                P.dma("sp", QB[16 * g8:16 * g8 + 16, ri, :, :, :].rearrange("p r c q -> p (r c q)"),
                      qd[ri, g8, :, :].rearrange("rc p -> (rc p)").partition_broadcast(16),
                      reads=[b_qd], writes=[b_B], allow_slow_non_contiguous=True)
        tB = [alloc([128, 2, 4, 64]) for _ in range(2)]

        def B_(fn):
            P.op("dve", fn, reads=[b_B], writes=[b_B])
        WBv = WBf.rearrange("p r c (h q) -> p r c h q", h=2)
        B_(lambda e: e.tensor_tensor(out=tB[0], in0=QB[:, 0], in1=BT[:, 0], op=ALU.mult))
        B_(lambda e: e.tensor_tensor(out=tB[1], in0=QB[:, 1], in1=BT[:, 1], op=ALU.mult))
        B_(lambda e: e.tensor_tensor(out=WBv[:, :, :, 0, :], in0=tB[0], in1=tB[1], op=ALU.subtract))
        B_(lambda e: e.tensor_tensor(out=tB[0], in0=QB[:, 0], in1=BT[:, 1], op=ALU.mult))
        B_(lambda e: e.tensor_tensor(out=tB[1], in0=QB[:, 1], in1=BT[:, 0], op=ALU.mult))
        B_(lambda e: e.tensor_tensor(out=WBv[:, :, :, 1, :], in0=tB[0], in1=tB[1], op=ALU.add))
        CT = alloc([128, 2, 32, 16])
        cst = alloc([128, 128 + 8 + 4 + 4])
        b_C = P.buf()
        P.dma("sp", CT, s5cT_d[:, :, :, :], writes=[b_C])
        P.dma("sp", cst[:, 0:140], s5cst_d[:, :], writes=[b_C])
        P.dma("sp", cst[:, 140:144], s5d_d[:, :], writes=[b_C])
        P.op("dve", lambda e: e.tensor_scalar(out=CT[64:128], in0=CT[64:128], scalar1=-1.0, scalar2=None, op0=ALU.mult), reads=[b_C], writes=[b_C])
        Smat, rowmask, onehot, dcol = cst[:, 0:128], cst[:, 128:136], cst[:, 136:140], cst[:, 140:144]
        AT = alloc([128, 12, 128])
        Bm = alloc([128, 128], BF16)
        CZ = alloc([128, 128])
        X = [alloc([128, T]) for _ in range(2)]
        Vb = alloc([128, T])
        Xc = [Vb[:, 0:256], Vb[:, 256:512]]
        Fall = alloc([128, 64, 2])
        yacc = alloc([128, 4, T])
        b_AT, b_Bm, b_CZ, b_Fall, b_yacc = P.buf(), P.buf(), P.buf(), P.buf(), P.buf()
        b_X = [P.bufs(2, "X%d_" % i) for i in range(2)]
        b_Xc = [P.bufs(2, "Xc%d_" % i) for i in range(2)]
        P.op("pool", lambda e: e.memset(CZ, 0.0), writes=[b_CZ])
        sT = S.sT

        def build_AT(col):
            for k in range(12):
                P.op("dve", (lambda k: lambda e: e.tensor_scalar(out=AT[:, k, :], in0=ident_f[:, :], scalar1=pw[:, k, 0, col:col + 1],
                                                                 scalar2=None, op0=ALU.mult))(k), reads=[b_pw, b_ident], writes=[b_AT])
                P.op("dve", (lambda k: lambda e: e.scalar_tensor_tensor(out=AT[:, k, :], in0=Smat, scalar=pw[:, k, 1, col:col + 1], in1=AT[:, k, :],
                                                                        op0=ALU.mult, op1=ALU.add))(k), reads=[b_pw, b_C, b_AT], writes=[b_AT])

        def scan_level(Xb, b_Xb, n, r, k, cw, cur, ev):
            sh = 1 << k
            lo_all, hi_all = (sh, n) if r == 0 else (0, n - sh)
            if r == 0:
                ca, cb = 0, min(sh, n)
            else:
                ca, cb = max(n - sh, 0), n
            P.op("act", lambda e: e.copy(out=Xb[1 - cur][:, ca:cb], in_=Xb[cur][:, ca:cb]), reads=b_Xb[cur], writes=[b_Xb[1 - cur][0]])
            for c0 in range(lo_all, hi_all, cw):
                c1 = min(hi_all, c0 + cw)
                bank = 5 + (ev[0] % 2)
                ev[0] += 1
                s0 = c0 - sh if r == 0 else c0 + sh
                P.op("pe", (lambda c0, c1, bank, s0: lambda e: e.matmul(
                    psum_all[:, bank * 512:bank * 512 + c1 - c0], lhsT=AT[:, k, :], rhs=Xb[cur][:, s0:s0 + c1 - c0],
                    start=True, stop=True))(c0, c1, bank, s0),
                    reads=b_Xb[cur] + [b_AT], writes=[b_pb[bank]])
                P.op("dve", (lambda c0, c1, bank: lambda e: e.tensor_tensor(
                    out=Xb[1 - cur][:, c0:c1], in0=psum_all[:, bank * 512:bank * 512 + c1 - c0], in1=Xb[cur][:, c0:c1], op=ALU.add))(c0, c1, bank),
                    reads=[b_pb[bank]] + b_Xb[cur], writes=[b_Xb[1 - cur][1]])

        def scan2(r):
            ev = [0]
            cur, curc = 0, 0
            for k in range(11):
                scan_level(X, b_X, T, r, k, 512, cur, ev)
                cur = 1 - cur
                if k < 8:
                    scan_level(Xc, b_Xc, 256, r, k, 256, curc, ev)
                    curc = 1 - curc
            return cur, curc

        def drive(dst, b_dst, ct, tok0, n, cw):
            for c0 in range(0, n, cw):
                c1 = min(n, c0 + cw)
                P.op("pe", (lambda c0, c1: lambda e: e.matmul(psum_all[:, 4 * 512:4 * 512 + c1 - c0], lhsT=Bm, rhs=sT[:, ct, tok0 + c0:tok0 + c1],
                                                              start=True, stop=True))(c0, c1), reads=[b_Bm, S.b_sT], writes=[b_pb[4]])
                P.op("act", (lambda c0, c1: lambda e: e.copy(out=dst[:, c0:c1], in_=psum_all[:, 4 * 512:4 * 512 + c1 - c0]))(c0, c1),
                     reads=[b_pb[4]], writes=b_dst)

        def out_contrib(src, b_src, first, last):
            for c in range(4):
                P.op("pe", (lambda c: lambda e: e.matmul(pb[c], lhsT=CZ, rhs=src[:, c * 512:(c + 1) * 512], start=first, stop=last,
                                                         skip_group_check=True))(c), reads=[b_CZ] + list(b_src), writes=[b_pb[c]])

        def set_group(ct, g8, r):
            g = ct * 8 + g8
            col = g8 * 8 + r * 4 + ct
            build_AT(col)
            P.op("dve", lambda e: e.tensor_scalar(out=Bm, in0=WBf[:, r, ct, :], scalar1=rowmask[:, g8:g8 + 1], scalar2=None, op0=ALU.mult),
                 reads=[b_B, b_C], writes=[b_Bm])
            P.op("act", lambda e: e.copy(out=CZ[:, 16 * g8:16 * g8 + 16], in_=CT[:, r, g, :]), reads=[b_C], writes=[b_CZ])
            return g, col

        def clear_group(g8):
            P.op("pool", lambda e: e.memset(CZ[:, 16 * g8:16 * g8 + 16], 0.0), writes=[b_CZ])

        for ct in range(4):
            n_acc = 0
            for g8 in range(8):
                for r in range(2):
                    g, col = set_group(ct, g8, r)
                    drive(X[0], b_X[0], ct, 0, T, 512)
                    drive(Xc[0], b_Xc[0], ct, T, 256, 256)
                    cur, curc = scan2(r)
                    fcol = T - 1 if r == 0 else 0
                    fcc = 255 if r == 0 else 0
                    P.op("act", (lambda cur, fcol, col: lambda e: e.copy(out=Fall[:, col, 0:1], in_=X[cur][:, fcol:fcol + 1]))(cur, fcol, col),
                         reads=b_X[cur], writes=[b_Fall])
                    P.op("act", (lambda curc, fcc, col: lambda e: e.copy(out=Fall[:, col, 1:2], in_=Xc[curc][:, fcc:fcc + 1]))(curc, fcc, col),
                         reads=b_Xc[curc], writes=[b_Fall])
                    out_contrib(X[cur], b_X[cur], n_acc == 0, n_acc == 15)
                    n_acc += 1
                clear_group(g8)
            for c in range(4):
                P.op("dve" if c % 2 else "act",
                     (lambda c, ct: (lambda e: e.tensor_copy(out=yacc[:, ct, c * 512:(c + 1) * 512], in_=pb[c])) if c % 2 else
                      (lambda e: e.copy(out=yacc[:, ct, c * 512:(c + 1) * 512], in_=pb[c])))(c, ct),
                     reads=[b_pb[c]], writes=[b_yacc])
        ccs_i = P.dram("ccs5_i", [128, 64], F32)
        ccs_o = P.dram("ccs5_o", [512, 64], F32)
        b_ci, b_co = P.buf(), P.buf()
        P.dma("sp", ccs_i[:, :], Fall[:, :, 0], reads=[b_Fall], writes=[b_ci], allow_slow_non_contiguous=True)
        P.collective("AllGather", G4, ccs_i.ap().opt(), ccs_o.ap().opt(), reads=[b_ci], writes=[b_co])
        Fg = alloc([128, 4, 64])
        b_Fg = P.buf()
        P.dma("sp", Fg, ccs_o.ap().rearrange("(k p) n -> p k n", p=128), reads=[b_co], writes=[b_Fg])
        Sk = alloc([128, 8])
        b_Sk, b_Vb = P.buf(), P.buf()

        for ct in range(4):
            n_acc = 0
            for g8 in range(8):
                for r in range(2):
                    g, col = set_group(ct, g8, r)
                    order = [0, 1, 2, 3] if r == 0 else [3, 2, 1, 0]
                    P.op("dve", (lambda col, k0: lambda e: e.tensor_copy(out=Sk[:, k0:k0 + 1], in_=Fall[:, col, 1:2]))(col, order[0]),
                         reads=[b_Fall], writes=[b_Sk])
                    for a_i in range(3):
                        kp, kn = order[a_i], order[a_i + 1]
                        P.op("pe", (lambda kp: lambda e: e.matmul(psum_all[:, 4 * 512:4 * 512 + 1], lhsT=AT[:, 11, :], rhs=Sk[:, kp:kp + 1], start=True, stop=True))(kp),
                             reads=[b_AT, b_Sk], writes=[b_pb[4]])
                        P.op("dve", (lambda kp, kn, col: lambda e: e.tensor_tensor(out=Sk[:, kn:kn + 1], in0=psum_all[:, 4 * 512:4 * 512 + 1],
                                                                                   in1=Fg[:, kp, col:col + 1], op=ALU.add))(kp, kn, col),
                             reads=[b_pb[4], b_Fg], writes=[b_Sk])
                    P.op("dve", lambda e: e.tensor_tensor(out=Sk[:, 4:8], in0=Sk[:, 0:4], in1=onehot, op=ALU.mult), reads=[b_Sk, b_C], writes=[b_Sk])
                    P.op("dve", lambda e: e.reduce_sum(out=Sk[:, 4:5], in_=Sk[:, 4:8], axis=AX.X), reads=[b_Sk], writes=[b_Sk])
                    def vcol(a, b, r=r):
                        return (Vb[:, a:b] if r == 0 else Vb[:, T - b:T - a])
                    P.op("pe", lambda e: e.matmul(psum_all[:, 4 * 512:4 * 512 + 1], lhsT=AT[:, 0, :], rhs=Sk[:, 4:5], start=True, stop=True),
                         reads=[b_AT, b_Sk], writes=[b_pb[4]])
                    P.op("dve", (lambda dst: lambda e: e.tensor_copy(out=dst, in_=psum_all[:, 4 * 512:4 * 512 + 1]))(vcol(0, 1)),
                         reads=[b_pb[4]], writes=[b_Vb])
                    for k in range(11):
                        sh = 1 << k
                        for c0 in range(0, sh, 512):
                            c1 = min(sh, c0 + 512)
                            bank = 5 + (k % 2)
                            P.op("pe", (lambda k, c0, c1, bank, src: lambda e: e.matmul(psum_all[:, bank * 512:bank * 512 + c1 - c0], lhsT=AT[:, k, :],
                                                                                        rhs=src, start=True, stop=True))(k, c0, c1, bank, vcol(c0, c1)),
                                 reads=[b_AT, b_Vb], writes=[b_pb[bank]])
                            P.op("act", (lambda c0, c1, bank, dst: lambda e: e.copy(out=dst,
                                                                                    in_=psum_all[:, bank * 512:bank * 512 + c1 - c0]))(c0, c1, bank, vcol(sh + c0, sh + c1)),
                                 reads=[b_pb[bank]], writes=[b_Vb])
                    out_contrib(Vb, [b_Vb], n_acc == 0, n_acc == 15)
                    n_acc += 1
                clear_group(g8)
            for c in range(4):
                P.op("dve", (lambda c, ct: lambda e: e.tensor_tensor(out=yacc[:, ct, c * 512:(c + 1) * 512], in0=pb[c], in1=yacc[:, ct, c * 512:(c + 1) * 512],
                                                                     op=ALU.add))(c, ct), reads=[b_pb[c]], writes=[b_yacc])
        P.barrier()
        o[0] = o_reuse
        wg = alloc([128, 4, 512])
        bgc = alloc([128, 4])
        b_wg = P.buf()
        P.dma("sp", wg, wglu_d.ap().rearrange("(k p) n -> p k n", p=128), writes=[b_wg])
        P.dma("sp", bgc, bgluT_d[:, :], writes=[b_wg])
        tmp = [X[0], X[1]]
        for ct in range(4):
            ya = yacc[:, ct, :]
            P.op("dve", (lambda ct, ya: lambda e: e.scalar_tensor_tensor(out=ya, in0=sT[:, ct, 0:T], scalar=dcol[:, ct:ct + 1], in1=ya,
                                                                         op0=ALU.mult, op1=ALU.add))(ct, ya), reads=[S.b_sT, b_C, b_yacc], writes=[b_yacc])
            P.op("dve", (lambda ya: lambda e: e.tensor_tensor(out=tmp[0], in0=ya, in1=ya, op=ALU.mult))(ya), reads=[b_yacc], writes=b_X[0])
            P.op("dve", lambda e: e.tensor_scalar(out=tmp[0], in0=tmp[0], scalar1=0.044715 * 0.7978845608, scalar2=0.7978845608, op0=ALU.mult, op1=ALU.add),
                 reads=b_X[0], writes=b_X[0])
            P.op("dve", (lambda ya: lambda e: e.tensor_tensor(out=tmp[0], in0=tmp[0], in1=ya, op=ALU.mult))(ya), reads=[b_yacc] + b_X[0], writes=b_X[0])
            P.op("act", lambda e: e.activation(out=tmp[0], in_=tmp[0], func=AF.Tanh), reads=b_X[0], writes=b_X[0])
            P.op("dve", lambda e: e.tensor_scalar(out=tmp[0], in0=tmp[0], scalar1=0.5, scalar2=0.5, op0=ALU.mult, op1=ALU.add), reads=b_X[0], writes=b_X[0])
            P.op("dve", (lambda ct, ya: lambda e: e.tensor_tensor(out=ya, in0=tmp[0], in1=ya, op=ALU.mult))(ct, ya), reads=b_X[0] + [b_yacc], writes=[b_yacc])
        ysT = alloc([128, 4, 512], BF16)
        b_ysT = P.buf()
        yts = alloc([128, 512], BF16)
        b_yts = P.buf()
        for c in range(4):
            for co in range(4):
                for k in range(4):
                    P.op("pe", (lambda c, co, k: lambda e: e.matmul(pb[co], lhsT=wg[:, k, co * 128:(co + 1) * 128], rhs=yacc[:, k, c * 512:(c + 1) * 512],
                                                                    start=(k == 0), stop=(k == 3)))(c, co, k), reads=[b_wg, b_yacc], writes=[b_pb[co]])
                P.op("act", (lambda co: lambda e: e.activation(out=tmp[1][:, co * 512:(co + 1) * 512], in_=pb[co], func=AF.Sigmoid,
                                                               bias=bgc[:, co:co + 1], scale=1.0))(co), reads=[b_pb[co], b_wg], writes=b_X[1])
                P.op("dve", (lambda c, co: lambda e: e.tensor_tensor(out=ysT[:, co, :], in0=tmp[1][:, co * 512:(co + 1) * 512],
                                                                     in1=yacc[:, co, c * 512:(c + 1) * 512], op=ALU.mult))(c, co),
                     reads=b_X[1] + [b_yacc], writes=[b_ysT])
            for tt in range(4):
                t = c * 4 + tt
                for co in range(4):
                    P.op("pe", (lambda co, tt: lambda e: e.matmul(psum_all[:, 4 * 512 + co * 128:4 * 512 + (co + 1) * 128], lhsT=ysT[:, co, tt * 128:(tt + 1) * 128],
                                                                  rhs=ident_b[:, :], start=True, stop=True, skip_group_check=True))(co, tt),
                         reads=[b_ysT, b_ident], writes=[b_pb[4]])
                P.op("act", lambda e: e.copy(out=yts, in_=pb[4]), reads=[b_pb[4]], writes=[b_yts])
                P.dma("sp", ymix[t * 128:(t + 1) * 128, 0:512], yts, reads=[b_yts], writes=[S.b_ymix[t]])
    s5_branch()
    out_proj_phase(1, False)
    ffn_phase(1, True)
    return P.finish()


def _prep_inputs(inp):
    f32 = np.float32
    cos, sin = _rope_tables()
    ccs, m1, w3re, w3im, cosc, sinc = _fourier_consts()
    vecs = np.stack([np.stack([inp["b_out"][l], inp["ln_mix_g"][l], inp["ln_mix_b"][l],
                               inp["b_ffn2"][l], inp["ln_ffn_g"][l], inp["ln_ffn_b"][l]], 0) for l in range(2)], 0)
    b1T = np.ascontiguousarray(inp["b_ffn1"].reshape(2, 32, 128).transpose(0, 2, 1))
    lamv = np.stack([inp["lam_q1"][0], inp["lam_k1"][0], inp["lam_q2"][0], inp["lam_k2"][0]], 0)
    common = {
        "vecs": np.ascontiguousarray(vecs.astype(f32)), "b1T": b1T.astype(f32),
        "win0": np.ascontiguousarray(inp["w_in_ab"][0]), "wfT": np.ascontiguousarray(inp["w_in_ab"][0][:, :256].T),
        "ccs": ccs, "wout": inp["w_out"], "w1": inp["w_ffn1"], "w2": inp["w_ffn2"],
        "ident": np.eye(128, dtype=f32), "lamv": lamv.astype(f32), "subg": inp["subln_g"].astype(f32),
        "wcd": np.ascontiguousarray(inp["w_in_cd"][0]),
        "wspT": np.ascontiguousarray(inp["w_sp"][0].transpose(0, 2, 1)),
        "bspT": np.ascontiguousarray(inp["b_sp"][0].T),
        "s5lam": np.ascontiguousarray(np.tile(np.stack([inp["s5_lam_re"][0].reshape(2, 4, 8, 64).transpose(3, 2, 0, 1).reshape(64, 64), inp["s5_lam_im"][0].reshape(2, 4, 8, 64).transpose(3, 2, 0, 1).reshape(64, 64)], 1), (2, 1, 1)).astype(f32)),
        "s5dt": np.ascontiguousarray(np.broadcast_to(inp["s5_log_dt"][0].reshape(2, 4, 8).transpose(2, 0, 1).reshape(1, 64), (128, 64)).astype(f32)),
        "s5bT": np.ascontiguousarray(np.stack([inp["s5_b_re"][0], inp["s5_b_im"][0]], 0).reshape(2, 2, 4, 8, 64, 16).transpose(3, 5, 0, 1, 2, 4).reshape(128, 2, 2, 4, 64).astype(f32)),
        "s5cT": np.ascontiguousarray(np.concatenate([inp["s5_c_re"][0].transpose(3, 0, 1, 2), inp["s5_c_im"][0].transpose(3, 0, 1, 2)], 0).astype(f32)),
        "s5d": np.ascontiguousarray(inp["s5_d"][0].reshape(4, 128).T.astype(f32)),
        "wglu": np.ascontiguousarray(inp["w_glu"][0]), "bgluT": np.ascontiguousarray(inp["b_glu"][0].reshape(4, 128).T.astype(f32)),
        "m1c": _bf(m1), "dftc": _bf(np.concatenate([cosc, sinc], 1)),
    }
    maps = []
    for r in range(NCORE):
        b, j = r // 4, r % 4
        m = dict(common)
        m["xin"] = np.ascontiguousarray(np.concatenate([inp["x"][b, TPC * j:TPC * (j + 1)], inp["ctx"][b]], 0))
        c_all = np.stack([inp["c"][b], inp["c_ctx"]], 0).astype(f32)
        m["cT"] = np.ascontiguousarray(c_all.reshape(2, 8, 128).transpose(2, 1, 0))
        m["wmod"] = np.ascontiguousarray(inp["w_mod"][:, :, 1536 * j:1536 * (j + 1)])
        m["bmod"] = np.ascontiguousarray(inp["b_mod"][:, 1536 * j:1536 * (j + 1)])
        m["rope"] = np.ascontiguousarray(np.concatenate([cos[b * 0 + TPC * j:TPC * (j + 1)], sin[TPC * j:TPC * (j + 1)]], 1))
        cst = np.zeros((128, 140), f32)
        for k_ in range(64):
            cst[k_, k_ + 64] = 1.0
            cst[k_ + 64, k_] = -1.0
        for p_ in range(128):
            cst[p_, 128 + p_ // 16] = 1.0
        cst[:, 136 + j] = 1.0
        m["s5cst"] = cst
        m["w3c"] = _bf(np.concatenate([w3re[:, 32 * j:32 * (j + 1)], w3im[:, 32 * j:32 * (j + 1)]], 1))
        maps.append(m)
    return maps


def kernel(**inputs):
    inp = {k: np.asarray(v) for k, v in inputs.items()}
    maps = _prep_inputs(inp)
    nc = build()
    res = run_bass_kernel_spmd(nc, maps, core_ids=list(range(NCORE)))
    out = np.zeros((2, SEQ, D), np.float32)
    for r in range(NCORE):
        b, j = r // 4, r % 4
        out[b, TPC * j:TPC * (j + 1)] = res.results[r]["yout"]
    return out
```

```python
from contextlib import ExitStack
import math
import numpy as np
import ml_dtypes
import concourse.bass as bass
import concourse.mybir as mybir
from concourse.bass_utils import run_bass_kernel_spmd

F32 = mybir.dt.float32
BF16 = mybir.dt.bfloat16
AF = mybir.ActivationFunctionType
ALU = mybir.AluOpType
AX = mybir.AxisListType
ENGS = ("pe", "act", "dve", "pool", "sp")
NDS = 48

D = 1024
SEQ = 8192
NCORE = 8
TPC = 2048
NT = 16
NCT = 2
NTT = 18
ROWS = NTT * 128
ALPHA = 4 ** 0.25
LN_EPS = 1e-5
DIFF_SCALE = 0.125


class Buf:
    __slots__ = ("name", "lw", "rd", "excl")

    def __init__(self, name):
        self.name = name
        self.lw = None
        self.rd = []
        self.excl = False


class Prog:
    def __init__(self):
        self.nc = bass.Bass("TRN2", target_bir_lowering=False)
        nc = self.nc
        self.es = ExitStack()
        self.q = {e: [] for e in ENGS}
        self.cnt = {e: 0 for e in ENGS}
        self.known = {e: {} for e in ENGS}
        self.esem = {e: self.es.enter_context(nc.semaphore("s_" + e)) for e in ENGS}
        self.dsem = [self.es.enter_context(nc.semaphore("d%d" % i)) for i in range(NDS)]
        self.dcnt = [0] * NDS
        self.dnext = 0
        self.nbuf = 0
        self.ncc = 0
        self.ccsems = []

    def sb(self, name, shape, dt):
        return self.es.enter_context(self.nc.sbuf_tensor(name, list(shape), dt))

    def ps(self, name, shape, dt):
        return self.es.enter_context(self.nc.psum_tensor(name, list(shape), dt))

    def dram(self, name, shape, dt, kind=None):
        if kind is None:
            return self.nc.dram_tensor(name, list(shape), dt)
        return self.nc.dram_tensor(name, list(shape), dt, kind=kind)

    def buf(self, name=None):
        self.nbuf += 1
        return Buf(name or ("b%d" % self.nbuf))

    def bufs(self, n, name="b"):
        return [self.buf("%s%d" % (name, i)) for i in range(n)]

    def _deps(self, e, reads, writes):
        deps = []
        for b in reads:
            if b.lw is not None:
                deps.append(b.lw)
        for b in writes:
            if b.lw is not None:
                deps.append(b.lw)
            deps.extend(b.rd)
        kn = self.known[e]
        best = {}
        for (sem, val, eng) in deps:
            if eng == e and e == "pe":
                continue
            if kn.get(sem, 0) >= val:
                continue
            if best.get(sem, (None, 0))[1] < val:
                best[sem] = (sem, val)
        waits = []
        for sem, (s, val) in best.items():
            kn[sem] = val
            waits.append((s, val))
        return waits

    def _commit(self, tok, reads, writes):
        for b in writes:
            b.lw = tok
            b.rd = []
        for b in reads:
            if b not in writes:
                b.rd.append(tok)

    def op(self, e, fn, reads=(), writes=()):
        reads = list(reads)
        writes = list(writes)
        if e != "pe":
            writes = writes + [b for b in reads if b.excl and b not in writes]
        waits = self._deps(e, reads, writes)
        self.cnt[e] += 1
        tok = (self.esem[e], self.cnt[e], e)
        self.q[e].append((waits, fn, (self.esem[e], 1)))
        self._commit(tok, reads, writes)
        return tok

    def dma(self, e, out, in_, reads=(), writes=(), **kw):
        reads = list(reads)
        writes = list(writes)
        waits = self._deps(e, reads, writes)
        i = self.dnext
        self.dnext = (self.dnext + 1) % NDS
        sem = self.dsem[i]
        if self.dcnt[i] > 0 and self.known[e].get(sem, 0) < self.dcnt[i]:
            waits.append((sem, self.dcnt[i]))
            self.known[e][sem] = self.dcnt[i]
        self.dcnt[i] += 16
        tok = (sem, self.dcnt[i], "dma")
        self.q[e].append((waits, (lambda eng: eng.dma_start(out=out, in_=in_, **kw)), (sem, 16)))
        self._commit(tok, reads, writes)
        return tok

    def collective(self, kind, groups, in_ap, out_ap, reads=(), writes=()):
        e = "pool"
        reads = list(reads)
        writes = list(writes)
        waits = self._deps(e, reads, writes)
        sem = self.es.enter_context(self.nc.semaphore("cc%d" % self.ncc))
        self.ncc += 1
        self.ccsems.append(sem)
        tok = (sem, 1, "cc")

        def fn(eng):
            return eng.collective_compute(kind, ALU.bypass, replica_groups=groups,
                                          ins=[in_ap], outs=[out_ap])
        self.q[e].append((waits, fn, (sem, None)))
        self._commit(tok, reads, writes)
        return tok

    def barrier(self):
        fin = []
        for i in range(NDS):
            if self.dcnt[i] > 0:
                fin.append((self.dsem[i], self.dcnt[i]))
        for s in self.ccsems:
            fin.append((s, 1))
        for e in ENGS:
            if self.cnt[e] > 0:
                fin.append((self.esem[e], self.cnt[e]))
        for e in ENGS:
            w = []
            for (s, v) in fin:
                if self.known[e].get(s, 0) < v and not (s is self.esem[e]):
                    w.append((s, v))
                    self.known[e][s] = v
            if w:
                self.q[e].append((w, None, None))

    def finish(self):
        self.barrier()
        nc = self.nc

        def mk(e):
            def body(eng):
                for waits, fn, inc in self.q[e]:
                    for (sem, val) in waits:
                        eng.wait_ge(sem, val)
                    if fn is None:
                        continue
                    ins = fn(eng)
                    if inc is not None:
                        if inc[1] is None:
                            ins.then_inc(inc[0])
                        else:
                            ins.then_inc(inc[0], inc[1])
            return body
        with nc.Block() as block:
            block.tensor(mk("pe"))
            block.scalar(mk("act"))
            block.vector(mk("dve"))
            block.gpsimd(mk("pool"))
            block.sync(mk("sp"))
        self.es.close()
        return nc


def _rope_tables():
    pos = np.arange(SEQ)
    row = (pos // 64).astype(np.float32)
    col = (pos % 64).astype(np.float32)
    inv = np.power(np.float32(10000.0), -np.arange(16, dtype=np.float32) / np.float32(16)).astype(np.float32)
    ang = np.stack([row[:, None] * inv, col[:, None] * inv], axis=1).astype(np.float32)
    return np.cos(ang).astype(np.float32).reshape(SEQ, 32), np.sin(ang).astype(np.float32).reshape(SEQ, 32)


def _fourier_consts():
    c = np.arange(64)
    ang = 2 * np.pi * np.outer(c, c) / 64.0
    cc = np.zeros((256, 512), np.float64)
    for g in range(4):
        cc[g * 64:(g + 1) * 64, g * 64:(g + 1) * 64] = np.cos(ang)
        cc[g * 64:(g + 1) * 64, 256 + g * 64:256 + (g + 1) * 64] = np.sin(ang)
    l1 = np.arange(64)
    m1 = np.zeros((128, 128, 128), np.float64)
    for l2 in range(128):
        ph = 2 * np.pi * (l2 * l1[:, None] / 8192.0 + np.outer(l1, l1) / 64.0)
        mr, mi = np.cos(ph), np.sin(ph)
        mfull = np.block([[mr, -mi], [mi, mr]])
        perm = np.array([pt * 64 + (16 * rk + 4 * c_ + q_) for pt in range(2) for c_ in range(4) for rk in range(4) for q_ in range(4)])
        m1[l2] = mfull.T[perm, :]
    l2 = np.arange(128)
    ph3 = 2 * np.pi * np.outer(l2, l2) / 128.0
    sc = 1.0 / math.sqrt(8192.0 * 64.0)
    w3re = (np.cos(ph3) * sc).T
    w3im = (-np.sin(ph3) * sc).T
    lc = np.arange(256)
    phc = 2 * np.pi * np.outer(lc, lc) / 256.0
    scc = 1.0 / math.sqrt(256.0 * 64.0)
    cosc = (np.cos(phc) * scc).T
    sinc = (-np.sin(phc) * scc).T
    return cc.astype(np.float32), m1, w3re, w3im, cosc, sinc


def _bf(a):
    return np.asarray(a, dtype=np.float32).astype(ml_dtypes.bfloat16)


class K:
    pass


def _pieces(start, n):
    out = []
    f = start
    while f < start + n:
        r = f // 768
        o = f % 768
        ln = min(768 - o, start + n - f)
        out.append((r, o, f - start, ln))
        f += ln
    return out


def build(stage=99, dbg=False):
    P = Prog()
    nc = P.nc
    S = K()
    S.P = P
    ein = lambda name, shape, dt=F32: P.dram(name, shape, dt, kind="ExternalInput")

    xin = ein("xin", [ROWS, D])
    cT_d = ein("cT", [128, 8, 2])
    wmod_d = ein("wmod", [2, D, 1536])
    bmod_d = ein("bmod", [2, 1536])
    vec_d = ein("vecs", [2, 6, D])
    b1T_d = ein("b1T", [2, 128, 32])
    win0_d = ein("win0", [D, 2560])
    wfT_d = ein("wfT", [256, D])
    cc_d = ein("ccs", [256, 512])
    wout_d = ein("wout", [2, D, D])
    import os
    KSMALL = 'K_SMALL' in os.environ
    w1_d = ein("w1", [2, D, 4096]) if not KSMALL else None
    w2_d = ein("w2", [2, 4096, D]) if not KSMALL else None
    wcd_d = ein("wcd", [D, 1536])
    s5lam_d = ein("s5lam", [128, 2, 64])
    s5dt_d = ein("s5dt", [128, 64])
    s5bT_d = ein("s5bT", [128, 2, 2, 4, 64])
    s5cT_d = ein("s5cT", [128, 2, 32, 16])
    s5cst_d = ein("s5cst", [128, 140])
    s5d_d = ein("s5d", [128, 4])
    wglu_d = ein("wglu", [512, 512])
    bgluT_d = ein("bgluT", [128, 4])
    wspT_d = ein("wspT", [4, 128, 128])
    bspT_d = ein("bspT", [128, 4])
    rope_d = ein("rope", [TPC, 64])
    ident_d = ein("ident", [128, 128])
    lamv_d = ein("lamv", [4, 64])
    subg_d = ein("subg", [1, 128])
    m1_d = ein("m1c", [128, 128, 128], BF16)
    w3_d = ein("w3c", [128, 64], BF16)
    dftc_d = ein("dftc", [256, 512], BF16)
    yout = P.dram("yout", [TPC, D], F32, kind="ExternalOutput")
    if dbg:
        S.dbg = {}

    xres = P.dram("xres", [ROWS, D], F32)
    ymix = P.dram("ymix", [ROWS, D], BF16)
    S.b_xres = P.bufs(NTT, "xres")
    S.b_ymix = P.bufs(NTT, "ymix")
    cc_mod_in = P.dram("cc_mod_in", [2, 3072], F32)
    cc_mod_out = P.dram("cc_mod_out", [8, 3072], F32)
    b_ccmi, b_ccmo = P.buf(), P.buf()
    wout_b = P.dram("wout_b", [2, D, D], BF16)
    w1_b = P.dram("w1_b", [2, D, 4096], BF16)
    w2_b = P.dram("w2_b", [2, 4096, D], BF16)
    b_woutb, b_w1b, b_w2b = P.bufs(2, "woutb"), P.bufs(2, "w1b"), P.bufs(2, "w2b")

    ident_f = P.sb("ident_f", [128, 128], F32)
    ident_b = P.sb("ident_b", [128, 128], BF16)
    b_ident = P.buf()
    epsc = P.sb("epsc", [128, 1], F32)
    b_eps = P.buf()
    P.op("pool", lambda e: e.memset(epsc[:, :], LN_EPS), writes=[b_eps])
    P.dma("sp", ident_f[:, :], ident_d[:, :], writes=[b_ident])
    P.op("dve", lambda e: e.tensor_copy(out=ident_b[:, :], in_=ident_f[:, :]), reads=[b_ident], writes=[b_ident])

    psum_all = P.ps("psum_all", [128, 4096], F32)
    pb = [psum_all[:, 512 * i:512 * (i + 1)] for i in range(8)]
    S.psum_all = psum_all
    b_pb = P.bufs(8, "pb")
    for b_ in b_pb:
        b_.excl = True
    S.pb, S.b_pb = pb, b_pb

    ARENA_EL = 49152
    arena = P.sb("arena", [128, ARENA_EL], BF16)
    S.arena = arena

    def av(off, shape, dt=BF16):
        n = 1
        for s_ in shape[1:]:
            n *= s_
        el = n * (2 if dt == F32 else 1)
        assert off + el <= ARENA_EL, (off, el)
        ap = arena[0:shape[0], off:off + el]
        if dt == F32:
            ap = ap.bitcast(F32)
        if len(shape) == 3:
            ap = ap.rearrange("p (a b) -> p a b", a=shape[1])
        elif len(shape) == 4:
            ap = ap.rearrange("p (a b c) -> p a b c", a=shape[1], b=shape[2])
        return ap
    S.av = av

    for l in range(2):
        P.dma("pool", wout_b[l, :, :], wout_d[l, :, :], writes=[b_woutb[l]])
    for l in range(0 if KSMALL else 2):
        for hh in range(4):
            P.dma("pool", w1_b[l, :, hh * 1024:(hh + 1) * 1024], w1_d[l, :, hh * 1024:(hh + 1) * 1024], writes=[b_w1b[l]])
            P.dma("pool", w2_b[l, hh * 1024:(hh + 1) * 1024, :], w2_d[l, hh * 1024:(hh + 1) * 1024, :], writes=[b_w2b[l]])

    cT = P.sb("cT_s", [128, 8, 2], F32)
    scT = P.sb("scT", [128, 8, 2], F32)
    b_cT = P.buf()
    P.dma("sp", cT[:, :, :], cT_d[:, :, :], writes=[b_cT])
    P.op("act", lambda e: e.activation(out=scT[:, :, :], in_=cT[:, :, :], func=AF.Silu), reads=[b_cT], writes=[b_cT])
    wst = [av(s_ * 3072, [128, 1536], F32) for s_ in range(2)]
    b_wst = P.bufs(2, "wst")
    bm3 = av(6144, [2, 3072], F32)
    mod3 = av(6144 + 6144, [2, 3072], F32)
    b_bm3, b_mod3 = P.buf(), P.buf()
    P.dma("sp", bm3, bmod_d.ap().rearrange("l n -> (l n)").partition_broadcast(2), writes=[b_bm3])
    ci = 0
    for l in range(2):
        for kc in range(8):
            s_ = ci % 2
            ci += 1
            P.dma("sp", wst[s_], wmod_d[l, kc * 128:(kc + 1) * 128, :], writes=[b_wst[s_]])
            for c3 in range(3):
                bank = l * 3 + c3
                P.op("pe", (lambda l, kc, c3, bank, s_: lambda e: e.matmul(
                    psum_all[0:2, 512 * bank:512 * bank + 512], lhsT=scT[:, kc, :], rhs=wst[s_][:, c3 * 512:(c3 + 1) * 512],
                    start=(kc == 0), stop=(kc == 7)))(l, kc, c3, bank, s_),
                    reads=[b_cT, b_wst[s_]], writes=[b_pb[bank]])
    for l in range(2):
        for c3 in range(3):
            bank = l * 3 + c3
            c0 = l * 1536 + c3 * 512
            P.op("dve", (lambda bank, c0: lambda e: e.tensor_tensor(
                out=mod3[:, c0:c0 + 512], in0=psum_all[0:2, 512 * bank:512 * bank + 512], in1=bm3[:, c0:c0 + 512], op=ALU.add))(bank, c0),
                reads=[b_pb[bank], b_bm3], writes=[b_mod3])
    P.dma("sp", cc_mod_in[:, :], mod3, reads=[b_mod3], writes=[b_ccmi])
    P.collective("AllGather", [[0, 1, 2, 3], [4, 5, 6, 7]], cc_mod_in.ap().opt(), cc_mod_out.ap().opt(), reads=[b_ccmi], writes=[b_ccmo])

    modv = P.dram("modv", [2, 2, 6144], F32)
    b_modv = P.buf()
    ccmo = cc_mod_out.ap().rearrange("(r w) n -> r w n", w=2)
    for l in range(2):
        for who in range(2):
            P.dma("sp", modv[who, l, :].rearrange("(r o) -> r o", r=4), ccmo[:, who, l * 1536:(l + 1) * 1536],
                  reads=[b_ccmo], writes=[b_modv])
    modT = P.sb("modT", [128, 2, 2, 48], F32)
    b_modT = P.buf()
    for l in range(2):
        for who in range(2):
            P.dma("sp", modT[:, l, who, :], modv[who, l, :].rearrange("(q p) -> p q", p=128), reads=[b_modv], writes=[b_modT],
                  allow_slow_non_contiguous=True)
    for kind in (1, 4):
        P.op("dve", (lambda kind: lambda e: e.tensor_scalar(
            out=modT[:, :, :, kind * 8:(kind + 1) * 8], in0=modT[:, :, :, kind * 8:(kind + 1) * 8],
            scalar1=1.0, scalar2=None, op0=ALU.add))(kind), reads=[b_modT], writes=[b_modT])
    S.modT, S.b_modT, S.modv, S.b_modv = modT, b_modT, modv, b_modv
    if dbg:
        d = P.dram("dbg_modT", [128, 192], F32, kind="ExternalOutput")
        P.dma("sp", d[:, :], modT[:, :, :, :].rearrange("p a b c -> p (a b c)"), reads=[b_modT])
    if stage == 0:
        return P.finish()

    xt = [P.sb("xt%d" % i, [128, D], F32) for i in range(2)]
    b_xt = P.bufs(2, "xt")
    xn = [P.sb("xn%d" % i, [128, D], BF16) for i in range(2)]
    b_xn = P.bufs(2, "xn")
    stt = [P.sb("stt%d" % i, [128, 16], F32) for i in range(2)]
    b_stt = P.bufs(2, "stt")

    def ln_stats(src, b_src, slot, eng_rs="act"):
        st = stt[slot]
        P.op("dve", lambda e: e.bn_stats(out=st[:, 0:6], in_=src[:, 0:512]), reads=[b_src], writes=[b_stt[slot]])
        P.op("dve", lambda e: e.bn_stats(out=st[:, 6:12], in_=src[:, 512:1024]), reads=[b_src], writes=[b_stt[slot]])
        P.op("dve", lambda e: e.bn_aggr(out=st[:, 12:14], in_=st[:, 0:12]), reads=[b_stt[slot]], writes=[b_stt[slot]])
        P.op("act", lambda e: e.activation(out=st[:, 14:15], in_=st[:, 13:14], func=AF.Sqrt, bias=epsc[:, 0:1], scale=1.0),
             reads=[b_stt[slot], b_eps], writes=[b_stt[slot]])
        P.op("dve", lambda e: e.reciprocal(out=st[:, 14:15], in_=st[:, 14:15]), reads=[b_stt[slot]], writes=[b_stt[slot]])
        return st[:, 12:13], st[:, 14:15]

    def ln_to_hT(t, x_src_ap, b_xsrc, slot, l, kinds, hT_ap_fn, b_hT, tbank):
        who = 0 if t < NT else 1
        P.dma("sp", xt[slot][:, :], x_src_ap, reads=[b_xsrc], writes=[b_xt[slot]])
        mean, rstd = ln_stats(xt[slot], b_xt[slot], slot)
        P.op("dve", lambda e: e.tensor_scalar(out=xn[slot][:, :], in0=xt[slot][:, :], scalar1=mean, scalar2=rstd,
                                              op0=ALU.subtract, op1=ALU.mult),
             reads=[b_xt[slot], b_stt[slot]], writes=[b_xn[slot]])
        ptb = psum_all[:, tbank * 512:tbank * 512 + 1024]
        for kc in range(8):
            P.op("pe", (lambda kc: lambda e: e.matmul(ptb[:, kc * 128:(kc + 1) * 128], lhsT=xn[slot][:, kc * 128:(kc + 1) * 128],
                                                      rhs=ident_b[:, :], start=True, stop=True))(kc),
                 reads=[b_xn[slot], b_ident], writes=[b_pb[tbank + kc // 4]])
        ksh, ksc = kinds
        for kc in range(8):
            sc_ap = modT[:, l, who, ksc * 8 + kc:ksc * 8 + kc + 1]
            sh_ap = modT[:, l, who, ksh * 8 + kc:ksh * 8 + kc + 1]
            if kc % 2 == 0:
                P.op("act", (lambda kc, sc_ap, sh_ap: lambda e: e.activation(
                    out=hT_ap_fn(kc), in_=ptb[:, kc * 128:(kc + 1) * 128], func=AF.Identity, bias=sh_ap, scale=sc_ap))(kc, sc_ap, sh_ap),
                    reads=[b_pb[tbank + kc // 4], b_modT], writes=[b_hT[0]])
            else:
                P.op("dve", (lambda kc, sc_ap, sh_ap: lambda e: e.tensor_scalar(
                    out=hT_ap_fn(kc), in0=ptb[:, kc * 128:(kc + 1) * 128], scalar1=sc_ap, scalar2=sh_ap,
                    op0=ALU.mult, op1=ALU.add))(kc, sc_ap, sh_ap),
                    reads=[b_pb[tbank + kc // 4], b_modT], writes=[b_hT[1]])

    S.ln_stats, S.ln_to_hT = ln_stats, ln_to_hT

    def x_src(t, first_layer):
        return (xin[t * 128:(t + 1) * 128, :] if first_layer else xres[t * 128:(t + 1) * 128, :])

    WIN_N = 2816
    win = arena[:, 0:8 * WIN_N].rearrange("p (k n) -> p k n", k=8)
    b_win = P.buf()
    P.barrier()
    P.dma("pool", win[:, :, 512:WIN_N], win0_d.ap()[:, 256:2560].rearrange("(k p) n -> p k n", p=128), writes=[b_win])
    wfT = av(8 * WIN_N, [128, 2, D], F32)
    ccs = av(8 * WIN_N + 4096, [128, 2, 512], F32)
    b_wfT = P.buf()
    P.dma("sp", wfT, wfT_d.ap().rearrange("(c p) k -> p c k", p=128), writes=[b_wfT])
    P.dma("sp", ccs, cc_d.ap().rearrange("(c p) n -> p c n", p=128), writes=[b_wfT])
    for kc in range(8):
        bank = kc % 4
        for c in range(2):
            P.op("pe", (lambda kc, c, bank: lambda e: e.matmul(pb[bank], lhsT=wfT[:, c, kc * 128:(kc + 1) * 128], rhs=ccs[:, c, :],
                                                               start=(c == 0), stop=(c == 1)))(kc, c, bank),
                 reads=[b_wfT], writes=[b_pb[bank]])
        P.op("act" if kc % 2 else "dve",
             (lambda kc, bank: (lambda e: e.copy(out=win[:, kc, 0:512], in_=pb[bank])) if kc % 2 else
              (lambda e: e.tensor_copy(out=win[:, kc, 0:512], in_=pb[bank])))(kc, bank),
             reads=[b_pb[bank]], writes=[b_win])

    import os
    KCUT = int(os.environ.get('K_CUT', '99'))
    if KCUT == 1:
        return P.finish()
    cc_f_in = [P.dram("cc_f_in%d" % c, [512, 512], BF16) for c in range(4)]
    cc_f_out = [P.dram("cc_f_out%d" % c, [2048, 512], BF16) for c in range(4)]
    cc_k_in = [P.dram("cc_k_in%d" % h, [128, TPC], BF16) for h in range(6)]
    cc_k_out = [P.dram("cc_k_out%d" % h, [512, TPC], BF16) for h in range(6)]
    ccval_i = [P.dram("ccval_i%d" % h, [TPC, 128], BF16) for h in range(6)]
    ccval_o = [P.dram("ccval_o%d" % h, [4 * TPC, 128], BF16) for h in range(6)]
    b_ccf_in, b_ccf_out = P.bufs(4, "ccfi"), P.bufs(4, "ccfo")
    b_cck_in, b_cck_out = P.bufs(6, "ccki"), P.bufs(6, "ccko")
    b_ccvali, b_ccvalo = P.bufs(6, "ccvi"), P.bufs(6, "ccvo")
    G4 = [[0, 1, 2, 3], [4, 5, 6, 7]]

    rope_sb = P.sb("rope_sb", [128, NT, 64], F32)
    b_rope = P.buf()
    P.dma("sp", rope_sb[:, :, :], rope_d.ap().rearrange("(t p) c -> p t c", p=128), writes=[b_rope])
    qT_all = P.sb("qT_all", [128, 6, ROWS], BF16)
    b_qT = P.buf()
    kTc = P.sb("kTc", [128, 6, 256], BF16)
    vc = P.sb("vc", [128, 2, 6, 129], BF16)
    abc = P.sb("abc", [128, 2, 512], BF16)
    b_kTc, b_vc, b_abc = P.buf(), P.buf(), P.buf()
    P.op("pool", lambda e: e.memset(vc[:, :, :, 128:129], 1.0), writes=[b_vc])
    hT1 = [P.sb("hT1_%d" % i, [128, 8, 128], BF16) for i in range(2)]
    b_hT1 = [P.bufs(2, "hT1_%d_" % i) for i in range(2)]
    zqk = av(28672, [128, 1536], F32)
    b_zqk = P.buf()
    rtmp = [av(28672 + 3072 + i * 1536, [128, 768], F32) for i in range(4)]
    b_rtmp = P.bufs(4, "rtmp")
    qkb = P.sb("qkb", [128, 1536], BF16)
    b_qkb = P.buf()
    ab_st = [P.sb("ab_st%d" % i, [128, 512], BF16) for i in range(2)]
    v_st = [P.sb("v_st%d" % i, [128, 768], BF16) for i in range(2)]
    kT_st = [P.sb("kT_st%d" % i, [128, 6, 128], BF16) for i in range(2)]
    b_ab, b_v, b_kTst = P.bufs(2, "ab"), P.bufs(2, "vst"), P.bufs(2, "kTst")

    import os
    def l0_tile(t):
        slot = t % 2
        main = t < NT
        ln_to_hT(t, x_src(t, True), Buf("xin"), slot, 0, (0, 1), (lambda kc, slot=slot: hT1[slot][:, kc, :]), b_hT1[slot], 6)
        if KCUT == 2:
            return P.finish()
        for cg in range(6):
            n0 = cg * 512
            n1 = min(WIN_N, n0 + 512)
            for kc in range(8):
                P.op("pe", (lambda cg, kc, n0, n1: lambda e: e.matmul(
                    psum_all[:, n0:n1], lhsT=hT1[slot][:, kc, :], rhs=win[:, kc, n0:n1], start=(kc == 0), stop=(kc == 7)))(cg, kc, n0, n1),
                    reads=b_hT1[slot] + [b_win], writes=[b_pb[cg]])
        if KCUT == 3:
            return P.finish()
        if main:
            P.op("act", lambda e, slot=slot: e.copy(out=ab_st[slot][:, :], in_=pb[0]), reads=[b_pb[0]], writes=[b_ab[slot]])
            if 'a' not in os.environ.get('K_NODMA', ''):
                P.dma("sp", cc_f_in[t // 4][(t % 4) * 128:(t % 4 + 1) * 128, :], ab_st[slot][:, :], reads=[b_ab[slot]], writes=[b_ccf_in[t // 4]])
        else:
            P.op("act", lambda e, t=t: e.copy(out=abc[:, t - NT, :], in_=pb[0]), reads=[b_pb[0]], writes=[b_abc])
        if main:
            for bk in range(3):
                P.op("act", lambda e, bk=bk: e.copy(out=zqk[:, bk * 512:(bk + 1) * 512], in_=pb[1 + bk]),
                     reads=[b_pb[1 + bk]], writes=[b_zqk])
            zv = zqk.rearrange("p (u a h f) -> p u a h f", u=24, a=2, h=2)
            ov = qkb[:, :].rearrange("p (u a h f) -> p u a h f", u=24, a=2, h=2)
            t1, t2 = zv[:, :, :, 0, :], zv[:, :, :, 1, :]
            cs = rope_sb[:, t, 0:32].rearrange("p (a f) -> p a f", a=2).unsqueeze(1).broadcast_to([128, 24, 2, 16])
            sn = rope_sb[:, t, 32:64].rearrange("p (a f) -> p a f", a=2).unsqueeze(1).broadcast_to([128, 24, 2, 16])
            rv = [r.rearrange("p (u a f) -> p u a f", u=24, a=2) for r in rtmp]
            P.op("dve", lambda e: e.tensor_tensor(out=rv[0], in0=t1, in1=cs, op=ALU.mult), reads=[b_zqk, b_rope], writes=[b_rtmp[0]])
            P.op("pool", lambda e: e.tensor_tensor(out=rv[1], in0=t2, in1=sn, op=ALU.mult), reads=[b_zqk, b_rope], writes=[b_rtmp[1]])
            P.op("dve", lambda e: e.tensor_tensor(out=ov[:, :, :, 0, :], in0=rv[0], in1=rv[1], op=ALU.subtract),
                 reads=[b_rtmp[0], b_rtmp[1]], writes=[b_qkb])
            P.op("pool", lambda e: e.tensor_tensor(out=rv[2], in0=t2, in1=cs, op=ALU.mult), reads=[b_zqk, b_rope], writes=[b_rtmp[2]])
            P.op("dve", lambda e: e.tensor_tensor(out=rv[3], in0=t1, in1=sn, op=ALU.mult), reads=[b_zqk, b_rope], writes=[b_rtmp[3]])
            P.op("pool", lambda e: e.tensor_tensor(out=ov[:, :, :, 1, :], in0=rv[2], in1=rv[3], op=ALU.add),
                 reads=[b_rtmp[2], b_rtmp[3]], writes=[b_qkb])
        else:
            for bk in range(3):
                P.op("act" if bk % 2 else "dve",
                     (lambda bk: (lambda e: e.copy(out=qkb[:, bk * 512:(bk + 1) * 512], in_=pb[1 + bk])) if bk % 2 else
                      (lambda e: e.tensor_copy(out=qkb[:, bk * 512:(bk + 1) * 512], in_=pb[1 + bk])))(bk),
                     reads=[b_pb[1 + bk]], writes=[b_qkb])
        if KCUT == 4:
            return P.finish()
        if main:
            P.op("dve", lambda e, slot=slot: e.tensor_copy(out=v_st[slot][:, 0:512], in_=pb[4]), reads=[b_pb[4]], writes=[b_v[slot]])
            P.op("act", lambda e, slot=slot: e.copy(out=v_st[slot][:, 512:768], in_=psum_all[:, 2560:2816]), reads=[b_pb[5]], writes=[b_v[slot]])
            if 'v' not in os.environ.get('K_NODMA', ''):
                for h_ in range(6):
                    P.dma("sp", ccval_i[h_][t * 128:(t + 1) * 128, :], v_st[slot][:, h_ * 128:(h_ + 1) * 128], reads=[b_v[slot]], writes=[b_ccvali[h_]])
        else:
            ci = t - NT
            P.op("dve", lambda e, ci=ci: e.tensor_copy(out=vc[:, ci, 0:4, 0:128], in_=pb[4].rearrange("p (h e) -> p h e", h=4)),
                 reads=[b_pb[4]], writes=[b_vc])
            P.op("act", lambda e, ci=ci: e.copy(out=vc[:, ci, 4:6, 0:128], in_=psum_all[:, 2560:2816].rearrange("p (h e) -> p h e", h=2)),
                 reads=[b_pb[5]], writes=[b_vc])
        if KCUT == 5:
            return P.finish()
        if KCUT == 7:
            return None
        pq = psum_all[:, 3072:4096]
        for u in range(8):
            P.op("pe", (lambda u: lambda e: e.matmul(pq[:, u * 128:(u + 1) * 128], lhsT=qkb[:, u * 128:(u + 1) * 128], rhs=ident_b[:, :],
                                                     start=True, stop=True))(u),
                 reads=[b_qkb, b_ident], writes=[b_pb[6 + u // 4]])
        P.op("act", lambda e: e.copy(out=qT_all[:, :, t * 128:(t + 1) * 128], in_=pq[:, 0:768].rearrange("p (h k) -> p h k", h=6)),
             reads=[b_pb[6], b_pb[7]], writes=[b_qT])
        kdst = kT_st[slot][:, :, :] if main else kTc[:, :, (t - NT) * 128:(t - NT + 1) * 128]
        b_kd = b_kTst[slot] if main else b_kTc
        P.op("dve", lambda e: e.tensor_copy(out=kdst[:, 0:2, :], in_=pq[:, 768:1024].rearrange("p (h k) -> p h k", h=2)),
             reads=[b_pb[7]], writes=[b_kd])
        for u in range(8, 12):
            P.op("pe", (lambda u: lambda e: e.matmul(pq[:, (u - 8) * 128:(u - 7) * 128], lhsT=qkb[:, u * 128:(u + 1) * 128], rhs=ident_b[:, :],
                                                     start=True, stop=True))(u),
                 reads=[b_qkb, b_ident], writes=[b_pb[6]])
        P.op("dve", lambda e: e.tensor_copy(out=kdst[:, 2:6, :], in_=pq[:, 0:512].rearrange("p (h k) -> p h k", h=4)),
             reads=[b_pb[6]], writes=[b_kd])
        if main and 'k' not in os.environ.get('K_NODMA', ''):
            for h_ in range(6):
                P.dma("sp", cc_k_in[h_][:, t * 128:(t + 1) * 128], kT_st[slot][:, h_, :], reads=[b_kTst[slot]], writes=[b_cck_in[h_]])
        return None
    for t_ in ([int(v) for v in os.environ['K_TILES'].split(',')] if 'K_TILES' in os.environ else range(NTT)):
        r_ = l0_tile(t_)
        if r_ is not None:
            return r_
    if 'K_NOCC' in os.environ:
        return P.finish()
    for h_ in range(6):
        P.collective("AllGather", G4, cc_k_in[h_].ap().opt(), cc_k_out[h_].ap().opt(), reads=[b_cck_in[h_]], writes=[b_cck_out[h_]])
        P.collective("AllGather", G4, ccval_i[h_].ap().opt(), ccval_o[h_].ap().opt(), reads=[b_ccvali[h_]], writes=[b_ccvalo[h_]])
    for c_ in range(4):
        P.collective("AllGather", G4, cc_f_in[c_].ap().opt(), cc_f_out[c_].ap().opt(), reads=[b_ccf_in[c_]], writes=[b_ccf_out[c_]])
    if dbg:
        d = P.dram("dbg_qT", [128, 6 * ROWS], BF16, kind="ExternalOutput")
        P.dma("sp", d[:, :], qT_all[:, :, :].rearrange("p h t -> p (h t)"), reads=[b_qT])
        d2 = P.dram("dbg_k", [512, TPC], BF16, kind="ExternalOutput")
        P.dma("sp", d2[:, :], cc_k_out[3][:, :], reads=[b_cck_out[3]])
        d3 = P.dram("dbg_v", [4 * TPC, 128], BF16, kind="ExternalOutput")
        P.dma("sp", d3[:, :], ccval_o[2][:, :], reads=[b_ccvalo[2]])
        d4 = P.dram("dbg_f", [2048, 512], BF16, kind="ExternalOutput")
        P.dma("sp", d4[:, :], cc_f_out[1][:, :], reads=[b_ccf_out[1]])
    if stage == 1:
        return P.finish()
    P.barrier()
    SLOT_EL = 8448 + 66 * 129
    kT_h = [av(s * SLOT_EL, [128, 8448]) for s in range(2)]
    Vp = [av(s * SLOT_EL + 8448, [128, 66, 129]) for s in range(2)]
    PT = [av(2 * SLOT_EL + s * 1024, [128, 1024]) for s in range(2)]
    b_kT, b_Vp, b_PT = P.bufs(2, "kTh"), P.bufs(2, "Vp"), P.bufs(2, "PT")
    for s in range(2):
        P.op("pool", lambda e, s=s: e.memset(Vp[s][:, :, 128:129], 1.0), writes=[b_Vp[s]])
    lamb = P.sb("lamb", [128, 4, 64], F32)
    lsm = P.sb("lsm", [128, 8], F32)
    gsub = P.sb("gsub", [128, 128], F32)
    b_lam = P.buf()
    P.dma("sp", lamb[:, :, :].rearrange("p a b -> p (a b)"), lamv_d.ap().rearrange("a b -> (a b)").partition_broadcast(128), writes=[b_lam])
    P.dma("sp", gsub[:, :], subg_d.ap().rearrange("a b -> (a b)").partition_broadcast(128), writes=[b_lam])
    for i in range(2):
        P.op("dve", lambda e, i=i: e.tensor_tensor(out=lamb[:, 2 * i, :], in0=lamb[:, 2 * i, :], in1=lamb[:, 2 * i + 1, :], op=ALU.mult),
             reads=[b_lam], writes=[b_lam])
        P.op("dve", lambda e, i=i: e.reduce_sum(out=lsm[:, i:i + 1], in_=lamb[:, 2 * i, :], axis=AX.X), reads=[b_lam], writes=[b_lam])
    P.op("act", lambda e: e.activation(out=lsm[:, 2:4], in_=lsm[:, 0:2], func=AF.Exp), reads=[b_lam], writes=[b_lam])
    LAM_INIT = 0.8 - 0.6 * math.exp(-0.3 * 0)
    P.op("dve", lambda e: e.tensor_tensor(out=lsm[:, 4:5], in0=lsm[:, 3:4], in1=lsm[:, 2:3], op=ALU.subtract), reads=[b_lam], writes=[b_lam])
    P.op("dve", lambda e: e.tensor_scalar(out=lsm[:, 4:5], in0=lsm[:, 4:5], scalar1=-LAM_INIT, scalar2=None, op0=ALU.add), reads=[b_lam], writes=[b_lam])
    P.op("dve", lambda e: e.tensor_scalar(out=gsub[:, :], in0=gsub[:, :], scalar1=1.0 - LAM_INIT, scalar2=None, op0=ALU.mult), reads=[b_lam], writes=[b_lam])
    neglam = lsm[:, 4:5]
    ep_r = [P.sb("ep_r%d" % i, [128, 8], F32) for i in range(2)]
    ep_o = [P.sb("ep_o%d" % i, [128, 128], F32) for i in range(2)]
    ep_j = [P.sb("ep_j%d" % i, [128, 128], F32) for i in range(2)]
    b_ep = P.bufs(2, "ep")
    yst = [P.sb("yst%d" % i, [128, 4, 128], BF16) for i in range(2)]
    b_yst = P.bufs(2, "yst")
    epi = 0
    ATT_H = int(os.environ.get('K_HEADS', '6'))
    def load_head(h, s):
        for j in range(4):
            P.dma("sp", kT_h[s][:, 256 + 2048 * j:256 + 2048 * (j + 1)], cc_k_out[h][j * 128:(j + 1) * 128, :],
                  reads=[b_cck_out[h]], writes=[b_kT[s]])
            P.dma("sp", Vp[s][:, 2 + 16 * j:2 + 16 * (j + 1), 0:128],
                  ccval_o[h][j * 2048:(j + 1) * 2048, :].rearrange("(kb p) e -> p kb e", p=128),
                  reads=[b_ccvalo[h]], writes=[b_Vp[s]])
        P.op("pool", lambda e: e.tensor_copy(out=kT_h[s][:, 0:256], in_=kTc[:, h, :]), reads=[b_kTc], writes=[b_kT[s]])
        P.op("pool", lambda e: e.tensor_copy(out=Vp[s][:, 0:2, 0:128], in_=vc[:, :, h, 0:128]), reads=[b_vc], writes=[b_Vp[s]])

    def att_S(h, s, q0, nq, i, kb):
        sp_i = i % 2
        for m in range(2):
            P.op("pe", (lambda m: lambda e: e.matmul(
                psum_all[:, (2 * sp_i + m) * 512:(2 * sp_i + m) * 512 + nq],
                lhsT=kT_h[s][64 * m:64 * m + 64, kb * 128:(kb + 1) * 128],
                rhs=qT_all[64 * m:64 * m + 64, h, q0:q0 + nq], start=True, stop=True))(m),
                reads=[b_kT[s], b_qT], writes=[b_pb[2 * sp_i + m]])
        P.op("act", lambda e: e.activation(
            out=PT[sp_i][:, :].rearrange("p (m q) -> p m q", m=2)[:, :, 0:nq],
            in_=psum_all[:, 2 * sp_i * 512:2 * sp_i * 512 + 1024].rearrange("p (m q) -> p m q", m=2)[:, :, 0:nq],
            func=AF.Exp, scale=DIFF_SCALE),
            reads=[b_pb[2 * sp_i], b_pb[2 * sp_i + 1]], writes=[b_PT[sp_i]])

    def att_PV(h, s, nqb, i, kb, last):
        sp_i = i % 2
        for qb in range(nqb):
            for m in range(2):
                P.op("pe", (lambda m, qb: lambda e: e.matmul(
                    psum_all[:, (4 + qb) * 512 + m * 129:(4 + qb) * 512 + m * 129 + 129],
                    lhsT=PT[sp_i][:, m * 512 + qb * 128:m * 512 + (qb + 1) * 128],
                    rhs=Vp[s][:, kb, 0:129], start=(i == 0 and m == 0), stop=last,
                    skip_group_check=True))(m, qb),
                    reads=[b_PT[sp_i], b_Vp[s]], writes=[b_pb[4 + qb]])

    def att_epi(h, ys, qb):
        es = qb % 2
        O = psum_all[:, (4 + qb) * 512:(4 + qb) * 512 + 258]
        r = ep_r[es]
        P.op("dve", lambda e: e.reciprocal(out=r[:, 0:2], in_=O.rearrange("p (m c) -> p m c", m=2)[:, :, 128]),
             reads=[b_pb[4 + qb]], writes=[b_ep[es]])
        P.op("dve", lambda e: e.tensor_tensor(out=r[:, 2:3], in0=r[:, 1:2], in1=neglam, op=ALU.mult),
             reads=[b_ep[es], b_lam], writes=[b_ep[es]])
        P.op("dve", lambda e: e.tensor_scalar(out=ep_o[es][:, :], in0=O[:, 0:128], scalar1=r[:, 0:1], scalar2=None, op0=ALU.mult),
             reads=[b_pb[4 + qb], b_ep[es]], writes=[b_ep[es]])
        P.op("dve", lambda e: e.scalar_tensor_tensor(out=ep_o[es][:, :], in0=O[:, 129:257], scalar=r[:, 2:3], in1=ep_o[es][:, :],
                                                     op0=ALU.mult, op1=ALU.add),
             reads=[b_pb[4 + qb], b_ep[es]], writes=[b_ep[es]])
        P.op("act", lambda e: e.activation(out=ep_j[es][:, :], in_=ep_o[es][:, :], func=AF.Square, accum_out=r[:, 3:4]),
             reads=[b_ep[es]], writes=[b_ep[es]])
        P.op("act", lambda e: e.activation(out=r[:, 4:5], in_=r[:, 3:4], func=AF.Sqrt, bias=epsc[:, 0:1], scale=1.0 / 128.0),
             reads=[b_ep[es], b_eps], writes=[b_ep[es]])
        P.op("dve", lambda e: e.reciprocal(out=r[:, 5:6], in_=r[:, 4:5]), reads=[b_ep[es]], writes=[b_ep[es]])
        P.op("dve", lambda e: e.scalar_tensor_tensor(
            out=yst[ys][:, qb, :], in0=ep_o[es][:, :], scalar=r[:, 5:6], in1=gsub[:, :], op0=ALU.mult, op1=ALU.mult),
            reads=[b_ep[es], b_lam], writes=[b_yst[ys]])

    def att_store(h, ys, q0, nq, nqb):
        t0 = q0 // 128
        P.dma("sp", ymix[q0:q0 + nq, 256 + h * 128:256 + (h + 1) * 128].rearrange("(qb p) e -> p qb e", p=128),
              yst[ys][:, 0:nqb, :], reads=[b_yst[ys]], writes=[S.b_ymix[t0 + i_] for i_ in range(nqb)])

    GROUPS = [int(v) for v in os.environ['K_GROUPS'].split(',')] if 'K_GROUPS' in os.environ else list(range(5))
    for h in range(ATT_H):
        s = h % 2
        load_head(h, s)
        for g in GROUPS:
            if g < 4:
                nq, nqb, q0, kbs = 512, 4, g * 512, list(range(66))
            else:
                nq, nqb, q0, kbs = 256, 2, 2048, [0, 1]
            att_S(h, s, q0, nq, 0, kbs[0])
            for i, kb in enumerate(kbs):
                if i + 1 < len(kbs):
                    att_S(h, s, q0, nq, i + 1, kbs[i + 1])
                att_PV(h, s, nqb, i, kb, i == len(kbs) - 1)
            ys = epi % 2
            epi += 1
            for qb in range(nqb):
                att_epi(h, ys, qb)
            att_store(h, ys, q0, nq, nqb)
    if dbg:
        d = P.dram("dbg_ymix", [ROWS, D], BF16, kind="ExternalOutput")
        P.dma("sp", d[:, :], ymix[:, :], reads=S.b_ymix)
    if stage == 2:
        return P.finish()
    P.barrier()
    Zd = P.dram("Zd", [128, 128, 256], BF16)
    b_Zd = P.bufs(4, "Zd")
    w3 = P.sb("w3_s", [128, 64], BF16)
    dftc = P.sb("dftc_s", [128, 2, 512], BF16)
    b_fc = P.buf()
    P.dma("sp", w3[:, :], w3_d[:, :], writes=[b_fc])
    P.dma("sp", dftc[:, :, :], dftc_d.ap().rearrange("(b p) n -> p b n", p=128), writes=[b_fc])
    Gb = [av(s_ * 8192, [128, 32, 256]) for s_ in range(1)]
    M1b = av(8192, [128, 32, 128])
    Zsb = av(8192 + 4096, [128, 32, 256])
    b_Gb, b_M1b, b_Zsb = P.buf(), P.buf(), P.buf()

    def f_stage1(blk):
        for part in range(2):
            for c_ in range(4):
                p0 = part * 64 + c_ * 16
                src = cc_f_out[c_][:, part * 256:(part + 1) * 256].rearrange("(g l) n -> g l n", l=128)[:, blk * 32:(blk + 1) * 32, :]
                P.dma("sp", Gb[0][p0:p0 + 16, :, :], src, reads=[b_ccf_out[c_]], writes=[b_Gb])
        P.dma("sp", M1b, m1_d.ap()[blk * 32:(blk + 1) * 32, :, :].rearrange("l k m -> k l m"), writes=[b_M1b])
        for l2 in range(32):
            bank = (l2 // 2) % 4
            o0 = bank * 512 + (l2 % 2) * 256
            P.op("pe", (lambda l2, o0: lambda e: e.matmul(psum_all[:, o0:o0 + 256], lhsT=M1b[:, l2, :], rhs=Gb[0][:, l2, :],
                                                          start=True, stop=True, skip_group_check=True))(l2, o0),
                 reads=[b_Gb, b_M1b], writes=[b_pb[bank]])
            if l2 % 2 == 1:
                eng = "act" if (l2 // 2) % 2 else "dve"
                P.op(eng, (lambda l2, bank, eng: (lambda e: e.copy(out=Zsb[:, l2 - 1:l2 + 1, :], in_=pb[bank].rearrange("p (a n) -> p a n", a=2)))
                           if eng == "act" else (lambda e: e.tensor_copy(out=Zsb[:, l2 - 1:l2 + 1, :], in_=pb[bank].rearrange("p (a n) -> p a n", a=2))))(l2, bank, eng),
                     reads=[b_pb[bank]], writes=[b_Zsb])
        P.dma("sp", Zd[:, blk * 32:(blk + 1) * 32, :], Zsb, reads=[b_Zsb], writes=[b_Zd[blk]])
    for blk in range(4):
        f_stage1(blk)

    Zt = [av(s_ * 2048, [128, 8, 256]) for s_ in range(2)]
    Ysb = av(4096, [32, 8, 256])
    b_Zt, b_Ysb = P.buf(), P.buf()
    P.barrier()

    def f_stage3(bt):
        for half in range(2):
            P.dma("sp", Zt[half], Zd[half * 64 + bt * 8:half * 64 + (bt + 1) * 8, :, :].rearrange("a l n -> l a n"),
                  reads=b_Zd, writes=[b_Zt])
        for pr in range(4):
            bank = pr % 2
            for half in range(2):
                P.op("pe", (lambda pr, half, bank: lambda e: e.matmul(
                    psum_all[0:32, bank * 512:(bank + 1) * 512], lhsT=w3[:, half * 32:(half + 1) * 32],
                    rhs=Zt[half][:, 2 * pr:2 * pr + 2, :], start=(half == 0), stop=(half == 1)))(pr, half, bank),
                    reads=[b_Zt, b_fc], writes=[b_pb[bank]])
            P.op("act", (lambda pr, bank: lambda e: e.copy(out=Ysb[:, 2 * pr:2 * pr + 2, :],
                                                           in_=psum_all[0:32, bank * 512:(bank + 1) * 512].rearrange("p (a n) -> p a n", a=2)))(pr, bank),
                 reads=[b_pb[bank]], writes=[b_Ysb])
        dst = ymix[0:TPC, 0:256].rearrange("(a b) n -> a b n", b=64)[:, bt * 8:(bt + 1) * 8, :]
        P.dma("sp", dst, Ysb, reads=[b_Ysb], writes=[S.b_ymix[t_] for t_ in range(NT)])
    for bt in range(8):
        f_stage3(bt)
    yc = av(8192, [128, 256])
    b_yc = P.buf()
    for lt in range(2):
        k_ = 0
        for lb in range(2):
            for part in range(2):
                P.op("pe", (lambda lt, lb, part, k_: lambda e: e.matmul(
                    psum_all[:, 1024:1280], lhsT=dftc[:, lb, part * 256 + lt * 128:part * 256 + (lt + 1) * 128],
                    rhs=abc[:, lb, part * 256:(part + 1) * 256], start=(k_ == 0), stop=(k_ == 3)))(lt, lb, part, k_),
                    reads=[b_fc, b_abc], writes=[b_pb[2]])
                k_ += 1
        P.op("dve", lambda e: e.tensor_copy(out=yc, in_=psum_all[:, 1024:1280]), reads=[b_pb[2]], writes=[b_yc])
        P.dma("sp", ymix[TPC + lt * 128:TPC + (lt + 1) * 128, 0:256], yc, reads=[b_yc], writes=[S.b_ymix[NT + lt]])
    if dbg:
        d = P.dram("dbg_yf", [ROWS, 256], BF16, kind="ExternalOutput")
        P.dma("sp", d[:, :], ymix[:, 0:256], reads=S.b_ymix)
    pnb = P.sb("pnb", [128, 5, D], F32)
    b_pnb = P.buf()
    yt = [P.sb("yt%d" % i, [128, D], BF16) for i in range(2)]
    b_yt = P.bufs(2, "yt")
    pt1 = [P.sb("pt1_%d" % i, [128, D], F32) for i in range(2)]
    b_pt1 = P.bufs(2, "pt1")
    b1T = P.sb("b1T_s", [128, 32], F32)
    b_b1T = P.buf()

    def load_pn(l, kind_gate, vi_bias, vi_g, vi_b):
        for who in range(2):
            P.dma("sp", pnb[:, who, :], modv[who, l, kind_gate * 1024:(kind_gate + 1) * 1024].partition_broadcast(128),
                  reads=[b_modv], writes=[b_pnb])
        for k_, vi in enumerate((vi_bias, vi_g, vi_b)):
            P.dma("sp", pnb[:, 2 + k_, :], vec_d[l, vi, :].partition_broadcast(128), writes=[b_pnb])

    def post_norm(t, ypsum, ybanks, x_ap, b_x, out_ap, b_out_list, slot):
        who = 0 if t < NT else 1
        p1 = pt1[slot]
        P.dma("sp", xt[slot][:, :], x_ap, reads=[b_x], writes=[b_xt[slot]])
        P.op("dve", lambda e: e.tensor_tensor(out=p1[:, :], in0=ypsum, in1=pnb[:, 2, :], op=ALU.add),
             reads=[b_pb[ybanks[0]], b_pb[ybanks[1]], b_pnb], writes=[b_pt1[slot]])
        P.op("pool", lambda e: e.tensor_tensor(out=p1[:, :], in0=p1[:, :], in1=pnb[:, who, :], op=ALU.mult),
             reads=[b_pnb], writes=[b_pt1[slot]])
        P.op("dve", lambda e: e.scalar_tensor_tensor(out=p1[:, :], in0=xt[slot][:, :], scalar=ALPHA, in1=p1[:, :],
                                                     op0=ALU.mult, op1=ALU.add),
             reads=[b_xt[slot]], writes=[b_pt1[slot]])
        mean, rstd = ln_stats(p1, b_pt1[slot], slot)
        P.op("dve", lambda e: e.tensor_scalar(out=p1[:, :], in0=p1[:, :], scalar1=mean, scalar2=rstd, op0=ALU.subtract, op1=ALU.mult),
             reads=[b_stt[slot]], writes=[b_pt1[slot]])
        P.op("pool", lambda e: e.tensor_tensor(out=p1[:, :], in0=p1[:, :], in1=pnb[:, 3, :], op=ALU.mult), reads=[b_pnb], writes=[b_pt1[slot]])
        P.op("pool", lambda e: e.tensor_tensor(out=p1[:, :], in0=p1[:, :], in1=pnb[:, 4, :], op=ALU.add), reads=[b_pnb], writes=[b_pt1[slot]])
        P.dma("sp", out_ap, p1[:, :], reads=[b_pt1[slot]], writes=b_out_list)

    def out_proj_phase(l, first_layer):
        P.barrier()
        wout_sb = av(0, [128, 8, D])
        b_wo = P.buf()
        P.dma("sp", wout_sb, wout_b[l, :, :].rearrange("(k p) n -> p k n", p=128), reads=[b_woutb[l]], writes=[b_wo])
        load_pn(l, 2, 0, 1, 2)
        yT = [av(8192 + s_ * 1024, [128, 8, 128]) for s_ in range(2)]
        b_yT = P.bufs(2, "yT")

        def tile(t):
            slot = t % 2
            P.dma("sp", yt[slot][:, :], ymix[t * 128:(t + 1) * 128, :], reads=[S.b_ymix[t]], writes=[b_yt[slot]])
            pq = psum_all[:, 3072:4096]
            for kc in range(8):
                P.op("pe", (lambda kc: lambda e: e.matmul(pq[:, kc * 128:(kc + 1) * 128], lhsT=yt[slot][:, kc * 128:(kc + 1) * 128],
                                                          rhs=ident_b[:, :], start=True, stop=True))(kc),
                     reads=[b_yt[slot], b_ident], writes=[b_pb[6 + kc // 4]])
            P.op("act", lambda e: e.copy(out=yT[slot][:, 0:4, :], in_=pq[:, 0:512].rearrange("p (k t) -> p k t", k=4)),
                 reads=[b_pb[6]], writes=[b_yT[slot]])
            P.op("dve", lambda e: e.tensor_copy(out=yT[slot][:, 4:8, :], in_=pq[:, 512:1024].rearrange("p (k t) -> p k t", k=4)),
                 reads=[b_pb[7]], writes=[b_yT[slot]])
            for hf in range(2):
                for kc in range(8):
                    P.op("pe", (lambda hf, kc: lambda e: e.matmul(pb[hf], lhsT=yT[slot][:, kc, :], rhs=wout_sb[:, kc, hf * 512:(hf + 1) * 512],
                                                                  start=(kc == 0), stop=(kc == 7)))(hf, kc),
                         reads=[b_yT[slot], b_wo], writes=[b_pb[hf]])
            x_ap = x_src(t, first_layer)
            post_norm(t, psum_all[:, 0:1024], (0, 1), x_ap, (Buf("xin") if first_layer else S.b_xres[t]),
                      xres[t * 128:(t + 1) * 128, :], [S.b_xres[t]], slot)
        for t in range(NTT):
            tile(t)

    def ffn_phase(l, last_layer):
        P.barrier()
        w2_sb = av(0, [128, 32, D])
        b_w2 = P.buf()
        for q4 in range(4):
            P.dma("sp", w2_sb[:, q4 * 8:(q4 + 1) * 8, :], w2_b[l, q4 * 1024:(q4 + 1) * 1024, :].rearrange("(k p) n -> p k n", p=128),
                  reads=[b_w2b[l]], writes=[b_w2])
        w1blk = [av(32768 + s_ * 4096, [128, 8, 512]) for s_ in range(2)]
        b_w1blk = P.bufs(2, "w1blk")
        hTg = av(32768 + 8192, [128, 8, 256])
        b_hTg = P.bufs(2, "hTg")
        aT = qT_all[:, :, :].rearrange("p h t -> p (h t)")[:, 0:8192].rearrange("p (c t) -> p c t", c=32)
        b_aT = P.buf()
        P.dma("sp", b1T[:, :], b1T_d[l, :, :], writes=[b_b1T])
        load_pn(l, 5, 3, 4, 5)
        cnt = [0]

        def group(t0):
            for i_ in range(2):
                t = t0 + i_
                ln_to_hT(t, xres[t * 128:(t + 1) * 128, :], S.b_xres[t], t % 2, l, (3, 4),
                         (lambda kc, i_=i_: hTg[:, kc, i_ * 128:(i_ + 1) * 128]), b_hTg, 6)
            for hb in range(8):
                s_ = cnt[0] % 2
                cnt[0] += 1
                P.dma("sp", w1blk[s_], w1_b[l, :, hb * 512:(hb + 1) * 512].rearrange("(k p) n -> p k n", p=128),
                      reads=[b_w1b[l]], writes=[b_w1blk[s_]])
                for hc in range(4):
                    bank = 4 + (hb * 4 + hc) % 2
                    for kc in range(8):
                        P.op("pe", (lambda hc, kc, bank, s_: lambda e: e.matmul(
                            psum_all[:, bank * 512:bank * 512 + 256], lhsT=w1blk[s_][:, kc, hc * 128:(hc + 1) * 128], rhs=hTg[:, kc, :],
                            start=(kc == 0), stop=(kc == 7)))(hc, kc, bank, s_),
                            reads=[b_w1blk[s_]] + b_hTg, writes=[b_pb[bank]])
                    c_ = hb * 4 + hc
                    P.op("act", (lambda c_, bank: lambda e: e.activation(out=aT[:, c_, :], in_=psum_all[:, bank * 512:bank * 512 + 256],
                                                                         func=AF.Relu, bias=b1T[:, c_:c_ + 1], scale=1.0))(c_, bank),
                         reads=[b_pb[bank], b_b1T], writes=[b_aT])
                    P.op("pool", (lambda c_: lambda e: e.tensor_tensor(out=aT[:, c_, :], in0=aT[:, c_, :], in1=aT[:, c_, :], op=ALU.mult))(c_),
                         reads=[b_aT], writes=[b_aT])
            for i_ in range(2):
                t = t0 + i_
                for hf in range(2):
                    bank = 2 * i_ + hf
                    for c_ in range(32):
                        P.op("pe", (lambda c_, hf, bank, i_: lambda e: e.matmul(
                            pb[bank], lhsT=aT[:, c_, i_ * 128:(i_ + 1) * 128], rhs=w2_sb[:, c_, hf * 512:(hf + 1) * 512],
                            start=(c_ == 0), stop=(c_ == 31)))(c_, hf, bank, i_),
                            reads=[b_aT, b_w2], writes=[b_pb[bank]])
                if last_layer and t < NT:
                    out_ap, bl = yout[t * 128:(t + 1) * 128, :], []
                elif last_layer:
                    continue
                else:
                    out_ap, bl = xres[t * 128:(t + 1) * 128, :], [S.b_xres[t]]
                post_norm(t, psum_all[:, 2 * i_ * 512:2 * i_ * 512 + 1024], (2 * i_, 2 * i_ + 1),
                          xres[t * 128:(t + 1) * 128, :], S.b_xres[t], out_ap, bl, t % 2)
        for t0 in range(0, NTT, 2):
            group(t0)

    out_proj_phase(0, True)
    if dbg:
        d = P.dram("dbg_x1a", [ROWS, D], F32, kind="ExternalOutput")
        P.dma("sp", d[:, :], xres[:, :], reads=S.b_xres)
    if stage == 3:
        return P.finish()
    ffn_phase(0, False)
    if dbg:
        d = P.dram("dbg_x1", [ROWS, D], F32, kind="ExternalOutput")
        P.dma("sp", d[:, :], xres[:, :], reads=S.b_xres)
    if stage == 4:
        return P.finish()
    P.barrier()
    wcd = av(0, [128, 8, 1536])
    b_wcd = P.buf()
    P.dma("pool", wcd, wcd_d.ap().rearrange("(k p) n -> p k n", p=128), writes=[b_wcd])
    wspT = av(12288, [128, 4, 128])
    bspT = P.sb("bspT_s", [128, 4], F32)
    b_wsp = P.buf()
    P.dma("pool", wspT, wspT_d.ap().rearrange("g q p -> q g p"), writes=[b_wsp])
    P.dma("sp", bspT[:, :], bspT_d[:, :], writes=[b_wsp])
    hT2 = [av(12288 + 512 + s_ * 1024, [128, 8, 128]) for s_ in range(2)]
    b_hT2 = [P.bufs(2, "hT2_%d_" % i) for i in range(2)]
    vgb = [av(12288 + 512 + 2048 + s_ * 512, [128, 512]) for s_ in range(2)]
    b_vg = P.bufs(2, "vg")
    usb = [P.sb("usb%d" % i, [128, 512], F32) for i in range(1)]
    b_usb = P.buf()
    gst = P.sb("gst", [128, 4, 8], F32)
    b_gst = P.buf()
    yg = [av(12288 + 512 + 2048 + 1024 + s_ * 1024, [128, 1024]) for s_ in range(2)]
    b_yg = P.bufs(2, "yg")

    S.sT = qT_all[:, :, :].rearrange("p h t -> p (h t)")[:, 0:4 * ROWS].rearrange("p (c t) -> p c t", c=4)
    S.b_sT = P.buf()

    def l1_tile(t):
        slot = t % 2
        ln_to_hT(t, xres[t * 128:(t + 1) * 128, :], S.b_xres[t], slot, 1, (0, 1), (lambda kc: hT2[slot][:, kc, :]), b_hT2[slot], 6)
        for ct in range(4):
            for kc in range(8):
                P.op("pe", (lambda ct, kc: lambda e: e.matmul(psum_all[:, 4 * 512 + ct * 128:4 * 512 + (ct + 1) * 128], lhsT=wcd[:, kc, ct * 128:(ct + 1) * 128],
                                                              rhs=hT2[slot][:, kc, :], start=(kc == 0 and ct == 0), stop=(kc == 7),
                                                              skip_group_check=True))(ct, kc),
                     reads=b_hT2[slot] + [b_wcd], writes=[b_pb[4]])
        P.op("act", lambda e: e.copy(out=S.sT[:, :, t * 128:(t + 1) * 128], in_=pb[4].rearrange("p (c t) -> p c t", c=4)),
             reads=[b_pb[4]], writes=[S.b_sT])
        if t >= NT:
            return
        for cg in range(3):
            for kc in range(8):
                P.op("pe", (lambda cg, kc: lambda e: e.matmul(pb[cg], lhsT=hT2[slot][:, kc, :], rhs=wcd[:, kc, cg * 512:(cg + 1) * 512],
                                                              start=(kc == 0), stop=(kc == 7)))(cg, kc),
                     reads=b_hT2[slot] + [b_wcd], writes=[b_pb[cg]])
        for g in range(4):
            vsl = pb[2][:, g * 128:(g + 1) * 128]
            P.op("dve", (lambda g, vsl: lambda e: e.bn_stats(out=gst[:, g, 0:6], in_=vsl))(g, vsl), reads=[b_pb[2]], writes=[b_gst])
            P.op("dve", (lambda g: lambda e: e.bn_aggr(out=gst[:, g, 6:8], in_=gst[:, g, 0:6]))(g), reads=[b_gst], writes=[b_gst])
        P.op("act", lambda e: e.activation(out=gst[:, :, 0], in_=gst[:, :, 7], func=AF.Sqrt, bias=epsc[:, 0:1], scale=1.0),
             reads=[b_gst, b_eps], writes=[b_gst])
        P.op("dve", lambda e: e.reciprocal(out=gst[:, :, 0], in_=gst[:, :, 0]), reads=[b_gst], writes=[b_gst])
        for g in range(4):
            P.op("dve", (lambda g: lambda e: e.tensor_scalar(out=vgb[slot][:, g * 128:(g + 1) * 128], in0=pb[2][:, g * 128:(g + 1) * 128],
                                                             scalar1=gst[:, g, 6:7], scalar2=gst[:, g, 0:1], op0=ALU.subtract, op1=ALU.mult))(g),
                 reads=[b_pb[2], b_gst], writes=[b_vg[slot]])
        for g in range(4):
            P.op("pe", (lambda g: lambda e: e.matmul(pb[3][:, g * 128:(g + 1) * 128], lhsT=wspT[:, g, :], rhs=vgb[slot][:, g * 128:(g + 1) * 128],
                                                     start=(g == 0), stop=True, skip_group_check=True))(g),
                 reads=[b_vg[slot], b_wsp], writes=[b_pb[3]])
        P.op("act", lambda e: e.copy(out=usb[0][:, :], in_=pb[1]), reads=[b_pb[1]], writes=[b_usb])
        for g in range(4):
            P.op("dve", (lambda g: lambda e: e.scalar_tensor_tensor(out=yg[slot][:, 512 + g * 128:512 + (g + 1) * 128], in0=pb[3][:, g * 128:(g + 1) * 128],
                                                                    scalar=bspT[:, g:g + 1], in1=usb[0][:, g * 128:(g + 1) * 128],
                                                                    op0=ALU.add, op1=ALU.mult))(g),
                 reads=[b_pb[3], b_wsp, b_usb], writes=[b_yg[slot]])
        P.op("pool", lambda e: e.memset(yg[slot][:, 0:512], 0.0), writes=[b_yg[slot]])
        P.dma("sp", ymix[t * 128:(t + 1) * 128, :], yg[slot], reads=[b_yg[slot]], writes=[S.b_ymix[t]])
    for t in range(NTT):
        l1_tile(t)
    zt2 = P.sb("zt2", [128, D], BF16)
    b_zt = P.buf()
    P.op("pool", lambda e: e.memset(zt2[:, :], 0.0), writes=[b_zt])
    for t in range(NT, NTT):
        P.dma("sp", ymix[t * 128:(t + 1) * 128, :], zt2[:, :], reads=[b_zt], writes=[S.b_ymix[t]])
    if dbg:
        d = P.dram("dbg_y1", [ROWS, D], BF16, kind="ExternalOutput")
        P.dma("sp", d[:, :], ymix[:, :], reads=S.b_ymix)
    if stage == 5:
        return P.finish()
    def s5_branch():
        P.barrier()
        T = TPC
        o = [0]

        def alloc(shape, dt=F32):
            n = 1
            for s_ in shape[1:]:
                n *= s_
            el = n * (2 if dt == F32 else 1)
            if len(shape) == 5:
                ap = av(o[0], [shape[0], shape[1] * shape[2], shape[3], shape[4]], dt).rearrange("p (a b) c d -> p a b c d", a=shape[1])
            else:
                ap = av(o[0], shape, dt)
            o[0] += el
            return ap
        lamT = alloc([128, 2, 64])
        dtT = alloc([128, 64])
        b_pp = P.buf()
        P.dma("sp", lamT, s5lam_d[:, :, :], writes=[b_pp])
        P.dma("sp", dtT, s5dt_d[:, :], writes=[b_pp])
        wk = [alloc([128, 64]) for _ in range(10)]
        pw = alloc([128, 12, 2, 64])
        b_pw = P.buf()

        def V_(fn, rd=(), wr=()):
            P.op("dve", fn, reads=[b_pp] + list(rd), writes=[b_pp] + list(wr))
        P.op("act", lambda e: e.activation(out=dtT, in_=dtT, func=AF.Exp), reads=[b_pp], writes=[b_pp])
        a_, th, mag, s8, c8, t0_, t1_, den, qr, qi = wk
        lr, li = lamT[:, 0, :], lamT[:, 1, :]
        V_(lambda e: e.tensor_tensor(out=a_, in0=lr, in1=dtT, op=ALU.mult))
        V_(lambda e: e.tensor_tensor(out=th, in0=li, in1=dtT, op=ALU.mult))
        P.op("act", lambda e: e.activation(out=mag, in_=a_, func=AF.Exp), reads=[b_pp], writes=[b_pp])
        P.op("act", lambda e: e.activation(out=s8, in_=th, func=AF.Sin, scale=1.0 / 8.0), reads=[b_pp], writes=[b_pp])
        P.op("act", lambda e: e.activation(out=t0_, in_=th, func=AF.Sin, scale=1.0 / 16.0), reads=[b_pp], writes=[b_pp])
        V_(lambda e: e.tensor_tensor(out=t0_, in0=t0_, in1=t0_, op=ALU.mult))
        V_(lambda e: e.tensor_scalar(out=c8, in0=t0_, scalar1=-2.0, scalar2=1.0, op0=ALU.mult, op1=ALU.add))
        for _ in range(3):
            V_(lambda e: e.tensor_tensor(out=t0_, in0=c8, in1=c8, op=ALU.mult))
            V_(lambda e: e.tensor_tensor(out=t1_, in0=s8, in1=s8, op=ALU.mult))
            V_(lambda e: e.scalar_tensor_tensor(out=s8, in0=s8, scalar=2.0, in1=c8, op0=ALU.mult, op1=ALU.mult))
            V_(lambda e: e.tensor_tensor(out=c8, in0=t0_, in1=t1_, op=ALU.subtract))
        V_(lambda e: e.tensor_tensor(out=pw[:, 0, 0, :], in0=mag, in1=c8, op=ALU.mult), wr=[b_pw])
        V_(lambda e: e.tensor_tensor(out=pw[:, 0, 1, :], in0=mag, in1=s8, op=ALU.mult), wr=[b_pw])
        V_(lambda e: e.tensor_scalar(out=t0_, in0=pw[:, 0, 0, :], scalar1=-1.0, scalar2=None, op0=ALU.add))
        V_(lambda e: e.tensor_tensor(out=den, in0=lr, in1=lr, op=ALU.mult))
        V_(lambda e: e.tensor_tensor(out=t1_, in0=li, in1=li, op=ALU.mult))
        V_(lambda e: e.tensor_tensor(out=den, in0=den, in1=t1_, op=ALU.add))
        V_(lambda e: e.reciprocal(out=den, in_=den))
        V_(lambda e: e.tensor_tensor(out=qr, in0=t0_, in1=lr, op=ALU.mult))
        V_(lambda e: e.tensor_tensor(out=t1_, in0=pw[:, 0, 1, :], in1=li, op=ALU.mult))
        V_(lambda e: e.tensor_tensor(out=qr, in0=qr, in1=t1_, op=ALU.add))
        V_(lambda e: e.tensor_tensor(out=qr, in0=qr, in1=den, op=ALU.mult))
        V_(lambda e: e.tensor_tensor(out=qi, in0=pw[:, 0, 1, :], in1=lr, op=ALU.mult))
        V_(lambda e: e.tensor_tensor(out=t1_, in0=t0_, in1=li, op=ALU.mult))
        V_(lambda e: e.tensor_tensor(out=qi, in0=qi, in1=t1_, op=ALU.subtract))
        V_(lambda e: e.tensor_tensor(out=qi, in0=qi, in1=den, op=ALU.mult))
        for k in range(11):
            V_((lambda k: lambda e: e.tensor_tensor(out=t0_, in0=pw[:, k, 0, :], in1=pw[:, k, 0, :], op=ALU.mult))(k))
            V_((lambda k: lambda e: e.tensor_tensor(out=t1_, in0=pw[:, k, 1, :], in1=pw[:, k, 1, :], op=ALU.mult))(k))
            V_((lambda k: lambda e: e.scalar_tensor_tensor(out=pw[:, k + 1, 1, :], in0=pw[:, k, 0, :], scalar=2.0, in1=pw[:, k, 1, :],
                                                           op0=ALU.mult, op1=ALU.mult))(k), wr=[b_pw])
            V_((lambda k: lambda e: e.tensor_tensor(out=pw[:, k + 1, 0, :], in0=t0_, in1=t1_, op=ALU.subtract))(k), wr=[b_pw])
        qd = P.dram("s5_qd", [2, 8, 8, 64], F32)
        b_qd = P.buf()
        P.dma("sp", qd[0, :, :, :].rearrange("g rc p -> p (g rc)"), qr[0:64, :], reads=[b_pp], writes=[b_qd], allow_slow_non_contiguous=True)
        P.dma("sp", qd[1, :, :, :].rearrange("g rc p -> p (g rc)"), qi[0:64, :], reads=[b_pp], writes=[b_qd], allow_slow_non_contiguous=True)
        o_reuse = o[0]
        BT = alloc([128, 2, 2, 4, 64])
        QB = alloc([128, 2, 2, 4, 64])
        WBf = alloc([128, 2, 4, 128])
        b_B = P.buf()
        P.dma("sp", BT, s5bT_d[:, :, :, :, :], writes=[b_B])
        for ri in range(2):
            for g8 in range(8):
                P.dma("sp", QB[16 * g8:16 * g8 + 16, ri, :, :, :].rearrange("p r c q -> p (r c q)"),
                      qd[ri, g8, :, :].rearrange("rc p -> (rc p)").partition_broadcast(16),
                      reads=[b_qd], writes=[b_B], allow_slow_non_contiguous=True)
        tB = [alloc([128, 2, 4, 64]) for _ in range(2)]

        def B_(fn):
            P.op("dve", fn, reads=[b_B], writes=[b_B])
        WBv = WBf.rearrange("p r c (h q) -> p r c h q", h=2)
        B_(lambda e: e.tensor_tensor(out=tB[0], in0=QB[:, 0], in1=BT[:, 0], op=ALU.mult))
        B_(lambda e: e.tensor_tensor(out=tB[1], in0=QB[:, 1], in1=BT[:, 1], op=ALU.mult))
        B_(lambda e: e.tensor_tensor(out=WBv[:, :, :, 0, :], in0=tB[0], in1=tB[1], op=ALU.subtract))
        B_(lambda e: e.tensor_tensor(out=tB[0], in0=QB[:, 0], in1=BT[:, 1], op=ALU.mult))
        B_(lambda e: e.tensor_tensor(out=tB[1], in0=QB[:, 1], in1=BT[:, 0], op=ALU.mult))
        B_(lambda e: e.tensor_tensor(out=WBv[:, :, :, 1, :], in0=tB[0], in1=tB[1], op=ALU.add))
        CT = alloc([128, 2, 32, 16])
        cst = alloc([128, 128 + 8 + 4 + 4])
        b_C = P.buf()
        P.dma("sp", CT, s5cT_d[:, :, :, :], writes=[b_C])
        P.dma("sp", cst[:, 0:140], s5cst_d[:, :], writes=[b_C])
        P.dma("sp", cst[:, 140:144], s5d_d[:, :], writes=[b_C])
        P.op("dve", lambda e: e.tensor_scalar(out=CT[64:128], in0=CT[64:128], scalar1=-1.0, scalar2=None, op0=ALU.mult), reads=[b_C], writes=[b_C])
        Smat, rowmask, onehot, dcol = cst[:, 0:128], cst[:, 128:136], cst[:, 136:140], cst[:, 140:144]
        AT = alloc([128, 12, 128])
        Bm = alloc([128, 128], BF16)
        CZ = alloc([128, 128])
        X = [alloc([128, T]) for _ in range(2)]
        Vb = alloc([128, T])
        Xc = [Vb[:, 0:256], Vb[:, 256:512]]
        Fall = alloc([128, 64, 2])
        yacc = alloc([128, 4, T])
        b_AT, b_Bm, b_CZ, b_Fall, b_yacc = P.buf(), P.buf(), P.buf(), P.buf(), P.buf()
        b_X = [P.bufs(2, "X%d_" % i) for i in range(2)]
        b_Xc = [P.bufs(2, "Xc%d_" % i) for i in range(2)]
        P.op("pool", lambda e: e.memset(CZ, 0.0), writes=[b_CZ])
        sT = S.sT

        def build_AT(col, ATb=None, b_ATb=None):
            ATb = AT if ATb is None else ATb
            b_ATb = b_AT if b_ATb is None else b_ATb
            for k in range(12):
                P.op("dve", (lambda k: lambda e: e.tensor_scalar(out=ATb[:, k, :], in0=ident_f[:, :], scalar1=pw[:, k, 0, col:col + 1],
                                                                 scalar2=None, op0=ALU.mult))(k), reads=[b_pw, b_ident], writes=[b_ATb])
                P.op("dve", (lambda k: lambda e: e.scalar_tensor_tensor(out=ATb[:, k, :], in0=Smat, scalar=pw[:, k, 1, col:col + 1], in1=ATb[:, k, :],
                                                                        op0=ALU.mult, op1=ALU.add))(k), reads=[b_pw, b_C, b_ATb], writes=[b_ATb])

        def scan_level(Xb, b_Xb, n, r, k, cw, cur, ev):
            sh = 1 << k
            lo_all, hi_all = (sh, n) if r == 0 else (0, n - sh)
            if r == 0:
                ca, cb = 0, min(sh, n)
            else:
                ca, cb = max(n - sh, 0), n
            P.op("act", lambda e: e.copy(out=Xb[1 - cur][:, ca:cb], in_=Xb[cur][:, ca:cb]), reads=b_Xb[cur], writes=[b_Xb[1 - cur][0]])
            for c0 in range(lo_all, hi_all, cw):
                c1 = min(hi_all, c0 + cw)
                bank = 5 + (ev[0] % 2)
                ev[0] += 1
                s0 = c0 - sh if r == 0 else c0 + sh
                P.op("pe", (lambda c0, c1, bank, s0: lambda e: e.matmul(
                    psum_all[:, bank * 512:bank * 512 + c1 - c0], lhsT=AT[:, k, :], rhs=Xb[cur][:, s0:s0 + c1 - c0],
                    start=True, stop=True))(c0, c1, bank, s0),
                    reads=b_Xb[cur] + [b_AT], writes=[b_pb[bank]])
                P.op("dve", (lambda c0, c1, bank: lambda e: e.tensor_tensor(
                    out=Xb[1 - cur][:, c0:c1], in0=psum_all[:, bank * 512:bank * 512 + c1 - c0], in1=Xb[cur][:, c0:c1], op=ALU.add))(c0, c1, bank),
                    reads=[b_pb[bank]] + b_Xb[cur], writes=[b_Xb[1 - cur][1]])

        def scan2(r):
            ev = [0]
            cur, curc = 0, 0
            for k in range(11):
                scan_level(X, b_X, T, r, k, 512, cur, ev)
                cur = 1 - cur
                if k < 8:
                    scan_level(Xc, b_Xc, 256, r, k, 256, curc, ev)
                    curc = 1 - curc
            return cur, curc

        def drive(dst, b_dst, ct, tok0, n, cw):
            for c0 in range(0, n, cw):
                c1 = min(n, c0 + cw)
                P.op("pe", (lambda c0, c1: lambda e: e.matmul(psum_all[:, 4 * 512:4 * 512 + c1 - c0], lhsT=Bm, rhs=sT[:, ct, tok0 + c0:tok0 + c1],
                                                              start=True, stop=True))(c0, c1), reads=[b_Bm, S.b_sT], writes=[b_pb[4]])
                P.op("act", (lambda c0, c1: lambda e: e.copy(out=dst[:, c0:c1], in_=psum_all[:, 4 * 512:4 * 512 + c1 - c0]))(c0, c1),
                     reads=[b_pb[4]], writes=b_dst)

        def out_contrib(src, b_src, first, last):
            for c in range(4):
                P.op("pe", (lambda c: lambda e: e.matmul(pb[c], lhsT=CZ, rhs=src[:, c * 512:(c + 1) * 512], start=first, stop=last,
                                                         skip_group_check=True))(c), reads=[b_CZ] + list(b_src), writes=[b_pb[c]])

        def set_group(ct, g8, r):
            g = ct * 8 + g8
            col = g8 * 8 + r * 4 + ct
            build_AT(col)
            P.op("dve", lambda e: e.tensor_scalar(out=Bm, in0=WBf[:, r, ct, :], scalar1=rowmask[:, g8:g8 + 1], scalar2=None, op0=ALU.mult),
                 reads=[b_B, b_C], writes=[b_Bm])
            P.op("act", lambda e: e.copy(out=CZ[:, 16 * g8:16 * g8 + 16], in_=CT[:, r, g, :]), reads=[b_C], writes=[b_CZ])
            return g, col

        def clear_group(g8):
            P.op("pool", lambda e: e.memset(CZ[:, 16 * g8:16 * g8 + 16], 0.0), writes=[b_CZ])

        for ct in range(4):
            n_acc = 0
            for g8 in range(8):
                for r in range(2):
                    g, col = set_group(ct, g8, r)
                    drive(X[0], b_X[0], ct, 0, T, 512)
                    drive(Xc[0], b_Xc[0], ct, T, 256, 256)
                    cur, curc = scan2(r)
                    fcol = T - 1 if r == 0 else 0
                    fcc = 255 if r == 0 else 0
                    P.op("act", (lambda cur, fcol, col: lambda e: e.copy(out=Fall[:, col, 0:1], in_=X[cur][:, fcol:fcol + 1]))(cur, fcol, col),
                         reads=b_X[cur], writes=[b_Fall])
                    P.op("act", (lambda curc, fcc, col: lambda e: e.copy(out=Fall[:, col, 1:2], in_=Xc[curc][:, fcc:fcc + 1]))(curc, fcc, col),
                         reads=b_Xc[curc], writes=[b_Fall])
                    out_contrib(X[cur], b_X[cur], n_acc == 0, n_acc == 15)
                    n_acc += 1
                clear_group(g8)
            for c in range(4):
                P.op("dve" if c % 2 else "act",
                     (lambda c, ct: (lambda e: e.tensor_copy(out=yacc[:, ct, c * 512:(c + 1) * 512], in_=pb[c])) if c % 2 else
                      (lambda e: e.copy(out=yacc[:, ct, c * 512:(c + 1) * 512], in_=pb[c])))(c, ct),
                     reads=[b_pb[c]], writes=[b_yacc])
        ccs_i = P.dram("ccs5_i", [128, 64], F32)
        ccs_o = P.dram("ccs5_o", [512, 64], F32)
        b_ci, b_co = P.buf(), P.buf()
        P.dma("sp", ccs_i[:, :], Fall[:, :, 0], reads=[b_Fall], writes=[b_ci], allow_slow_non_contiguous=True)
        P.collective("AllGather", G4, ccs_i.ap().opt(), ccs_o.ap().opt(), reads=[b_ci], writes=[b_co])
        Fg = alloc([128, 4, 64])
        b_Fg = P.buf()
        P.dma("sp", Fg, ccs_o.ap().rearrange("(k p) n -> p k n", p=128), reads=[b_co], writes=[b_Fg])
        Sk = alloc([128, 8])
        b_Sk, b_Vb = P.buf(), P.buf()

        P.barrier()
        AT2 = X[1][:, 0:1536].rearrange("p (k m) -> p k m", k=12)
        CZ2 = X[1][:, 1536:1664]
        Sk2 = X[1][:, 1664:1672]
        b_AT2, b_CZ2, b_Sk2, b_V1 = P.buf(), P.buf(), P.buf(), P.buf()
        P.op("pool", lambda e: e.memset(CZ2, 0.0), writes=[b_CZ2])
        RB = [dict(AT=AT, bAT=b_AT, CZ=CZ, bCZ=b_CZ, Sk=Sk, bSk=b_Sk, V=Vb, bV=b_Vb, bank0=4, banks=(5,)),
              dict(AT=AT2, bAT=b_AT2, CZ=CZ2, bCZ=b_CZ2, Sk=Sk2, bSk=b_Sk2, V=X[0], bV=b_V1, bank0=7, banks=(6,))]

        def p2_setup(ct, g8, r):
            R_ = RB[r]
            g = ct * 8 + g8
            col = g8 * 8 + r * 4 + ct
            build_AT(col, R_["AT"], R_["bAT"])
            P.op("act", lambda e: e.copy(out=R_["CZ"][:, 16 * g8:16 * g8 + 16], in_=CT[:, r, g, :]), reads=[b_C], writes=[R_["bCZ"]])
            Skr, bSk, ATr, bAT, bk = R_["Sk"], R_["bSk"], R_["AT"], R_["bAT"], R_["bank0"]
            order = [0, 1, 2, 3] if r == 0 else [3, 2, 1, 0]
            P.op("dve", lambda e: e.tensor_copy(out=Skr[:, order[0]:order[0] + 1], in_=Fall[:, col, 1:2]), reads=[b_Fall], writes=[bSk])
            for a_i in range(3):
                kp, kn = order[a_i], order[a_i + 1]
                P.op("pe", (lambda kp: lambda e: e.matmul(psum_all[:, bk * 512:bk * 512 + 1], lhsT=ATr[:, 11, :], rhs=Skr[:, kp:kp + 1], start=True, stop=True))(kp),
                     reads=[bAT, bSk], writes=[b_pb[bk]])
                P.op("dve", (lambda kp, kn: lambda e: e.tensor_tensor(out=Skr[:, kn:kn + 1], in0=psum_all[:, bk * 512:bk * 512 + 1],
                                                                      in1=Fg[:, kp, col:col + 1], op=ALU.add))(kp, kn),
                     reads=[b_pb[bk], b_Fg], writes=[bSk])
            P.op("dve", lambda e: e.tensor_tensor(out=Skr[:, 4:8], in0=Skr[:, 0:4], in1=onehot, op=ALU.mult), reads=[bSk, b_C], writes=[bSk])
            P.op("dve", lambda e: e.reduce_sum(out=Skr[:, 4:5], in_=Skr[:, 4:8], axis=AX.X), reads=[bSk], writes=[bSk])

        def vcol_r(r, a, b):
            V_ = RB[r]["V"]
            return V_[:, a:b] if r == 0 else V_[:, T - b:T - a]

        def p2_first(r):
            R_ = RB[r]
            bk = R_["bank0"]
            dst = vcol_r(r, 0, 1)
            P.op("pe", lambda e: e.matmul(psum_all[:, bk * 512:bk * 512 + 1], lhsT=R_["AT"][:, 0, :], rhs=R_["Sk"][:, 4:5], start=True, stop=True),
                 reads=[R_["bAT"], R_["bSk"]], writes=[b_pb[bk]])
            P.op("dve", lambda e: e.tensor_copy(out=dst, in_=psum_all[:, bk * 512:bk * 512 + 1]), reads=[b_pb[bk]], writes=[R_["bV"]])

        def p2_level(r, k):
            R_ = RB[r]
            sh = 1 << k
            bank = R_["banks"][0]
            for c0 in range(0, sh, 512):
                c1 = min(sh, c0 + 512)
                src, dst = vcol_r(r, c0, c1), vcol_r(r, sh + c0, sh + c1)
                P.op("pe", (lambda c0, c1, src: lambda e: e.matmul(psum_all[:, bank * 512:bank * 512 + c1 - c0], lhsT=R_["AT"][:, k, :],
                                                                   rhs=src, start=True, stop=True))(c0, c1, src),
                     reads=[R_["bAT"], R_["bV"]], writes=[b_pb[bank]])
                eng = "act" if r == 0 else "dve"
                P.op(eng, (lambda c0, c1, dst: (lambda e: e.copy(out=dst, in_=psum_all[:, bank * 512:bank * 512 + c1 - c0])) if eng == "act" else
                           (lambda e: e.tensor_copy(out=dst, in_=psum_all[:, bank * 512:bank * 512 + c1 - c0])))(c0, c1, dst),
                     reads=[b_pb[bank]], writes=[R_["bV"]])

        def p2_out(r, first, last):
            R_ = RB[r]
            for c in range(4):
                P.op("pe", (lambda c: lambda e: e.matmul(pb[c], lhsT=R_["CZ"], rhs=R_["V"][:, c * 512:(c + 1) * 512], start=first, stop=last,
                                                         skip_group_check=True))(c), reads=[R_["bCZ"], R_["bV"]], writes=[b_pb[c]])

        for ct in range(4):
            n_acc = 0
            for g8 in range(8):
                for r in range(2):
                    p2_setup(ct, g8, r)
                for r in range(2):
                    p2_first(r)
                for k in range(11):
                    for r in range(2):
                        p2_level(r, k)
                for r in range(2):
                    p2_out(r, n_acc == 0, n_acc == 15)
                    n_acc += 1
                P.op("pool", (lambda g8: lambda e: e.memset(CZ[:, 16 * g8:16 * g8 + 16], 0.0))(g8), writes=[b_CZ])
                P.op("pool", (lambda g8: lambda e: e.memset(CZ2[:, 16 * g8:16 * g8 + 16], 0.0))(g8), writes=[b_CZ2])
            for c in range(4):
                P.op("dve", (lambda c, ct: lambda e: e.tensor_tensor(out=yacc[:, ct, c * 512:(c + 1) * 512], in0=pb[c], in1=yacc[:, ct, c * 512:(c + 1) * 512],
                                                                     op=ALU.add))(c, ct), reads=[b_pb[c]], writes=[b_yacc])
        P.barrier()
        o[0] = o_reuse
        wg = alloc([128, 4, 512])
        bgc = alloc([128, 4])
        b_wg = P.buf()
        P.dma("sp", wg, wglu_d.ap().rearrange("(k p) n -> p k n", p=128), writes=[b_wg])
        P.dma("sp", bgc, bgluT_d[:, :], writes=[b_wg])
        tmp = [X[0], X[1]]
        for ct in range(4):
            ya = yacc[:, ct, :]
            P.op("dve", (lambda ct, ya: lambda e: e.scalar_tensor_tensor(out=ya, in0=sT[:, ct, 0:T], scalar=dcol[:, ct:ct + 1], in1=ya,
                                                                         op0=ALU.mult, op1=ALU.add))(ct, ya), reads=[S.b_sT, b_C, b_yacc], writes=[b_yacc])
            P.op("dve", (lambda ya: lambda e: e.tensor_tensor(out=tmp[0], in0=ya, in1=ya, op=ALU.mult))(ya), reads=[b_yacc], writes=b_X[0])
            P.op("dve", lambda e: e.tensor_scalar(out=tmp[0], in0=tmp[0], scalar1=0.044715 * 0.7978845608, scalar2=0.7978845608, op0=ALU.mult, op1=ALU.add),
                 reads=b_X[0], writes=b_X[0])
            P.op("dve", (lambda ya: lambda e: e.tensor_tensor(out=tmp[0], in0=tmp[0], in1=ya, op=ALU.mult))(ya), reads=[b_yacc] + b_X[0], writes=b_X[0])
            P.op("act", lambda e: e.activation(out=tmp[0], in_=tmp[0], func=AF.Tanh), reads=b_X[0], writes=b_X[0])
            P.op("dve", lambda e: e.tensor_scalar(out=tmp[0], in0=tmp[0], scalar1=0.5, scalar2=0.5, op0=ALU.mult, op1=ALU.add), reads=b_X[0], writes=b_X[0])
            P.op("dve", (lambda ct, ya: lambda e: e.tensor_tensor(out=ya, in0=tmp[0], in1=ya, op=ALU.mult))(ct, ya), reads=b_X[0] + [b_yacc], writes=[b_yacc])
        ysT = alloc([128, 4, 512], BF16)
        b_ysT = P.buf()
        yts = alloc([128, 512], BF16)
        b_yts = P.buf()
        for c in range(4):
            for co in range(4):
                for k in range(4):
                    P.op("pe", (lambda c, co, k: lambda e: e.matmul(pb[co], lhsT=wg[:, k, co * 128:(co + 1) * 128], rhs=yacc[:, k, c * 512:(c + 1) * 512],
                                                                    start=(k == 0), stop=(k == 3)))(c, co, k), reads=[b_wg, b_yacc], writes=[b_pb[co]])
                P.op("act", (lambda co: lambda e: e.activation(out=tmp[1][:, co * 512:(co + 1) * 512], in_=pb[co], func=AF.Sigmoid,
                                                               bias=bgc[:, co:co + 1], scale=1.0))(co), reads=[b_pb[co], b_wg], writes=b_X[1])
                P.op("dve", (lambda c, co: lambda e: e.tensor_tensor(out=ysT[:, co, :], in0=tmp[1][:, co * 512:(co + 1) * 512],
                                                                     in1=yacc[:, co, c * 512:(c + 1) * 512], op=ALU.mult))(c, co),
                     reads=b_X[1] + [b_yacc], writes=[b_ysT])
            for tt in range(4):
                t = c * 4 + tt
                for co in range(4):
                    P.op("pe", (lambda co, tt: lambda e: e.matmul(psum_all[:, 4 * 512 + co * 128:4 * 512 + (co + 1) * 128], lhsT=ysT[:, co, tt * 128:(tt + 1) * 128],
                                                                  rhs=ident_b[:, :], start=True, stop=True, skip_group_check=True))(co, tt),
                         reads=[b_ysT, b_ident], writes=[b_pb[4]])
                P.op("act", lambda e: e.copy(out=yts, in_=pb[4]), reads=[b_pb[4]], writes=[b_yts])
                P.dma("sp", ymix[t * 128:(t + 1) * 128, 0:512], yts, reads=[b_yts], writes=[S.b_ymix[t]])
    s5_branch()
    out_proj_phase(1, False)
    ffn_phase(1, True)
    return P.finish()


def _prep_inputs(inp):
    f32 = np.float32
    cos, sin = _rope_tables()
    ccs, m1, w3re, w3im, cosc, sinc = _fourier_consts()
    vecs = np.stack([np.stack([inp["b_out"][l], inp["ln_mix_g"][l], inp["ln_mix_b"][l],
                               inp["b_ffn2"][l], inp["ln_ffn_g"][l], inp["ln_ffn_b"][l]], 0) for l in range(2)], 0)
    b1T = np.ascontiguousarray(inp["b_ffn1"].reshape(2, 32, 128).transpose(0, 2, 1))
    lamv = np.stack([inp["lam_q1"][0], inp["lam_k1"][0], inp["lam_q2"][0], inp["lam_k2"][0]], 0)
    common = {
        "vecs": np.ascontiguousarray(vecs.astype(f32)), "b1T": b1T.astype(f32),
        "win0": np.ascontiguousarray(inp["w_in_ab"][0]), "wfT": np.ascontiguousarray(inp["w_in_ab"][0][:, :256].T),
        "ccs": ccs, "wout": inp["w_out"], "w1": inp["w_ffn1"], "w2": inp["w_ffn2"],
        "ident": np.eye(128, dtype=f32), "lamv": lamv.astype(f32), "subg": inp["subln_g"].astype(f32),
        "wcd": np.ascontiguousarray(inp["w_in_cd"][0]),
        "wspT": np.ascontiguousarray(inp["w_sp"][0].transpose(0, 2, 1)),
        "bspT": np.ascontiguousarray(inp["b_sp"][0].T),
        "s5lam": np.ascontiguousarray(np.tile(np.stack([inp["s5_lam_re"][0].reshape(2, 4, 8, 64).transpose(3, 2, 0, 1).reshape(64, 64), inp["s5_lam_im"][0].reshape(2, 4, 8, 64).transpose(3, 2, 0, 1).reshape(64, 64)], 1), (2, 1, 1)).astype(f32)),
        "s5dt": np.ascontiguousarray(np.broadcast_to(inp["s5_log_dt"][0].reshape(2, 4, 8).transpose(2, 0, 1).reshape(1, 64), (128, 64)).astype(f32)),
        "s5bT": np.ascontiguousarray(np.stack([inp["s5_b_re"][0], inp["s5_b_im"][0]], 0).reshape(2, 2, 4, 8, 64, 16).transpose(3, 5, 0, 1, 2, 4).reshape(128, 2, 2, 4, 64).astype(f32)),
        "s5cT": np.ascontiguousarray(np.concatenate([inp["s5_c_re"][0].transpose(3, 0, 1, 2), inp["s5_c_im"][0].transpose(3, 0, 1, 2)], 0).astype(f32)),
        "s5d": np.ascontiguousarray(inp["s5_d"][0].reshape(4, 128).T.astype(f32)),
        "wglu": np.ascontiguousarray(inp["w_glu"][0]), "bgluT": np.ascontiguousarray(inp["b_glu"][0].reshape(4, 128).T.astype(f32)),
        "m1c": _bf(m1), "dftc": _bf(np.concatenate([cosc, sinc], 1)),
    }
    maps = []
    for r in range(NCORE):
        b, j = r // 4, r % 4
        m = dict(common)
        m["xin"] = np.ascontiguousarray(np.concatenate([inp["x"][b, TPC * j:TPC * (j + 1)], inp["ctx"][b]], 0))
        c_all = np.stack([inp["c"][b], inp["c_ctx"]], 0).astype(f32)
        m["cT"] = np.ascontiguousarray(c_all.reshape(2, 8, 128).transpose(2, 1, 0))
        m["wmod"] = np.ascontiguousarray(inp["w_mod"][:, :, 1536 * j:1536 * (j + 1)])
        m["bmod"] = np.ascontiguousarray(inp["b_mod"][:, 1536 * j:1536 * (j + 1)])
        m["rope"] = np.ascontiguousarray(np.concatenate([cos[b * 0 + TPC * j:TPC * (j + 1)], sin[TPC * j:TPC * (j + 1)]], 1))
        cst = np.zeros((128, 140), f32)
        for k_ in range(64):
            cst[k_, k_ + 64] = 1.0
            cst[k_ + 64, k_] = -1.0
        for p_ in range(128):
            cst[p_, 128 + p_ // 16] = 1.0
        cst[:, 136 + j] = 1.0
        m["s5cst"] = cst
        m["w3c"] = _bf(np.concatenate([w3re[:, 32 * j:32 * (j + 1)], w3im[:, 32 * j:32 * (j + 1)]], 1))
        maps.append(m)
    return maps


def kernel(**inputs):
    inp = {k: np.asarray(v) for k, v in inputs.items()}
    maps = _prep_inputs(inp)
    nc = build()
    res = run_bass_kernel_spmd(nc, maps, core_ids=list(range(NCORE)))
    out = np.zeros((2, SEQ, D), np.float32)
    for r in range(NCORE):
        b, j = r // 4, r % 4
        out[b, TPC * j:TPC * (j + 1)] = res.results[r]["yout"]
    return out
```
